# Optimizing a Trainium2 kernel written in Bass

```python
import math
import jax, jax.numpy as jnp
from jax import lax
import numpy as np

D_MODEL = 2048
BATCH = 4
SEQ = 2048
DEPTH = 4

CHUNK = 64
N_MIXERS = 4
HEAD_DIM = 128
N_HEADS = D_MODEL // HEAD_DIM
D_FF = 5632
RG_WIDTH = D_MODEL
RG_BLOCKS = N_HEADS
RG_BLOCK = RG_WIDTH // RG_BLOCKS
RG_CONV = 4
RG_C = 8.0
SC_CONV = 3
Q_BLOCK = 128
EPS = 1e-6

kernel_name = "hybrid_interleaved_streaming_encoder"


def n_occ(m):
    return len(range(m, DEPTH, N_MIXERS))


def rms(x):
    xf = x.astype(jnp.float32)
    return (xf * lax.rsqrt(jnp.mean(xf * xf, axis=-1, keepdims=True) + EPS)).astype(x.dtype)


def modulate(x, g, m):
    return rms(x) * g * (1.0 + m[:, 1][:, None, :]) + m[:, 0][:, None, :]


def swiglu(h, w_gate, w_up, w_down):
    return (jax.nn.silu(h @ w_gate) * (h @ w_up)) @ w_down


def causal_conv(x, w, b=None):
    W = w.shape[0]
    S = x.shape[1]
    xp = jnp.pad(x, ((0, 0), (W - 1, 0), (0, 0)))
    y = w[0] * xp[:, 0:S]
    for k in range(1, W):
        y = y + w[k] * xp[:, k:k + S]
    return y if b is None else y + b


def rglru_mixer(h, w_in, conv_w, conv_b, w_r, b_r, w_i, b_i, lam, w_out):
    B, S, _ = h.shape
    gate_branch, xb = jnp.split(h @ w_in, 2, axis=-1)
    xb = causal_conv(xb, conv_w, conv_b)
    xh = xb.reshape(B, S, RG_BLOCKS, RG_BLOCK)
    r = jax.nn.sigmoid(jnp.einsum('bsgi,gij->bsgj', xh, w_r).reshape(B, S, RG_WIDTH) + b_r)
    ig = jax.nn.sigmoid(jnp.einsum('bsgi,gij->bsgj', xh, w_i).reshape(B, S, RG_WIDTH) + b_i)
    log_a = RG_C * r.astype(jnp.float32) * jax.nn.log_sigmoid(lam.astype(jnp.float32))
    a = jnp.exp(log_a)
    bx = jnp.sqrt(-jnp.expm1(2.0 * log_a)) * (ig * xb).astype(jnp.float32)

    def combine(left, right):
        a1, b1 = left
        a2, b2 = right
        return a1 * a2, a2 * b1 + b2

    _, hs = lax.associative_scan(combine, (a, bx), axis=1)
    y = hs.astype(h.dtype) * jax.nn.gelu(gate_branch)
    return y @ w_out


def shortconv_mixer(h, w_in, conv_w, w_out):
    bg, cg, xv = jnp.split(h @ w_in, 3, axis=-1)
    y = bg * causal_conv(cg * xv, conv_w)
    return y @ w_out


def fox_mixer(h, w_in, b_f, q_gain, k_gain, w_out):
    B, S, D = h.shape
    H, dh = N_HEADS, HEAD_DIM
    q, k, v, g, fl = jnp.split(h @ w_in, [D, 2 * D, 3 * D, 4 * D], axis=-1)
    q = rms(q.reshape(B, S, H, dh)) * q_gain
    k = rms(k.reshape(B, S, H, dh)) * k_gain
    v = v.reshape(B, S, H, dh)
    logf = jax.nn.log_sigmoid((fl + b_f).astype(jnp.float32))
    F = jnp.cumsum(logf, axis=1)
    Fk = F.transpose(0, 2, 1)
    nq = S // Q_BLOCK
    qb = q.reshape(B, nq, Q_BLOCK, H, dh).transpose(1, 0, 2, 3, 4)
    Fq = F.reshape(B, nq, Q_BLOCK, H).transpose(1, 0, 3, 2)
    pos_k = jnp.arange(S)
    scale = 1.0 / math.sqrt(dh)

    def block(args):
        qi, Fi, bi = args
        s = jnp.einsum('bqhd,bkhd->bhqk', qi, k).astype(jnp.float32) * scale
        s = s + Fi[..., None] - Fk[:, :, None, :]
        pos_q = bi * Q_BLOCK + jnp.arange(Q_BLOCK)
        s = jnp.where(pos_k[None, :] <= pos_q[:, None], s, -jnp.inf)
        p = jax.nn.softmax(s, axis=-1).astype(v.dtype)
        return jnp.einsum('bhqk,bkhd->bqhd', p, v)

    o = lax.map(block, (qb, Fq, jnp.arange(nq)))
    o = o.transpose(1, 0, 2, 3, 4).reshape(B, S, D)
    return (o * jax.nn.sigmoid(g)) @ w_out


def hgrn2_mixer(h, w_in, lb, norm_gain, w_out):
    B, S, D = h.shape
    H, dh, C = N_HEADS, HEAD_DIM, CHUNK
    N = S // C
    q, fz, iv, g = jnp.split(h @ w_in, 4, axis=-1)
    logf = jnp.log(lb + (1.0 - lb) * jax.nn.sigmoid(fz.astype(jnp.float32)))
    kk = -jnp.expm1(logf)

    def heads(t):
        return t.astype(jnp.float32).reshape(B, N, C, H, dh).transpose(0, 3, 1, 2, 4)

    qh, kh, vh, gh = heads(q), heads(kk), heads(iv), heads(logf)
    bcum = jnp.cumsum(gh, axis=3)
    b_last = bcum[:, :, :, -1:, :]
    q_in = qh * jnp.exp(bcum)
    k_in = kh * jnp.exp(-bcum)
    tril = jnp.tril(jnp.ones((C, C), dtype=bool))
    scores = jnp.where(tril, jnp.einsum('bhncd,bhnsd->bhncs', q_in, k_in), 0.0)
    o_intra = jnp.einsum('bhncs,bhnse->bhnce', scores, vh)
    U = jnp.einsum('bhncd,bhnce->bhnde', kh * jnp.exp(b_last - bcum), vh)
    decay = jnp.exp(b_last[:, :, :, 0, :])

    def step(S_prev, inp):
        dec, u = inp
        return dec[..., None] * S_prev + u, S_prev

    S0 = jnp.zeros((B, H, dh, dh), jnp.float32)
    _, S_start = lax.scan(step, S0, (decay.transpose(2, 0, 1, 3), U.transpose(2, 0, 1, 3, 4)))
    S_start = S_start.transpose(1, 2, 0, 3, 4)
    o = o_intra + jnp.einsum('bhncd,bhnde->bhnce', q_in, S_start)
    o = o.transpose(0, 2, 3, 1, 4).reshape(B, S, H, dh)
    o = (rms(o) * norm_gain).reshape(B, S, D).astype(h.dtype)
    return (o * jax.nn.silu(g)) @ w_out


def setup_inputs(seed: int = 0) -> dict:
    key = jax.random.key(seed)
    keys = jax.random.split(key, 32)
    cnt = [0]

    def nk():
        k = keys[cnt[0]]
        cnt[0] += 1
        return k

    def w(shape, fan_in, s=1.0):
        return s * jax.random.normal(nk(), shape, jnp.float32) * fan_in ** -0.5

    def gain(shape):
        return 1.0 + 0.1 * jax.random.normal(nk(), shape, jnp.float32)

    def bias(shape, s=0.02):
        return s * jax.random.normal(nk(), shape, jnp.float32)

    D, H, dh = D_MODEL, N_HEADS, HEAD_DIM
    na, nb, nc, nd = n_occ(0), n_occ(1), n_occ(2), n_occ(3)
    x = jax.random.normal(nk(), (BATCH, SEQ, D), jnp.float32)
    c = jax.random.normal(nk(), (BATCH, D), jnp.float32)
    ada_w = w((DEPTH, D, 9 * D), D, 0.5)
    ada_b = bias((DEPTH, 9 * D))
    norm_g = gain((DEPTH, 3, D))
    ffn_w_gate = w((DEPTH, 2, D, D_FF), D)
    ffn_w_up = w((DEPTH, 2, D, D_FF), D)
    ffn_w_down = w((DEPTH, 2, D_FF, D), D_FF)
    rg_w_in = w((na, D, 2 * RG_WIDTH), D)
    rg_conv_w = w((na, RG_CONV, RG_WIDTH), RG_CONV)
    rg_conv_b = bias((na, RG_WIDTH))
    rg_w_r = w((na, RG_BLOCKS, RG_BLOCK, RG_BLOCK), RG_BLOCK)
    rg_b_r = bias((na, RG_WIDTH))
    rg_w_i = w((na, RG_BLOCKS, RG_BLOCK, RG_BLOCK), RG_BLOCK)
    rg_b_i = bias((na, RG_WIDTH))
    u = jax.random.uniform(nk(), (na, RG_WIDTH), jnp.float32, 0.9, 0.999)
    a0 = u ** (1.0 / RG_C)
    rg_lam = jnp.log(a0) - jnp.log1p(-a0)
    rg_w_out = w((na, RG_WIDTH, D), RG_WIDTH)
    sc_w_in = w((nb, D, 3 * D), D)
    sc_conv_w = w((nb, SC_CONV, D), SC_CONV)
    sc_w_out = w((nb, D, D), D)
    fox_w_in = w((nc, D, 4 * D + H), D)
    fox_b_f = 2.0 + bias((nc, H), 0.5)
    fox_q_gain = gain((nc, dh))
    fox_k_gain = gain((nc, dh))
    fox_w_out = w((nc, D, D), D)
    hg_w_in = w((nd, D, 4 * D), D)
    hg_lb_logits = bias((DEPTH, D), 0.1)
    hg_norm_gain = gain((nd, dh))
    hg_w_out = w((nd, D, D), D)
    return {"x": x, "c": c, "ada_w": ada_w, "ada_b": ada_b, "norm_g": norm_g,
            "ffn_w_gate": ffn_w_gate, "ffn_w_up": ffn_w_up, "ffn_w_down": ffn_w_down,
            "rg_w_in": rg_w_in, "rg_conv_w": rg_conv_w, "rg_conv_b": rg_conv_b,
            "rg_w_r": rg_w_r, "rg_b_r": rg_b_r, "rg_w_i": rg_w_i, "rg_b_i": rg_b_i,
            "rg_lam": rg_lam, "rg_w_out": rg_w_out,
            "sc_w_in": sc_w_in, "sc_conv_w": sc_conv_w, "sc_w_out": sc_w_out,
            "fox_w_in": fox_w_in, "fox_b_f": fox_b_f, "fox_q_gain": fox_q_gain,
            "fox_k_gain": fox_k_gain, "fox_w_out": fox_w_out,
            "hg_w_in": hg_w_in, "hg_lb_logits": hg_lb_logits, "hg_norm_gain": hg_norm_gain,
            "hg_w_out": hg_w_out}


def reference(x, c, ada_w, ada_b, norm_g, ffn_w_gate, ffn_w_up, ffn_w_down,
              rg_w_in, rg_conv_w, rg_conv_b, rg_w_r, rg_b_r, rg_w_i, rg_b_i, rg_lam, rg_w_out,
              sc_w_in, sc_conv_w, sc_w_out,
              fox_w_in, fox_b_f, fox_q_gain, fox_k_gain, fox_w_out,
              hg_w_in, hg_lb_logits, hg_norm_gain, hg_w_out):
    B = x.shape[0]
    D = D_MODEL
    sc = jax.nn.silu(c)
    lb_p = jax.nn.softmax(hg_lb_logits.astype(jnp.float32), axis=0)
    lb_table = jnp.cumsum(lb_p, axis=0) - lb_p
    for i in range(DEPTH):
        mod = (sc @ ada_w[i] + ada_b[i]).reshape(B, 3, 3, D)
        h = modulate(x, norm_g[i, 0], mod[:, 0])
        x = x + 0.5 * mod[:, 0, 2][:, None, :] * swiglu(h, ffn_w_gate[i, 0], ffn_w_up[i, 0], ffn_w_down[i, 0])
        h = modulate(x, norm_g[i, 1], mod[:, 1])
        m, j = i % N_MIXERS, i // N_MIXERS
        if m == 0:
            y = rglru_mixer(h, rg_w_in[j], rg_conv_w[j], rg_conv_b[j], rg_w_r[j], rg_b_r[j],
                            rg_w_i[j], rg_b_i[j], rg_lam[j], rg_w_out[j])
        elif m == 1:
            y = shortconv_mixer(h, sc_w_in[j], sc_conv_w[j], sc_w_out[j])
        elif m == 2:
            y = fox_mixer(h, fox_w_in[j], fox_b_f[j], fox_q_gain[j], fox_k_gain[j], fox_w_out[j])
        else:
            y = hgrn2_mixer(h, hg_w_in[j], lb_table[i], hg_norm_gain[j], hg_w_out[j])
        x = x + mod[:, 1, 2][:, None, :] * y
        h = modulate(x, norm_g[i, 2], mod[:, 2])
        x = x + 0.5 * mod[:, 2, 2][:, None, :] * swiglu(h, ffn_w_gate[i, 1], ffn_w_up[i, 1], ffn_w_down[i, 1])
    return x
```

```python
import numpy as np
from contextlib import ExitStack
import concourse.bass as bass
import concourse.mybir as mybir
from concourse.bass_utils import run_bass_kernel_spmd

F32 = mybir.dt.float32
BF16 = mybir.dt.bfloat16
AF = mybir.ActivationFunctionType
ALU = mybir.AluOpType

D = 2048
NT = 16
T = 1024
TB = 512
FF = 5632
NF = 44
FC = 4
NCH = NF // FC
EPS = 1e-6
NS = 6
NTF = 10
NTH = 7
N_CORES = 4


def I(name, *args, **kwargs):
    return (name, args, kwargs)


class Eng:
    def __init__(self, name, sem, unit=1):
        self.name, self.sem, self.unit = name, sem, unit
        self.count = 0
        self.seen = {}
        self.ops = []


class Buf:
    __slots__ = ("name", "w", "r")

    def __init__(self, name=""):
        self.name = name
        self.w = None
        self.r = {}


class FW:
    def __init__(self, nc, stack, n_chan=6):
        self.nc = nc
        self.stack = stack
        self.planning = False
        sem = lambda n: stack.enter_context(nc.semaphore(n))
        self.pe = Eng("pe", sem("s_pe"))
        self.act = Eng("act", sem("s_act"))
        self.dve = Eng("dve", sem("s_dve"))
        self.pool = Eng("pool", sem("s_pool"))
        self.sp = Eng("sp", sem("s_sp"))
        self.chans = {
            "sp": [Eng(f"chs{i}", sem(f"s_chs{i}"), unit=16) for i in range(n_chan)],
            "pool": [Eng(f"chp{i}", sem(f"s_chp{i}"), unit=16) for i in range(n_chan)],
        }
        self.chan_i = {"sp": 0, "pool": 0}
        self.cch = Eng("cch", sem("s_cch"), unit=16)

    def buf(self, name=""):
        return Buf(name)

    def sbuf(self, name, shape, dt):
        return self.stack.enter_context(self.nc.sbuf_tensor(name, list(shape), dt))

    def psum(self, name, shape, dt=F32):
        return self.stack.enter_context(self.nc.psum_tensor(name, list(shape), dt))

    def _deps(self, E, reads, writes):
        deps = {}
        for b in reads:
            if b.w is not None:
                e, i = b.w
                if deps.get(e, 0) < i:
                    deps[e] = i
        for b in writes:
            if b.w is not None:
                e, i = b.w
                if deps.get(e, 0) < i:
                    deps[e] = i
            for e, i in b.r.items():
                if deps.get(e, 0) < i:
                    deps[e] = i
        waits = []
        for e, i in deps.items():
            if e is E and E is self.pe:
                continue
            if E.seen.get(e, 0) < i:
                E.seen[e] = i
                waits.append((e.sem, i * e.unit))
        return waits

    def _mark(self, E, idx, reads, writes):
        for b in writes:
            b.w = (E, idx)
            b.r = {}
        for b in reads:
            if b.w is not None and b.w[0] is E and b.w[1] == idx:
                continue
            b.r[E] = idx

    def op(self, E, fn, reads=(), writes=()):
        if self.planning:
            return
        waits = self._deps(E, reads, writes)
        E.count += 1
        idx = E.count
        sem = E.sem

        def run(h, waits=waits, fn=fn, sem=sem):
            for s, v in waits:
                h.wait_ge(s, v)
            getattr(h, fn[0])(*fn[1], **fn[2]).then_inc(sem, 1)

        E.ops.append(run)
        self._mark(E, idx, reads, writes)

    def multi(self, E, fns, reads=(), writes=()):
        if self.planning:
            return
        waits = self._deps(E, reads, writes)
        E.count += 1
        idx = E.count
        sem = E.sem

        def run(h, waits=waits, fns=fns, sem=sem):
            for s, v in waits:
                h.wait_ge(s, v)
            for f in fns[:-1]:
                getattr(h, f[0])(*f[1], **f[2])
            f = fns[-1]
            getattr(h, f[0])(*f[1], **f[2]).then_inc(sem, 1)

        E.ops.append(run)
        self._mark(E, idx, reads, writes)

    def dma(self, Q, out_ap, in_ap, reads=(), writes=()):
        if self.planning:
            return
        chl = self.chans[Q.name]
        ch = chl[self.chan_i[Q.name] % len(chl)]
        self.chan_i[Q.name] += 1
        waits = self._deps(Q, reads, writes)
        if ch.count > 0 and Q.seen.get(ch, 0) < ch.count:
            Q.seen[ch] = ch.count
            waits.append((ch.sem, ch.count * 16))
        ch.count += 1
        idx = ch.count
        sem = ch.sem

        def run(h, waits=waits, sem=sem):
            for s, v in waits:
                h.wait_ge(s, v)
            h.dma_start(out=out_ap, in_=in_ap).then_inc(sem, 16)

        Q.ops.append(run)
        for b in writes:
            b.w = (ch, idx)
            b.r = {}
        for b in reads:
            b.r[ch] = idx

    def coll(self, src, dst, groups, reads=(), writes=()):
        if self.planning:
            return
        Q = self.pool
        ch = self.cch
        waits = self._deps(Q, reads, writes)
        if ch.count > 0 and Q.seen.get(ch, 0) < ch.count:
            Q.seen[ch] = ch.count
            waits.append((ch.sem, ch.count * 16))
        ch.count += 1
        idx = ch.count
        sem = ch.sem

        def run(h, waits=waits, sem=sem):
            for s_, v in waits:
                h.wait_ge(s_, v)
            h.collective_compute("AllGather", ALU.bypass, replica_groups=groups, ins=[src], outs=[dst]).then_inc(sem, 16)

        Q.ops.append(run)
        for b in writes:
            b.w = (ch, idx)
            b.r = {}
        for b in reads:
            b.r[ch] = idx

    def finish(self):
        waits = []
        for chl in self.chans.values():
            for ch in chl:
                if ch.count:
                    waits.append((ch.sem, ch.count * 16))
        for E in (self.pe, self.act, self.dve, self.pool):
            if E.count:
                waits.append((E.sem, E.count))
        if self.cch.count:
            waits.append((self.cch.sem, self.cch.count * 16))

        def run(h):
            for s, v in waits:
                h.wait_ge(s, v)

        self.sp.ops.append(run)

    def emit(self):
        with self.nc.Block() as block:
            @block.sync
            def _(h):
                for f in self.sp.ops:
                    f(h)

            @block.tensor
            def _(h):
                for f in self.pe.ops:
                    f(h)

            @block.scalar
            def _(h):
                for f in self.act.ops:
                    f(h)

            @block.vector
            def _(h):
                for f in self.dve.ops:
                    f(h)

            @block.gpsimd
            def _(h):
                for f in self.pool.ops:
                    f(h)


PC = {}


def _pc_layout():
    off = 0

    def add(n, w):
        nonlocal off
        PC[n] = off
        off += w

    add("c", 16)
    for l in range(4):
        add(f"ada_b{l}", 144)
    for l in range(4):
        for s in range(3):
            add(f"ng{l}_{s}", 16)
    add("rg_cw", 64)
    add("rg_cb", 16)
    add("rg_br", 16)
    add("rg_bi", 16)
    add("rg_lam", 16)
    add("sc_cw", 48)
    add("fox_qg", 1)
    add("fox_kg", 1)
    add("fox_bf", 1)
    add("hg_lg", 64)
    add("hg_ng", 1)
    return off


NP = _pc_layout()


def _v16(v):
    return np.asarray(v, np.float32).reshape(-1, 128).T


def pack_params(inp, b):
    P = np.zeros((128, NP), np.float32)

    def put(name, arr):
        a = np.asarray(arr, np.float32)
        P[: a.shape[0], PC[name]: PC[name] + a.shape[1]] = a

    put("c", _v16(inp["c"][b]))
    for l in range(4):
        put(f"ada_b{l}", _v16(inp["ada_b"][l]))
        for s in range(3):
            put(f"ng{l}_{s}", _v16(inp["norm_g"][l, s]))
    put("rg_cw", np.concatenate([_v16(inp["rg_conv_w"][0, k]) for k in range(4)], axis=1))
    put("rg_cb", _v16(inp["rg_conv_b"][0]))
    put("rg_br", _v16(inp["rg_b_r"][0]))
    put("rg_bi", _v16(inp["rg_b_i"][0]))
    put("rg_lam", _v16(inp["rg_lam"][0]))
    put("sc_cw", np.concatenate([_v16(inp["sc_conv_w"][0, k]) for k in range(3)], axis=1))
    put("fox_qg", np.asarray(inp["fox_q_gain"][0]).reshape(128, 1))
    put("fox_kg", np.asarray(inp["fox_k_gain"][0]).reshape(128, 1))
    put("fox_bf", np.asarray(inp["fox_b_f"][0]).reshape(16, 1))
    put("hg_lg", np.concatenate([_v16(inp["hg_lb_logits"][l]) for l in range(4)], axis=1))
    put("hg_ng", np.asarray(inp["hg_norm_gain"][0]).reshape(128, 1))
    return P


MIXW = {
    0: {"rg_w_in": [2048, 4096], "rg_w_r": [16, 128, 128], "rg_w_i": [16, 128, 128], "rg_w_out": [2048, 2048]},
    1: {"sc_w_in": [2048, 6144], "sc_w_out": [2048, 2048]},
    2: {"fox_w_in": [2048, 8208], "fox_w_out": [2048, 2048]},
    3: {"hg_w_in": [2048, 8192], "hg_w_out": [2048, 2048]},
}


def weight_shapes(layers):
    ws = {}
    for l in layers:
        ws[f"ada_w{l}"] = [2048, 18432]
        for w in range(2):
            ws[f"ffn_g{l}_{w}"] = [2048, 5632]
            ws[f"ffn_u{l}_{w}"] = [2048, 5632]
            ws[f"ffn_d{l}_{w}"] = [5632, 2048]
        ws.update(MIXW[l % 4])
    return ws


def weight_arrays(inp, layers):
    out = {}
    for name in weight_shapes(layers):
        if name.startswith("ada_w"):
            a = inp["ada_w"][int(name[5:])]
        elif name.startswith("ffn_"):
            l, w = int(name[5]), int(name[7])
            a = inp[{"g": "ffn_w_gate", "u": "ffn_w_up", "d": "ffn_w_down"}[name[4]]][l, w]
        else:
            a = inp[name][0]
        out[name] = np.ascontiguousarray(np.asarray(a, np.float32))
    return out


def tbs(tb):
    return slice(tb * TB, (tb + 1) * TB)


class Prog:
    def __init__(self, layers=(0, 1, 2, 3), n_halves=2, stop=None):
        self.layers = tuple(layers)
        self.n_halves = n_halves
        self.stop = stop
        import os
        self.dbg = int(os.environ.get("KDBG", "9"))
        self.nc = bass.Bass("TRN2", target_bir_lowering=False)

    def alloc(self, st):
        nc = self.nc
        fw = self.fw = FW(nc, st)
        dr = lambda n, s, dt, kind: nc.dram_tensor(n, list(s), dt, kind=kind).ap()
        self.xT = dr("xT", [2, D, T], F32, "ExternalInput")
        self.oT = dr("oT", [2, D, T], F32, "ExternalOutput")
        self.prm = dr("prm", [128, NP], F32, "ExternalInput")
        self.W = {k: dr(k, s, F32, "ExternalInput") for k, s in weight_shapes(self.layers).items()}
        self.Kd = dr("Kd", [16, 128, T], BF16, "Internal")
        self.Vd = dr("Vd", [16, 128, T], BF16, "Internal")
        self.Sd = dr("Sd", [16, 128, 128], F32, "Internal")
        self.F3d = dr("F3d", [3, 16, T], BF16, "Internal")
        B = fw.buf
        self.b_Kd = [B() for _ in range(16)]
        self.b_Vd = [B() for _ in range(16)]
        self.b_Sd = [B() for _ in range(16)]
        self.b_F3d = B()

        self.x = fw.sbuf("x", [128, NT, T], F32)
        self.xb = [[B() for _ in range(2)] for _ in range(NT)]
        self.h = fw.sbuf("h", [128, NT, T], BF16)
        self.hb = [B(), B()]
        self.y = fw.sbuf("y", [128, NT, T], BF16)
        self.yb = [[B() for _ in range(2)] for _ in range(NT)]
        self.ring = [fw.sbuf(f"ring{i}", [128, 2048], BF16) for i in range(NS)]
        self.ringb = [B() for _ in range(NS)]
        self.TFt = [fw.sbuf(f"tf{i}", [128, TB], F32) for i in range(NTF)]
        self.TFb = [B() for _ in range(NTF)]
        self.THt = [fw.sbuf(f"th{i}", [128, T], BF16) for i in range(NTH)]
        self.THb = [B() for _ in range(NTH)]
        self.PT = [fw.sbuf(f"pt{i}", [128, TB], BF16) for i in range(2)]
        self.PTb = [B(), B()]
        self.prm_sb = fw.sbuf("prm_sb", [128, NP], F32)
        self.b_prm = B()
        self.mods = fw.sbuf("mods", [128, 4, 144], F32)
        self.b_mods = [B() for _ in range(4)]
        self.derA = fw.sbuf("derA", [128, 4, 3, 16], F32)
        self.derG = fw.sbuf("derG", [128, 4, 3, 16], F32)
        self.b_der = [B() for _ in range(4)]
        self.ones_f = fw.sbuf("ones_f", [128, 128], F32)
        self.ones_b = fw.sbuf("ones_b", [128, 128], BF16)
        self.ident_f = fw.sbuf("ident_f", [128, 128], F32)
        self.ident_b = fw.sbuf("ident_b", [128, 128], BF16)
        self.mtri = fw.sbuf("mtri", [128, 128], BF16)
        self.maskC = fw.sbuf("maskC", [128, TB], BF16)
        self.ones16 = fw.sbuf("ones16", [16, TB], BF16)
        self.b_const = B()
        self.scb = fw.sbuf("scb", [128, 16], BF16)
        self.b_scb = B()
        self.pad = fw.sbuf("pad", [128, T + 8], F32)
        self.b_pad = B()
        self.tail = fw.sbuf("tail", [128, 16, 3], F32)
        self.b_tail = B()
        self.tail2 = fw.sbuf("tail2", [128, 16, 2], F32)
        self.b_tail2 = B()
        self.hstate = fw.sbuf("hstate", [128, 16], F32)
        self.b_hstate = B()
        self.small = fw.sbuf("small", [128, 160], F32)
        self.b_small = B()
        self.flw = fw.sbuf("flw", [128, 16, 16], BF16)
        self.b_flw = B()
        self.CsT = fw.sbuf("CsT", [128, 8, 16], F32)
        self.b_CsT = B()
        self.DkT = fw.sbuf("DkT", [128, 8, 16], F32)
        self.b_DkT = B()
        self.dec = fw.sbuf("dec", [128, 8], F32)
        self.b_dec = B()
        self.S = fw.sbuf("S", [128, 128], F32)
        self.b_S = B()
        self.Sbf = fw.sbuf("Sbf", [128, 128], BF16)
        self.b_Sbf = B()
        self.scbf = [fw.sbuf(f"scbf{i}", [128, 128], BF16) for i in range(2)]
        self.b_scbf = [B(), B()]
        self.ps = [fw.psum(f"ps{i}", [128, TB], F32) for i in range(7)]
        self.psb = [B() for _ in range(7)]
        self.pst = fw.psum("pst", [128, T], BF16)
        self.pstb = B()

    def TF(self, i):
        return self.TFt[i], self.TFb[i]

    def TH(self, i):
        return self.THt[i], self.THb[i]

    def P_(self, i):
        return self.ps[i], self.psb[i]

    def pcol(self, name, w=16, off=0):
        o = PC[name] + off
        return self.prm_sb[:, o:o + w]

    def ws_reset(self):
        self.ws_next = 0
        self.ws_issued = 0

    def ws_issue(self, i):
        tag, src, view = self.ws_units[i]
        slot = i % NS
        if view == "col":
            kt = src.shape[0] // 128
            dst = self.ring[slot][:, 0:kt * 128].rearrange("p (kt n) -> p kt n", n=128)
            s = src.rearrange("(kt p) n -> p kt n", p=128)
        elif view == "row":
            dst = self.ring[slot][:, :]
            s = src
        else:
            dst = self.ring[slot][:, 0:128]
            s = src
        self.fw.dma(self.fw.pool, dst, s, writes=[self.ringb[slot]])

    def ws_get(self, tag, src, view):
        i = self.ws_next
        self.ws_next += 1
        if self.fw.planning:
            self.ws_units.append((tag, src, view))
        else:
            assert self.ws_units[i][0] == tag, (self.ws_units[i][0], tag)
            while self.ws_issued <= i:
                self.ws_issue(self.ws_issued)
                self.ws_issued += 1
        return self.ring[i % NS], self.ringb[i % NS]

    def ws_pump(self):
        if self.fw.planning:
            return
        lim = min(len(self.ws_units), self.ws_next + NS)
        while self.ws_issued < lim:
            self.ws_issue(self.ws_issued)
            self.ws_issued += 1

    def mm16(self, p, pb, u, ub, tb, extra_reads=()):
        h = self.h
        fns = [(I("matmul", p[:, :], lhsT=u[:, kt * 128:(kt + 1) * 128], rhs=h[:, kt, tbs(tb)],
                                          start=(kt == 0), stop=(kt == 15))) for kt in range(16)]
        self.fw.multi(self.fw.pe, fns, reads=[ub, self.hb[tb]] + list(extra_reads), writes=[pb])

    def stop_here(self, l, s):
        return self.stop is not None and (l, s) >= tuple(self.stop)

    def setup(self):
        fw = self.fw
        A, Dv, Pl = fw.act, fw.dve, fw.pool
        bc = self.b_const
        fw.dma(fw.sp, self.prm_sb[:, :], self.prm, writes=[self.b_prm])
        fw.op(Dv, I("memset", self.ones_f[:, :], 1.0), writes=[bc])
        fw.op(Dv, I("memset", self.ones_b[:, :], 1.0), writes=[bc])
        fw.op(Dv, I("memset", self.ones16[:, :], 1.0), writes=[bc])
        fw.op(Dv, I("memset", self.maskC[:, :], 1.0), writes=[bc])
        for j in range(4):
            fw.op(Dv, I("memset", self.maskC[:, j * 128:j * 128 + 1], 0.0), reads=[bc], writes=[bc])
        fw.op(Pl, I("memset", self.ident_f[:, :], 0.0), writes=[bc])
        fw.op(Pl, I("affine_select", out=self.ident_f[:, :], in_=self.ident_f[:, :], pattern=[[-1, 128]],
                                            compare_op=ALU.not_equal, fill=1.0, base=0, channel_multiplier=1),
              reads=[bc], writes=[bc])
        self.mtri_f = self.TFt[0][:, 0:128]
        fw.op(Pl, I("memset", self.mtri_f, 1.0), reads=[bc, self.TFb[0]], writes=[bc, self.TFb[0]])
        fw.op(Pl, I("affine_select", out=self.mtri_f, in_=self.mtri_f, pattern=[[1, 128]],
                                            compare_op=ALU.is_ge, fill=0.0, base=0, channel_multiplier=-1),
              reads=[bc, self.TFb[0]], writes=[bc, self.TFb[0]])
        fw.op(Dv, I("tensor_copy", out=self.ident_b[:, :], in_=self.ident_f[:, :]), reads=[bc], writes=[bc])
        fw.op(Dv, I("tensor_copy", out=self.mtri[:, :], in_=self.mtri_f), reads=[bc, self.TFb[0]], writes=[bc])
        fw.op(A, I("activation", out=self.scb[:, :], in_=self.pcol("c"), func=AF.Silu),
              reads=[self.b_prm], writes=[self.b_scb])
        sm = self.small
        bs = self.b_small
        fw.op(A, I("activation", out=sm[:, 0:16], in_=self.pcol("rg_lam"), func=AF.Exp, scale=-1.0),
              reads=[self.b_prm], writes=[bs])
        fw.op(A, I("activation", out=sm[:, 0:16], in_=sm[:, 0:16], func=AF.Ln, bias=1.0), reads=[bs], writes=[bs])
        fw.op(Dv, I("tensor_single_scalar", out=sm[:, 16:32], in_=sm[:, 0:16], scalar=-16.0, op=ALU.mult),
              reads=[bs], writes=[bs])
        fw.op(Dv, I("tensor_single_scalar", out=sm[:, 0:16], in_=sm[:, 0:16], scalar=-8.0, op=ALU.mult),
              reads=[bs], writes=[bs])
        fw.op(A, I("activation", out=sm[:, 64:128], in_=self.pcol("hg_lg", 64), func=AF.Exp),
              reads=[self.b_prm], writes=[bs])
        fw.op(Dv, I("tensor_tensor", out=sm[:, 128:144], in0=sm[:, 64:80], in1=sm[:, 80:96], op=ALU.add), reads=[bs], writes=[bs])
        fw.op(Dv, I("tensor_tensor", out=sm[:, 128:144], in0=sm[:, 128:144], in1=sm[:, 96:112], op=ALU.add), reads=[bs], writes=[bs])
        fw.op(Dv, I("tensor_tensor", out=sm[:, 128:144], in0=sm[:, 128:144], in1=sm[:, 112:128], op=ALU.add), reads=[bs], writes=[bs])
        fw.op(Dv, I("reciprocal", out=sm[:, 128:144], in_=sm[:, 128:144]), reads=[bs], writes=[bs])
        fw.op(Dv, I("tensor_tensor", out=sm[:, 32:48], in0=sm[:, 112:128], in1=sm[:, 128:144], op=ALU.mult), reads=[bs], writes=[bs])
        fw.op(Dv, I("tensor_single_scalar", out=sm[:, 48:64], in_=sm[:, 32:48], scalar=-1.0, op=ALU.mult), reads=[bs], writes=[bs])
        fw.op(Dv, I("tensor_single_scalar", out=sm[:, 144:145], in_=self.pcol("fox_bf", 1), scalar=-1.0, op=ALU.mult),
              reads=[self.b_prm, bs], writes=[bs])
        fw.op(Dv, I("tensor_single_scalar", out=sm[:, 145:146], in_=self.pcol("fox_qg", 1), scalar=float(128 ** -0.5), op=ALU.mult),
              reads=[self.b_prm, bs], writes=[bs])
        if "fox_w_in" in self.W:
            fw.dma(fw.pool, self.flw[:, :, :], self.W["fox_w_in"][:, 8192:8208].rearrange("(kt p) n -> p kt n", p=128),
                   writes=[self.b_flw])
        fw.op(Dv, I("memset", self.tail[:, :, :], 0.0), writes=[self.b_tail])
        fw.op(Dv, I("memset", self.tail2[:, :, :], 0.0), writes=[self.b_tail2])
        fw.op(Dv, I("memset", self.hstate[:, :], 0.0), writes=[self.b_hstate])

    def ada_begin(self, l):
        self.ada_l, self.ada_j, self.ada_fl = l, 0, 0

    def ada_some(self, n):
        fw = self.fw
        l = self.ada_l
        if l is None:
            return
        pN, pNb = self.P_(6)
        Wl = self.W[f"ada_w{l}"]
        for _ in range(n):
            j = self.ada_j
            if j >= 144:
                return
            u, ub = self.ws_get(("ada", l, j), Wl[:, j * 128:(j + 1) * 128], "col")
            fns = [I("matmul", pN[:, j:j + 1], lhsT=u[:, kt * 128:(kt + 1) * 128],
                     rhs=self.scb[:, kt:kt + 1], start=(kt == 0), stop=(kt == 15))
                   for kt in range(16)]
            fw.multi(fw.pe, fns, reads=[ub, self.b_scb], writes=[pNb])
            self.ws_pump()
            self.ada_j += 1

    def ada_flush(self):
        fw = self.fw
        l = self.ada_l
        if l is None:
            return
        pN, pNb = self.P_(6)
        a, b = self.ada_fl, self.ada_j
        bm = self.b_mods[l]
        if b > a:
            o = PC[f"ada_b{l}"]
            fw.op(fw.dve, I("tensor_tensor", out=self.mods[:, l, a:b], in0=pN[:, a:b], in1=self.prm_sb[:, o + a:o + b], op=ALU.add),
                  reads=[pNb, self.b_prm, bm], writes=[bm])
            self.ada_fl = b
        if b >= 144:
            bd = self.b_der[l]
            for s in range(3):
                sc_ = self.mods[:, l, (s * 3 + 1) * 16:(s * 3 + 2) * 16]
                gt_ = self.mods[:, l, (s * 3 + 2) * 16:(s * 3 + 3) * 16]
                fw.op(fw.dve, I("scalar_tensor_tensor", out=self.derA[:, l, s, :], in0=sc_, scalar=1.0,
                                in1=self.pcol(f"ng{l}_{s}"), op0=ALU.add, op1=ALU.mult),
                      reads=[bm, self.b_prm, bd], writes=[bd])
                fw.op(fw.dve, I("tensor_single_scalar", out=self.derG[:, l, s, :], in_=gt_,
                                scalar=(1.0 if s == 1 else 0.5), op=ALU.mult),
                      reads=[bm, bd], writes=[bd])
            self.ada_l = None

    def ada_flush_partial(self):
        if self.ada_l is not None:
            self.ada_flush()

    def ada(self, l):
        self.ada_begin(l)
        self.ada_some(144)
        self.ada_flush()

    def norm(self, l, s):
        fw = self.fw
        x, h = self.x, self.h
        bd = self.b_der[l]
        bm = self.b_mods[l]
        pN, pNb = self.P_(6)
        for tb in range(2):
            for d in range(NT):
                sq, sqb = self.TF(d % 2)
                fw.op(fw.act, I("activation", out=sq[:, :], in_=x[:, d, tbs(tb)], func=AF.Square),
                      reads=[self.xb[d][tb]], writes=[sqb])
                fw.op(fw.pe, I("matmul", pN[:, :], lhsT=self.ones_f[:, :], rhs=sq[:, :], start=(d == 0), stop=(d == 15)),
                      reads=[sqb, self.b_const], writes=[pNb])
            rt, rtb = self.TF(2)
            fw.op(fw.act, I("activation", out=rt[:, :], in_=pN[:, :], func=AF.Sqrt, scale=1.0 / D, bias=EPS),
                  reads=[pNb], writes=[rtb])
            rs, rsb = self.TF(3)
            fw.op(fw.dve, I("reciprocal", out=rs[:, :], in_=rt[:, :]), reads=[rtb], writes=[rsb])
            for d in range(NT):
                tn, tnb = self.TF(4 + d % 2)
                fw.op(fw.dve, I("scalar_tensor_tensor", out=tn[:, :], in0=x[:, d, tbs(tb)], scalar=self.derA[:, l, s, d:d + 1],
                                                                          in1=rs[:, :], op0=ALU.mult, op1=ALU.mult),
                      reads=[self.xb[d][tb], rsb, bd], writes=[tnb])
                sh = self.mods[:, l, (s * 3) * 16 + d:(s * 3) * 16 + d + 1]
                fw.op(fw.act, I("activation", out=h[:, d, tbs(tb)], in_=tn[:, :], func=AF.Identity, bias=sh),
                      reads=[tnb, bm], writes=[self.hb[tb]])

    def ffn(self, l, w):
        fw = self.fw
        s = 0 if w == 0 else 2
        self.norm(l, s)
        bd = self.b_der[l]
        Wg, Wu, Wd = self.W[f"ffn_g{l}_{w}"], self.W[f"ffn_u{l}_{w}"], self.W[f"ffn_d{l}_{w}"]
        x, y = self.x, self.y
        rot = 0
        import os
        nch_dbg = int(os.environ.get("KCH", str(NCH)))
        nodown = int(os.environ.get("KNODOWN", "0"))
        for ch in range(nch_dbg):
            for fi in range(FC):
                ft = ch * FC + fi
                sl = (ch % 2) * FC + fi
                ug, ugb = self.ws_get(("g", l, w, ft), Wg[:, ft * 128:(ft + 1) * 128], "col")
                uu, uub = self.ws_get(("u", l, w, ft), Wu[:, ft * 128:(ft + 1) * 128], "col")
                for tb in range(2):
                    pg, pgb = self.P_(tb)
                    pu, pub = self.P_(2 + tb)
                    self.mm16(pg, pgb, ug, ugb, tb)
                    self.mm16(pu, pub, uu, uub, tb)
                    sg, sgb = self.TF(6 + tb)
                    fw.op(fw.act, I("activation", out=sg[:, :], in_=pg[:, :], func=AF.Silu),
                          reads=[pgb], writes=[sgb])
                    fw.op(fw.dve, I("tensor_tensor", out=y[:, sl, tbs(tb)], in0=sg[:, :], in1=pu[:, :], op=ALU.mult),
                          reads=[sgb, pub], writes=[self.yb[sl][tb]])
                self.ws_pump()
                self.ada_some(2)
            if nodown:
                continue
            uds = [self.ws_get(("d", l, w, ch * FC + fi), Wd[(ch * FC + fi) * 128:(ch * FC + fi + 1) * 128, :], "row")
                   for fi in range(FC)]
            for d in range(NT):
                for tb in range(2):
                    pd, pdb = self.P_(4 + rot % 2)
                    rot += 1
                    fns = [(I("matmul", pd[:, :], lhsT=uds[fi][0][:, d * 128:(d + 1) * 128],
                                                                        rhs=y[:, (ch % 2) * FC + fi, tbs(tb)],
                                                                        start=(fi == 0), stop=(fi == FC - 1)))
                           for fi in range(FC)]
                    fw.multi(fw.pe, fns, reads=[u[1] for u in uds] + [self.yb[(ch % 2) * FC + fi][tb] for fi in range(FC)],
                             writes=[pdb])
                    fw.op(fw.dve, I("scalar_tensor_tensor", out=x[:, d, tbs(tb)], in0=pd[:, :],
                                                                                     scalar=self.derG[:, l, s, d:d + 1], in1=x[:, d, tbs(tb)],
                                                                                     op0=ALU.mult, op1=ALU.add),
                          reads=[pdb, self.xb[d][tb], bd], writes=[self.xb[d][tb]])
            self.ws_pump()
        self.ada_flush_partial()

    def outproj(self, l, Wout, name):
        fw = self.fw
        bd = self.b_der[l]
        x, y = self.x, self.y
        rot = 0
        for d in range(NT):
            u, ub = self.ws_get((name, "out", d), Wout[:, d * 128:(d + 1) * 128], "col")
            for tb in range(2):
                pd, pdb = self.P_(4 + rot % 2)
                rot += 1
                fns = [(I("matmul", pd[:, :], lhsT=u[:, c * 128:(c + 1) * 128], rhs=y[:, c, tbs(tb)],
                                                                  start=(c == 0), stop=(c == 15))) for c in range(16)]
                fw.multi(fw.pe, fns, reads=[ub] + [self.yb[c][tb] for c in range(16)], writes=[pdb])
                fw.op(fw.dve, I("scalar_tensor_tensor", out=x[:, d, tbs(tb)], in0=pd[:, :],
                                                                                 scalar=self.derG[:, l, 1, d:d + 1], in1=x[:, d, tbs(tb)],
                                                                                 op0=ALU.mult, op1=ALU.add),
                      reads=[pdb, self.xb[d][tb], bd], writes=[self.xb[d][tb]])
            self.ws_pump()

    def inproj(self, Win, col0, tag, banks):
        u, ub = self.ws_get(tag, Win[:, col0:col0 + 128], "col")
        out = []
        for tb in range(2):
            p, pb = self.P_(banks[tb])
            self.mm16(p, pb, u, ub, tb)
            out.append((p, pb))
        self.ws_pump()
        return out

    def mixer_rg(self, l, hf):
        fw = self.fw
        A, Dv = fw.act, fw.dve
        self.norm(l, 1)
        Win = self.W["rg_w_in"]
        pad, bp = self.pad, self.b_pad
        sm, bs = self.small, self.b_small
        y = self.y
        for c in range(16):
            pg = self.inproj(Win, c * 128, ("rg", "g", c), (0, 1))
            px = self.inproj(Win, 2048 + c * 128, ("rg", "x", c), (2, 3))
            wr, wrb = self.ws_get(("rg", "wr", c), self.W["rg_w_r"][c], "blk")
            wi, wib = self.ws_get(("rg", "wi", c), self.W["rg_w_i"][c], "blk")
            fw.op(Dv, I("tensor_copy", out=pad[:, 0:3], in_=self.tail[:, c, :]), reads=[self.b_tail], writes=[bp])
            for tb in range(2):
                fw.op(A, I("activation", out=pad[:, 3 + tb * TB:3 + (tb + 1) * TB], in_=px[tb][0][:, :], func=AF.Copy),
                      reads=[px[tb][1]], writes=[bp])
            fw.op(Dv, I("tensor_copy", out=self.tail[:, c, :], in_=pad[:, T:T + 3]), reads=[bp], writes=[self.b_tail])
            for tb in range(2):
                o = tb * TB
                gl, glb = self.TF(0)
                fw.op(A, I("activation", out=gl[:, :], in_=pg[tb][0][:, :], func=AF.Gelu), reads=[pg[tb][1]], writes=[glb])
                xc, xcb = self.TF(1)
                cw = lambda k, c=c: self.pcol("rg_cw", 1, k * 16 + c)
                fw.op(A, I("activation", out=xc[:, :], in_=pad[:, 3 + o:3 + o + TB], func=AF.Identity,
                                                                scale=cw(3), bias=self.pcol("rg_cb", 1, c)),
                      reads=[bp, self.b_prm], writes=[xcb])
                for k in range(3):
                    fw.op(Dv, I("scalar_tensor_tensor", out=xc[:, :], in0=pad[:, k + o:k + o + TB], scalar=cw(k),
                                                                               in1=xc[:, :], op0=ALU.mult, op1=ALU.add),
                          reads=[bp, xcb, self.b_prm], writes=[xcb])
                xh, xhb = self.TH(0)
                fw.op(A, I("activation", out=xh[:, 0:TB], in_=xc[:, :], func=AF.Copy), reads=[xcb], writes=[xhb])
                pr, prb = self.P_(4)
                pi, pib = self.P_(5)
                fw.op(fw.pe, I("matmul", pr[:, :], lhsT=wr[:, 0:128], rhs=xh[:, 0:TB], start=True, stop=True),
                      reads=[wrb, xhb], writes=[prb])
                fw.op(fw.pe, I("matmul", pi[:, :], lhsT=wi[:, 0:128], rhs=xh[:, 0:TB], start=True, stop=True),
                      reads=[wib, xhb], writes=[pib])
                rm, rmb = self.TF(2)
                ig, igb = self.TF(3)
                fw.op(A, I("activation", out=rm[:, :], in_=pr[:, :], func=AF.Sigmoid, bias=self.pcol("rg_br", 1, c)),
                      reads=[prb, self.b_prm], writes=[rmb])
                fw.op(A, I("activation", out=ig[:, :], in_=pi[:, :], func=AF.Sigmoid, bias=self.pcol("rg_bi", 1, c)),
                      reads=[pib, self.b_prm], writes=[igb])
                a_, ab = self.TF(4)
                fw.op(A, I("activation", out=a_[:, :], in_=rm[:, :], func=AF.Exp, scale=sm[:, c:c + 1]),
                      reads=[rmb, bs], writes=[ab])
                fw.op(A, I("activation", out=rm[:, :], in_=rm[:, :], func=AF.Exp, scale=sm[:, 16 + c:17 + c]),
                      reads=[rmb, bs], writes=[rmb])
                fw.op(A, I("activation", out=rm[:, :], in_=rm[:, :], func=AF.Sqrt, scale=-1.0, bias=1.0),
                      reads=[rmb], writes=[rmb])
                fw.op(Dv, I("tensor_tensor", out=ig[:, :], in0=ig[:, :], in1=xc[:, :], op=ALU.mult),
                      reads=[igb, xcb], writes=[igb])
                fw.op(Dv, I("tensor_tensor", out=ig[:, :], in0=ig[:, :], in1=rm[:, :], op=ALU.mult),
                      reads=[igb, rmb], writes=[igb])
                hs, hsb = self.TF(5)
                fw.op(Dv, I("tensor_tensor_scan", out=hs[:, :], data0=a_[:, :], data1=ig[:, :],
                                                                                 initial=self.hstate[:, c:c + 1], op0=ALU.mult, op1=ALU.add),
                      reads=[ab, igb, self.b_hstate], writes=[hsb])
                fw.op(Dv, I("tensor_copy", out=self.hstate[:, c:c + 1], in_=hs[:, TB - 1:TB]),
                      reads=[hsb], writes=[self.b_hstate])
                fw.op(Dv, I("tensor_tensor", out=y[:, c, tbs(tb)], in0=hs[:, :], in1=gl[:, :], op=ALU.mult),
                      reads=[hsb, glb], writes=[self.yb[c][tb]])
        self.outproj(l, self.W["rg_w_out"], "rg")

    def mixer_sc(self, l, hf):
        fw = self.fw
        A, Dv = fw.act, fw.dve
        self.norm(l, 1)
        Win = self.W["sc_w_in"]
        pad, bp = self.pad, self.b_pad
        y = self.y
        for c in range(16):
            pbg = self.inproj(Win, c * 128, ("sc", "b", c), (0, 1))
            pcg = self.inproj(Win, 2048 + c * 128, ("sc", "c", c), (2, 3))
            bgs = []
            for tb in range(2):
                t, tb_ = self.TF(0 + tb)
                fw.op(A, I("activation", out=t[:, :], in_=pbg[tb][0][:, :], func=AF.Copy), reads=[pbg[tb][1]], writes=[tb_])
                bgs.append((t, tb_))
            cgs = []
            for tb in range(2):
                t, tb_ = self.TF(2 + tb)
                fw.op(A, I("activation", out=t[:, :], in_=pcg[tb][0][:, :], func=AF.Copy), reads=[pcg[tb][1]], writes=[tb_])
                cgs.append((t, tb_))
            pxv = self.inproj(Win, 4096 + c * 128, ("sc", "x", c), (0, 1))
            fw.op(Dv, I("tensor_copy", out=pad[:, 0:2], in_=self.tail2[:, c, 0:2]), reads=[self.b_tail2], writes=[bp])
            for tb in range(2):
                fw.op(Dv, I("tensor_tensor", out=pad[:, 2 + tb * TB:2 + (tb + 1) * TB], in0=cgs[tb][0][:, :], in1=pxv[tb][0][:, :], op=ALU.mult),
                      reads=[cgs[tb][1], pxv[tb][1]], writes=[bp])
            fw.op(Dv, I("tensor_copy", out=self.tail2[:, c, 0:2], in_=pad[:, T:T + 2]), reads=[bp], writes=[self.b_tail2])
            for tb in range(2):
                o = tb * TB
                uc, ucb = self.TF(4 + tb)
                cw = lambda k, c=c: self.pcol("sc_cw", 1, k * 16 + c)
                fw.op(A, I("activation", out=uc[:, :], in_=pad[:, 2 + o:2 + o + TB], func=AF.Identity, scale=cw(2)),
                      reads=[bp, self.b_prm], writes=[ucb])
                for k in range(2):
                    fw.op(Dv, I("scalar_tensor_tensor", out=uc[:, :], in0=pad[:, k + o:k + o + TB], scalar=cw(k),
                                                                               in1=uc[:, :], op0=ALU.mult, op1=ALU.add),
                          reads=[bp, ucb, self.b_prm], writes=[ucb])
                fw.op(Dv, I("tensor_tensor", out=y[:, c, tbs(tb)], in0=bgs[tb][0][:, :], in1=uc[:, :], op=ALU.mult),
                      reads=[ucb, bgs[tb][1]], writes=[self.yb[c][tb]])
        self.outproj(l, self.W["sc_w_out"], "sc")

    def rms_part(self, src, srcb, tf_sq, tf_rs, bank):
        fw = self.fw
        sq, sqb = self.TF(tf_sq)
        fw.op(fw.act, I("activation", out=sq[:, :], in_=src[:, :], func=AF.Square), reads=[srcb], writes=[sqb])
        pn, pnb = self.P_(bank)
        fw.op(fw.pe, I("matmul", pn[:, :], lhsT=self.ones_f[:, :], rhs=sq[:, :], start=True, stop=True),
              reads=[sqb, self.b_const], writes=[pnb])
        fw.op(fw.act, I("activation", out=sq[:, :], in_=pn[:, :], func=AF.Sqrt, scale=1.0 / 128, bias=EPS),
              reads=[pnb], writes=[sqb])
        rs, rsb = self.TF(tf_rs)
        fw.op(fw.dve, I("reciprocal", out=rs[:, :], in_=sq[:, :]), reads=[sqb], writes=[rsb])
        return rs, rsb

    def mixer_fox(self, l, hf):
        fw = self.fw
        A, Dv, PE = fw.act, fw.dve, fw.pe
        self.norm(l, 1)
        Win = self.W["fox_w_in"]
        sm, bs = self.small, self.b_small
        y = self.y
        Cs = [self.TF(8), self.TF(9)]
        for tb in range(2):
            pF, pFb = self.P_(4)
            fns = [(I("matmul", pF[0:16, :], lhsT=self.flw[:, kt, :], rhs=self.h[:, kt, tbs(tb)], start=(kt == 0), stop=(kt == 15)))
                   for kt in range(16)]
            fw.multi(PE, fns, reads=[self.b_flw, self.hb[tb]], writes=[pFb])
            t0, t0b = self.TF(0)
            fw.op(A, I("activation", out=t0[0:16, :], in_=pF[0:16, :], func=AF.Exp, scale=-1.0, bias=sm[0:16, 144:145]),
                  reads=[pFb, bs], writes=[t0b])
            fw.op(A, I("activation", out=t0[0:16, :], in_=t0[0:16, :], func=AF.Ln, bias=1.0), reads=[t0b], writes=[t0b])
            cs, csb = Cs[tb]
            init = 0.0 if tb == 0 else Cs[0][0][0:16, TB - 1:TB]
            fw.op(Dv, I("tensor_tensor_scan", out=cs[0:16, :], data0=self.ones16[:, :], data1=t0[0:16, :],
                                                                      initial=init, op0=ALU.mult, op1=ALU.add),
                  reads=[t0b, self.b_const] + ([Cs[0][1]] if tb else []), writes=[csb])
        hi, hib = self.TH(0)
        mid, midb = self.TH(1)
        lo, lob = self.TH(2)
        for tb in range(2):
            cs, csb = Cs[tb]
            r1, r1b = self.TF(0)
            fw.op(Dv, I("tensor_single_scalar", out=hi[0:16, tbs(tb)], in_=cs[0:16, :], scalar=-1.0, op=ALU.mult), reads=[csb], writes=[hib])
            fw.op(Dv, I("scalar_tensor_tensor", out=r1[0:16, :], in0=cs[0:16, :], scalar=-1.0, in1=hi[0:16, tbs(tb)], op0=ALU.mult, op1=ALU.subtract),
                  reads=[csb, hib], writes=[r1b])
            fw.op(Dv, I("tensor_copy", out=mid[0:16, tbs(tb)], in_=r1[0:16, :]), reads=[r1b], writes=[midb])
            fw.op(Dv, I("tensor_tensor", out=lo[0:16, tbs(tb)], in0=r1[0:16, :], in1=mid[0:16, tbs(tb)], op=ALU.subtract), reads=[r1b, midb], writes=[lob])
        for i, (t, tb_) in enumerate(((hi, hib), (mid, midb), (lo, lob))):
            fw.dma(fw.sp, self.F3d[i], t[0:16, :], reads=[tb_], writes=[self.b_F3d])
        pT, pTb = self.P_(5)
        if hf == 0:
            for kt in range(8):
                cs, csb = Cs[kt // 4]
                fw.op(PE, I("transpose", out=pT[:, kt * 16:(kt + 1) * 16], in_=cs[0:16, (kt % 4) * 128:(kt % 4 + 1) * 128],
                                                             identity=self.ident_f[0:16, 0:16]), reads=[csb, self.b_const], writes=[pTb])
        else:
            for kt in range(8):
                cs, csb = Cs[kt // 4]
                fw.op(PE, I("transpose", out=pT[:, kt * 16:(kt + 1) * 16], in_=cs[0:16, (kt % 4) * 128:(kt % 4 + 1) * 128],
                                                             identity=self.ident_f[0:16, 0:16]), reads=[csb, self.b_const], writes=[pTb])
        fw.op(Dv, I("tensor_copy", out=self.CsT[:, :, :], in_=pT[:, 0:128].rearrange("p (k h) -> p k h", h=16)), reads=[pTb], writes=[self.b_CsT])
        if hf == 0 and self.n_halves > 1:
            for tb in range(2):
                cs, csb = Cs[tb]
                d0, d0b = self.TF(0 + tb)
                fw.op(Dv, I("tensor_single_scalar", out=d0[0:16, :], in_=cs[0:16, :], scalar=Cs[1][0][0:16, TB - 1:TB], op=ALU.subtract),
                      reads=[csb, Cs[1][1]], writes=[d0b])
            pT2, pT2b = self.P_(4)
            for kt in range(8):
                d0, d0b = self.TF(kt // 4)
                fw.op(PE, I("transpose", out=pT2[:, kt * 16:(kt + 1) * 16], in_=d0[0:16, (kt % 4) * 128:(kt % 4 + 1) * 128],
                                                             identity=self.ident_f[0:16, 0:16]), reads=[d0b, self.b_const], writes=[pT2b])
            fw.op(Dv, I("tensor_copy", out=self.DkT[:, :, :], in_=pT2[:, 0:128].rearrange("p (k h) -> p k h", h=16)), reads=[pT2b], writes=[self.b_DkT])
        f3, f3b = self.TH(6)
        fw.op(Dv, I("memset", f3[:, :], 0.0), writes=[f3b])
        for hd in range(16):
            pq = self.inproj(Win, hd * 128, ("fox", "q", hd), (0, 1))
            pk = self.inproj(Win, 2048 + hd * 128, ("fox", "k", hd), (2, 3))
            qn, qnb = self.TH(0)
            kn, knb = self.TH(1)
            for (pp, dst, dstb, gcol) in ((pq, qn, qnb, sm[:, 145:146]), (pk, kn, knb, self.pcol("fox_kg", 1))):
                for tb in range(2):
                    qs, qsb = self.TF(2)
                    fw.op(A, I("activation", out=qs[:, :], in_=pp[tb][0][:, :], func=AF.Copy), reads=[pp[tb][1]], writes=[qsb])
                    rs, rsb = self.rms_part(qs, qsb, 3, 4, 6)
                    fw.op(Dv, I("scalar_tensor_tensor", out=dst[:, tbs(tb)], in0=qs[:, :], scalar=gcol, in1=rs[:, :],
                                                                                                        op0=ALU.mult, op1=ALU.mult),
                          reads=[qsb, rsb, bs, self.b_prm], writes=[dstb])
            pv = self.inproj(Win, 4096 + hd * 128, ("fox", "v", hd), (0, 1))
            pg = self.inproj(Win, 6144 + hd * 128, ("fox", "g", hd), (2, 3))
            vfm, vfmb = self.TH(2)
            for tb in range(2):
                fw.op(A, I("activation", out=vfm[:, tbs(tb)], in_=pv[tb][0][:, :], func=AF.Copy), reads=[pv[tb][1]], writes=[vfmb])
            for n in range(8):
                fw.op(PE, I("transpose", out=self.pst[:, n * 128:(n + 1) * 128], in_=vfm[:, n * 128:(n + 1) * 128], identity=self.ident_b[:, :]),
                      reads=[vfmb, self.b_const], writes=[self.pstb])
            vt, vtb = self.TH(3)
            fw.op(Dv, I("tensor_copy", out=vt[:, :], in_=self.pst[:, :]), reads=[self.pstb], writes=[vtb])
            sgm = []
            for tb in range(2):
                t, tb_ = self.TF(6 + tb)
                fw.op(A, I("activation", out=t[:, :], in_=pg[tb][0][:, :], func=AF.Sigmoid), reads=[pg[tb][1]], writes=[tb_])
                sgm.append((t, tb_))
            fw.dma(fw.sp, f3[0:3, :], self.F3d[:, hd, :], reads=[self.b_F3d], writes=[f3b])
            if hf == 0 and self.n_halves > 1:
                fw.dma(fw.sp, self.Kd[hd], kn[:, :], reads=[knb], writes=[self.b_Kd[hd]])
                fw.dma(fw.sp, self.Vd[hd], vt[:, :], reads=[vtb], writes=[self.b_Vd[hd]])
            if hf == 1:
                kp, kpb = self.TH(4)
                vp, vpb = self.TH(5)
                fw.dma(fw.sp, kp[:, :], self.Kd[hd], reads=[self.b_Kd[hd]], writes=[kpb])
                fw.dma(fw.sp, vp[:, :], self.Vd[hd], reads=[self.b_Vd[hd]], writes=[vpb])
            for qb in range(2):
                tiles = []
                if hf == 1:
                    tiles += [("p", kt, 0) for kt in range(8)]
                for kt in range(4 * qb + 4):
                    j = kt - 4 * qb
                    tiles.append(("o", kt, 0 if j < 0 else 128 * j))
                po, pob = self.P_(6)
                pl, plb = self.P_(3)
                nt_ = len(tiles)
                for i, (kind, kt, q0) in enumerate(tiles):
                    psc, pscb = self.P_(4 + i % 2)
                    Ksrc, Kb = (kp, kpb) if kind == "p" else (kn, knb)
                    Vsrc, Vb = (vp, vpb) if kind == "p" else (vt, vtb)
                    qsl = slice(qb * TB + q0, (qb + 1) * TB)
                    fns = [I("matmul", psc[:, q0:TB], lhsT=Ksrc[:, kt * 128:(kt + 1) * 128], rhs=qn[:, qsl], start=True, stop=False),
                           I("matmul", psc[:, q0:TB], lhsT=self.ones_b[:, :], rhs=f3[:, qsl], start=False, stop=True)]
                    fw.multi(PE, fns, reads=[Kb, qnb, f3b, self.b_const], writes=[pscb])
                    if kind == "p":
                        bias = self.DkT[:, kt, hd:hd + 1]
                        bb = self.b_DkT
                    else:
                        bias = self.CsT[:, kt, hd:hd + 1]
                        bb = self.b_CsT
                    ptt, ptb = self.PT[i % 2], self.PTb[i % 2]
                    fw.op(A, I("activation", out=ptt[:, q0:TB], in_=psc[:, q0:TB], func=AF.Exp, bias=bias),
                          reads=[pscb, bb], writes=[ptb])
                    if kind == "o" and kt >= 4 * qb:
                        fw.op(Dv, I("tensor_tensor", out=ptt[:, q0:q0 + 128], in0=ptt[:, q0:q0 + 128], in1=self.mtri[:, :], op=ALU.mult),
                              reads=[ptb, self.b_const], writes=[ptb])
                    fns = [I("matmul", po[:, q0:TB], lhsT=Vsrc[:, kt * 128:(kt + 1) * 128], rhs=ptt[:, q0:TB], start=(i == 0), stop=(i == nt_ - 1)),
                           I("matmul", pl[:, q0:TB], lhsT=self.ones_b[:, :], rhs=ptt[:, q0:TB], start=(i == 0), stop=(i == nt_ - 1))]
                    fw.multi(PE, fns, reads=[Vb, ptb, self.b_const], writes=[pob, plb])
                rl, rlb = self.TF(2)
                fw.op(Dv, I("reciprocal", out=rl[:, :], in_=pl[:, :]), reads=[plb], writes=[rlb])
                on, onb = self.TF(3)
                fw.op(Dv, I("tensor_tensor", out=on[:, :], in0=po[:, :], in1=rl[:, :], op=ALU.mult), reads=[pob, rlb], writes=[onb])
                fw.op(Dv, I("tensor_tensor", out=y[:, hd, tbs(qb)], in0=on[:, :], in1=sgm[qb][0][:, :], op=ALU.mult),
                      reads=[onb, sgm[qb][1]], writes=[self.yb[hd][qb]])
        self.outproj(l, self.W["fox_w_out"], "fox")

    def mixer_hg(self, l, hf):
        fw = self.fw
        A, Dv, PE = fw.act, fw.dve, fw.pe
        self.norm(l, 1)
        Win = self.W["hg_w_in"]
        sm, bs = self.small, self.b_small
        y = self.y
        for hd in range(16):
            pq = self.inproj(Win, hd * 128, ("hg", "q", hd), (0, 1))
            pf = self.inproj(Win, 2048 + hd * 128, ("hg", "f", hd), (2, 3))
            qin, qinb = self.TH(0)
            kin, kinb = self.TH(1)
            kdec, kdecb = self.TH(2)
            for tb in range(2):
                qs, qsb = self.TF(0)
                fw.op(A, I("activation", out=qs[:, :], in_=pq[tb][0][:, :], func=AF.Copy), reads=[pq[tb][1]], writes=[qsb])
                sn, snb = self.TF(1)
                fw.op(A, I("activation", out=sn[:, :], in_=pf[tb][0][:, :], func=AF.Sigmoid, scale=-1.0), reads=[pf[tb][1]], writes=[snb])
                kk, kkb = self.TF(2)
                fw.op(Dv, I("tensor_single_scalar", out=kk[:, :], in_=sn[:, :], scalar=sm[:, 32 + hd:33 + hd], op=ALU.mult), reads=[snb, bs], writes=[kkb])
                fw.op(Dv, I("tensor_scalar", out=sn[:, :], in0=sn[:, :], scalar1=sm[:, 48 + hd:49 + hd], scalar2=1.0, op0=ALU.mult, op1=ALU.add),
                      reads=[snb, bs], writes=[snb])
                fw.op(A, I("activation", out=sn[:, :], in_=sn[:, :], func=AF.Ln), reads=[snb], writes=[snb])
                bc_, bcb = self.TF(3)
                fw.op(Dv, I("tensor_tensor_scan", out=bc_[:, :], data0=self.maskC[:, :], data1=sn[:, :], initial=0.0, op0=ALU.mult, op1=ALU.add),
                      reads=[snb, self.b_const], writes=[bcb])
                eb, ebb = self.TF(4)
                fw.op(A, I("activation", out=eb[:, :], in_=bc_[:, :], func=AF.Exp), reads=[bcb], writes=[ebb])
                fw.op(Dv, I("tensor_tensor", out=qin[:, tbs(tb)], in0=qs[:, :], in1=eb[:, :], op=ALU.mult), reads=[qsb, ebb], writes=[qinb])
                enb, enbb = self.TF(5)
                fw.op(A, I("activation", out=enb[:, :], in_=bc_[:, :], func=AF.Exp, scale=-1.0), reads=[bcb], writes=[enbb])
                fw.op(Dv, I("tensor_tensor", out=kin[:, tbs(tb)], in0=kk[:, :], in1=enb[:, :], op=ALU.mult), reads=[kkb, enbb], writes=[kinb])
                for n in range(4):
                    fw.op(A, I("activation", out=self.dec[:, tb * 4 + n:tb * 4 + n + 1], in_=bc_[:, n * 128 + 127:n * 128 + 128], func=AF.Exp),
                          reads=[bcb], writes=[self.b_dec])
                for n in range(4):
                    c0 = tb * TB + n * 128
                    fw.op(Dv, I("tensor_single_scalar", out=kdec[:, c0:c0 + 128], in_=kin[:, c0:c0 + 128],
                                                                                scalar=self.dec[:, tb * 4 + n:tb * 4 + n + 1], op=ALU.mult),
                          reads=[kinb, self.b_dec], writes=[kdecb])
            pv = self.inproj(Win, 4096 + hd * 128, ("hg", "v", hd), (0, 1))
            pg = self.inproj(Win, 6144 + hd * 128, ("hg", "g", hd), (2, 3))
            vfm, vfmb = self.TH(3)
            for tb in range(2):
                fw.op(A, I("activation", out=vfm[:, tbs(tb)], in_=pv[tb][0][:, :], func=AF.Copy), reads=[pv[tb][1]], writes=[vfmb])
            sg = []
            for tb in range(2):
                t, tb_ = self.TF(6 + tb)
                fw.op(A, I("activation", out=t[:, :], in_=pg[tb][0][:, :], func=AF.Silu), reads=[pg[tb][1]], writes=[tb_])
                sg.append((t, tb_))
            vt, vtb = self.TH(4)
            kdt, kdtb = self.TH(5)
            for (src, srcb, dst, dstb, eng) in ((vfm, vfmb, vt, vtb, Dv), (kdec, kdecb, kdt, kdtb, A)):
                for n in range(8):
                    fw.op(PE, I("transpose", out=self.pst[:, n * 128:(n + 1) * 128], in_=src[:, n * 128:(n + 1) * 128], identity=self.ident_b[:, :]),
                          reads=[srcb, self.b_const], writes=[self.pstb])
                if eng is Dv:
                    fw.op(Dv, I("tensor_copy", out=dst[:, :], in_=self.pst[:, :]), reads=[self.pstb], writes=[dstb])
                else:
                    fw.op(A, I("activation", out=dst[:, :], in_=self.pst[:, :], func=AF.Copy), reads=[self.pstb], writes=[dstb])
            S, Sb, Sbf, Sbfb = self.S, self.b_S, self.Sbf, self.b_Sbf
            if hf == 0:
                fw.op(Dv, I("memset", S[:, :], 0.0), reads=[Sb], writes=[Sb])
            else:
                fw.dma(fw.sp, S[:, :], self.Sd[hd], reads=[self.b_Sd[hd]], writes=[Sb])
            fw.op(A, I("activation", out=Sbf[:, :], in_=S[:, :], func=AF.Copy), reads=[Sb], writes=[Sbfb])
            po, pob = self.P_(6)
            for n in range(8):
                tb = n // 4
                cs = slice(n * 128, (n + 1) * 128)
                psc, pscb = self.P_(4)
                fw.op(PE, I("matmul", psc[:, 0:128], lhsT=kin[:, cs], rhs=qin[:, cs], start=True, stop=True),
                      reads=[kinb, qinb], writes=[pscb])
                scb_, scbb = self.scbf[n % 2], self.b_scbf[n % 2]
                fw.op(Dv, I("tensor_tensor", out=scb_[:, :], in0=psc[:, 0:128], in1=self.mtri[:, :], op=ALU.mult),
                      reads=[pscb, self.b_const], writes=[scbb])
                oc = slice((n % 4) * 128, (n % 4 + 1) * 128)
                fns = [I("matmul", po[:, oc], lhsT=vt[:, cs], rhs=scb_[:, :], start=True, stop=False),
                       I("matmul", po[:, oc], lhsT=Sbf[:, :], rhs=qin[:, cs], start=False, stop=True)]
                fw.multi(PE, fns, reads=[vtb, scbb, Sbfb, qinb], writes=[pob])
                pU, pUb = self.P_(5)
                fw.op(PE, I("matmul", pU[:, 0:128], lhsT=kdt[:, cs], rhs=vt[:, cs], start=True, stop=True),
                      reads=[kdtb, vtb], writes=[pUb])
                fw.op(Dv, I("scalar_tensor_tensor", out=S[:, :], in0=S[:, :], scalar=self.dec[:, n:n + 1], in1=pU[:, 0:128], op0=ALU.mult, op1=ALU.add),
                      reads=[Sb, self.b_dec, pUb], writes=[Sb])
                fw.op(A, I("activation", out=Sbf[:, :], in_=S[:, :], func=AF.Copy), reads=[Sb], writes=[Sbfb])
                if n % 4 == 3:
                    osq, osqb = self.TF(8)
                    fw.op(A, I("activation", out=osq[:, :], in_=po[:, :], func=AF.Square), reads=[pob], writes=[osqb])
                    pn, pnb = self.P_(2)
                    fw.op(PE, I("matmul", pn[:, :], lhsT=self.ones_f[:, :], rhs=osq[:, :], start=True, stop=True), reads=[osqb, self.b_const], writes=[pnb])
                    fw.op(A, I("activation", out=osq[:, :], in_=pn[:, :], func=AF.Sqrt, scale=1.0 / 128, bias=EPS), reads=[pnb], writes=[osqb])
                    rs, rsb = self.TF(9)
                    fw.op(Dv, I("reciprocal", out=rs[:, :], in_=osq[:, :]), reads=[osqb], writes=[rsb])
                    fw.op(Dv, I("scalar_tensor_tensor", out=rs[:, :], in0=po[:, :], scalar=self.pcol("hg_ng", 1), in1=rs[:, :], op0=ALU.mult, op1=ALU.mult),
                          reads=[pob, rsb, self.b_prm], writes=[rsb])
                    fw.op(Dv, I("tensor_tensor", out=y[:, hd, tbs(tb)], in0=rs[:, :], in1=sg[tb][0][:, :], op=ALU.mult),
                          reads=[rsb, sg[tb][1]], writes=[self.yb[hd][tb]])
            if hf == 0 and self.n_halves > 1:
                fw.dma(fw.sp, self.Sd[hd], S[:, :], reads=[Sb], writes=[self.b_Sd[hd]])
        self.outproj(l, self.W["hg_w_out"], "hg")

    def emit_program(self):
        fw = self.fw
        self.ada_l = None
        self.ws_reset()
        self.setup()
        for hf in range(self.n_halves):
            for d in range(NT):
                fw.dma(fw.sp, self.x[:, d, :], self.xT[hf, d * 128:(d + 1) * 128, :], writes=[self.xb[d][0], self.xb[d][1]])
            stop_all = False
            for l in self.layers:
                if self.dbg >= 2 and hf == 0:
                    if l == self.layers[0]:
                        self.ada(l)
                    else:
                        if self.ada_l == l:
                            self.ada_some(144)
                            self.ada_flush()
                    nxt = [m for m in self.layers if m > l]
                    if nxt and self.stop is None:
                        self.ada_begin(nxt[0])
                if self.dbg == 3:
                    self.norm(l, 0)
                if self.dbg < 4:
                    break
                for s_ in range(3):
                    if s_ == 1:
                        [self.mixer_rg, self.mixer_sc, self.mixer_fox, self.mixer_hg][l % 4](l, hf)
                    else:
                        self.ffn(l, 0 if s_ == 0 else 1)
                    if self.stop is not None and tuple(self.stop) == (l, s_):
                        stop_all = True
                        break
                if stop_all:
                    break
            for d in range(NT):
                fw.dma(fw.sp, self.oT[hf, d * 128:(d + 1) * 128, :], self.x[:, d, :], reads=[self.xb[d][0], self.xb[d][1]])

    def build(self):
        with ExitStack() as st:
            self.alloc(st)
            self.ws_units = []
            self.fw.planning = True
            self.emit_program()
            self.fw.planning = False
            self.emit_program()
            self.fw.finish()
            self.fw.emit()
        return self.nc


_CACHE = {}


ACTIVE = (0, 1, 4, 5)


def run(inputs, layers=(0, 1, 2, 3), n_halves=2, stop=None, trace=False, n_cores=N_CORES, spread=False):
    key = (tuple(layers), n_halves, stop)
    if key not in _CACHE:
        _CACHE[key] = Prog(layers, n_halves, stop).build()
    nc = _CACHE[key]
    x = np.asarray(inputs["x"], np.float32)
    wts = weight_arrays(inputs, layers)
    maps = []
    for b in range(n_cores):
        xt = np.ascontiguousarray(x[b].T.reshape(D, 2, T).transpose(1, 0, 2))
        m = {"xT": xt, "prm": pack_params(inputs, b)}
        m.update(wts)
        maps.append(m)
    if spread:
        zero = {k: np.zeros_like(v) for k, v in maps[0].items()}
        in_maps = [zero] * 8
        in_maps = list(in_maps)
        for b in range(n_cores):
            in_maps[ACTIVE[b]] = maps[b]
        cores = [ACTIVE[b] for b in range(n_cores)]
    else:
        in_maps = maps
        cores = list(range(n_cores))
    res = run_bass_kernel_spmd(nc, in_maps, core_ids=list(range(len(in_maps))), **({"trace": True} if trace else {}))
    out = np.empty((n_cores, 2 * T, D), np.float32)
    for b in range(n_cores):
        o = res.results[cores[b]]["oT"]
        out[b] = o.transpose(0, 2, 1).reshape(2 * T, D)
    return out, res


def kernel(**inputs):
    out, _ = run(inputs)
    return out
```

```python
import numpy as np
from contextlib import ExitStack
import concourse.bass as bass
import concourse.mybir as mybir
from concourse.bass_utils import run_bass_kernel_spmd

F32 = mybir.dt.float32
BF16 = mybir.dt.bfloat16
AF = mybir.ActivationFunctionType
ALU = mybir.AluOpType

D = 2048
NT = 16
T = 1024
TB = 512
FF = 5632
NF = 44
FC = 4
NCH = NF // FC
EPS = 1e-6
NS = 6
NTF = 10
NTH = 7
N_CORES = 4


def I(name, *args, **kwargs):
    return (name, args, kwargs)


class Eng:
    def __init__(self, name, sem, unit=1):
        self.name, self.sem, self.unit = name, sem, unit
        self.count = 0
        self.seen = {}
        self.ops = []


class Buf:
    __slots__ = ("name", "w", "r")

    def __init__(self, name=""):
        self.name = name
        self.w = None
        self.r = {}


class FW:
    def __init__(self, nc, stack, n_chan=6):
        self.nc = nc
        self.stack = stack
        self.planning = False
        sem = lambda n: stack.enter_context(nc.semaphore(n))
        self.pe = Eng("pe", sem("s_pe"))
        self.act = Eng("act", sem("s_act"))
        self.dve = Eng("dve", sem("s_dve"))
        self.pool = Eng("pool", sem("s_pool"))
        self.sp = Eng("sp", sem("s_sp"))
        self.chans = {
            "sp": [Eng(f"chs{i}", sem(f"s_chs{i}"), unit=16) for i in range(n_chan)],
            "pool": [Eng(f"chp{i}", sem(f"s_chp{i}"), unit=16) for i in range(n_chan)],
        }
        self.chan_i = {"sp": 0, "pool": 0}
        self.cch = Eng("cch", sem("s_cch"), unit=16)

    def buf(self, name=""):
        return Buf(name)

    def sbuf(self, name, shape, dt):
        return self.stack.enter_context(self.nc.sbuf_tensor(name, list(shape), dt))

    def psum(self, name, shape, dt=F32):
        return self.stack.enter_context(self.nc.psum_tensor(name, list(shape), dt))

    def _deps(self, E, reads, writes):
        deps = {}
        for b in reads:
            if b.w is not None:
                e, i = b.w
                if deps.get(e, 0) < i:
                    deps[e] = i
        for b in writes:
            if b.w is not None:
                e, i = b.w
                if deps.get(e, 0) < i:
                    deps[e] = i
            for e, i in b.r.items():
                if deps.get(e, 0) < i:
                    deps[e] = i
        waits = []
        for e, i in deps.items():
            if e is E and E is self.pe:
                continue
            if E.seen.get(e, 0) < i:
                E.seen[e] = i
                waits.append((e.sem, i * e.unit))
        return waits

    def _mark(self, E, idx, reads, writes):
        for b in writes:
            b.w = (E, idx)
            b.r = {}
        for b in reads:
            if b.w is not None and b.w[0] is E and b.w[1] == idx:
                continue
            b.r[E] = idx

    def op(self, E, fn, reads=(), writes=()):
        if self.planning:
            return
        waits = self._deps(E, reads, writes)
        E.count += 1
        idx = E.count
        sem = E.sem

        def run(h, waits=waits, fn=fn, sem=sem):
            for s, v in waits:
                h.wait_ge(s, v)
            getattr(h, fn[0])(*fn[1], **fn[2]).then_inc(sem, 1)

        E.ops.append(run)
        self._mark(E, idx, reads, writes)

    def multi(self, E, fns, reads=(), writes=()):
        if self.planning:
            return
        waits = self._deps(E, reads, writes)
        E.count += 1
        idx = E.count
        sem = E.sem

        def run(h, waits=waits, fns=fns, sem=sem):
            for s, v in waits:
                h.wait_ge(s, v)
            for f in fns[:-1]:
                getattr(h, f[0])(*f[1], **f[2])
            f = fns[-1]
            getattr(h, f[0])(*f[1], **f[2]).then_inc(sem, 1)

        E.ops.append(run)
        self._mark(E, idx, reads, writes)

    def dma(self, Q, out_ap, in_ap, reads=(), writes=()):
        if self.planning:
            return
        chl = self.chans[Q.name]
        ch = chl[self.chan_i[Q.name] % len(chl)]
        self.chan_i[Q.name] += 1
        waits = self._deps(Q, reads, writes)
        if ch.count > 0 and Q.seen.get(ch, 0) < ch.count:
            Q.seen[ch] = ch.count
            waits.append((ch.sem, ch.count * 16))
        ch.count += 1
        idx = ch.count
        sem = ch.sem

        def run(h, waits=waits, sem=sem):
            for s, v in waits:
                h.wait_ge(s, v)
            h.dma_start(out=out_ap, in_=in_ap).then_inc(sem, 16)

        Q.ops.append(run)
        for b in writes:
            b.w = (ch, idx)
            b.r = {}
        for b in reads:
            b.r[ch] = idx

    def coll(self, src, dst, groups, reads=(), writes=()):
        if self.planning:
            return
        Q = self.pool
        ch = self.cch
        waits = self._deps(Q, reads, writes)
        if ch.count > 0 and Q.seen.get(ch, 0) < ch.count:
            Q.seen[ch] = ch.count
            waits.append((ch.sem, ch.count * 16))
        ch.count += 1
        idx = ch.count
        sem = ch.sem

        def run(h, waits=waits, sem=sem):
            for s_, v in waits:
                h.wait_ge(s_, v)
            h.collective_compute("AllGather", ALU.bypass, replica_groups=groups, ins=[src], outs=[dst]).then_inc(sem, 16)

        Q.ops.append(run)
        for b in writes:
            b.w = (ch, idx)
            b.r = {}
        for b in reads:
            b.r[ch] = idx

    def finish(self):
        waits = []
        for chl in self.chans.values():
            for ch in chl:
                if ch.count:
                    waits.append((ch.sem, ch.count * 16))
        for E in (self.pe, self.act, self.dve, self.pool):
            if E.count:
                waits.append((E.sem, E.count))
        if self.cch.count:
            waits.append((self.cch.sem, self.cch.count * 16))

        def run(h):
            for s, v in waits:
                h.wait_ge(s, v)

        self.sp.ops.append(run)

    def emit(self):
        with self.nc.Block() as block:
            @block.sync
            def _(h):
                for f in self.sp.ops:
                    f(h)

            @block.tensor
            def _(h):
                for f in self.pe.ops:
                    f(h)

            @block.scalar
            def _(h):
                for f in self.act.ops:
                    f(h)

            @block.vector
            def _(h):
                for f in self.dve.ops:
                    f(h)

            @block.gpsimd
            def _(h):
                for f in self.pool.ops:
                    f(h)


PC = {}


def _pc_layout():
    off = 0

    def add(n, w):
        nonlocal off
        PC[n] = off
        off += w

    add("c", 16)
    for l in range(4):
        add(f"ada_b{l}", 144)
    for l in range(4):
        for s in range(3):
            add(f"ng{l}_{s}", 16)
    add("rg_cw", 64)
    add("rg_cb", 16)
    add("rg_br", 16)
    add("rg_bi", 16)
    add("rg_lam", 16)
    add("sc_cw", 48)
    add("fox_qg", 1)
    add("fox_kg", 1)
    add("fox_bf", 1)
    add("hg_lg", 64)
    add("hg_ng", 1)
    return off


NP = _pc_layout()


def _v16(v):
    return np.asarray(v, np.float32).reshape(-1, 128).T


def pack_params(inp, b):
    P = np.zeros((128, NP), np.float32)

    def put(name, arr):
        a = np.asarray(arr, np.float32)
        P[: a.shape[0], PC[name]: PC[name] + a.shape[1]] = a

    put("c", _v16(inp["c"][b]))
    for l in range(4):
        put(f"ada_b{l}", _v16(inp["ada_b"][l]))
        for s in range(3):
            put(f"ng{l}_{s}", _v16(inp["norm_g"][l, s]))
    put("rg_cw", np.concatenate([_v16(inp["rg_conv_w"][0, k]) for k in range(4)], axis=1))
    put("rg_cb", _v16(inp["rg_conv_b"][0]))
    put("rg_br", _v16(inp["rg_b_r"][0]))
    put("rg_bi", _v16(inp["rg_b_i"][0]))
    put("rg_lam", _v16(inp["rg_lam"][0]))
    put("sc_cw", np.concatenate([_v16(inp["sc_conv_w"][0, k]) for k in range(3)], axis=1))
    put("fox_qg", np.asarray(inp["fox_q_gain"][0]).reshape(128, 1))
    put("fox_kg", np.asarray(inp["fox_k_gain"][0]).reshape(128, 1))
    put("fox_bf", np.asarray(inp["fox_b_f"][0]).reshape(16, 1))
    put("hg_lg", np.concatenate([_v16(inp["hg_lb_logits"][l]) for l in range(4)], axis=1))
    put("hg_ng", np.asarray(inp["hg_norm_gain"][0]).reshape(128, 1))
    return P


MIXW = {
    0: {"rg_w_in": [2048, 4096], "rg_w_r": [16, 128, 128], "rg_w_i": [16, 128, 128], "rg_w_out": [2048, 2048]},
    1: {"sc_w_in": [2048, 6144], "sc_w_out": [2048, 2048]},
    2: {"fox_w_in": [2048, 8208], "fox_w_out": [2048, 2048]},
    3: {"hg_w_in": [2048, 8192], "hg_w_out": [2048, 2048]},
}


def weight_shapes(layers):
    ws = {}
    for l in layers:
        ws[f"ada_w{l}"] = [2048, 18432]
        for w in range(2):
            ws[f"ffn_g{l}_{w}"] = [2048, 5632]
            ws[f"ffn_u{l}_{w}"] = [2048, 5632]
            ws[f"ffn_d{l}_{w}"] = [5632, 2048]
        ws.update(MIXW[l % 4])
    return ws


def weight_arrays(inp, layers):
    out = {}
    for name in weight_shapes(layers):
        if name.startswith("ada_w"):
            a = inp["ada_w"][int(name[5:])]
        elif name.startswith("ffn_"):
            l, w = int(name[5]), int(name[7])
            a = inp[{"g": "ffn_w_gate", "u": "ffn_w_up", "d": "ffn_w_down"}[name[4]]][l, w]
        else:
            a = inp[name][0]
        out[name] = np.ascontiguousarray(np.asarray(a, np.float32))
    return out


def tbs(tb):
    return slice(tb * TB, (tb + 1) * TB)


class Prog:
    def __init__(self, layers=(0, 1, 2, 3), n_halves=2, stop=None):
        self.layers = tuple(layers)
        self.n_halves = n_halves
        self.stop = stop
        import os
        self.dbg = int(os.environ.get("KDBG", "9"))
        self.nc = bass.Bass("TRN2", target_bir_lowering=False)

    def alloc(self, st):
        nc = self.nc
        fw = self.fw = FW(nc, st)
        dr = lambda n, s, dt, kind: nc.dram_tensor(n, list(s), dt, kind=kind).ap()
        self.xT = dr("xT", [2, D, T], F32, "ExternalInput")
        self.oT = dr("oT", [2, D, T], F32, "ExternalOutput")
        self.prm = dr("prm", [128, NP], F32, "ExternalInput")
        self.W = {k: dr(k, s, F32, "ExternalInput") for k, s in weight_shapes(self.layers).items()}
        self.Kd = dr("Kd", [16, 128, T], BF16, "Internal")
        self.Vd = dr("Vd", [16, 128, T], BF16, "Internal")
        self.Sd = dr("Sd", [16, 128, 128], F32, "Internal")
        self.F3d = dr("F3d", [3, 16, T], BF16, "Internal")
        B = fw.buf
        self.b_Kd = [B() for _ in range(16)]
        self.b_Vd = [B() for _ in range(16)]
        self.b_Sd = [B() for _ in range(16)]
        self.b_F3d = B()

        self.x = fw.sbuf("x", [128, NT, T], F32)
        self.xb = [[B() for _ in range(2)] for _ in range(NT)]
        self.h = fw.sbuf("h", [128, NT, T], BF16)
        self.hb = [B(), B()]
        self.y = fw.sbuf("y", [128, NT, T], BF16)
        self.yb = [[B() for _ in range(2)] for _ in range(NT)]
        self.ring = [fw.sbuf(f"ring{i}", [128, 2048], BF16) for i in range(NS)]
        self.ringb = [B() for _ in range(NS)]
        self.TFt = [fw.sbuf(f"tf{i}", [128, TB], F32) for i in range(NTF)]
        self.TFb = [B() for _ in range(NTF)]
        self.THt = [fw.sbuf(f"th{i}", [128, T], BF16) for i in range(NTH)]
        self.THb = [B() for _ in range(NTH)]
        self.PT = [fw.sbuf(f"pt{i}", [128, TB], BF16) for i in range(2)]
        self.PTb = [B(), B()]
        self.prm_sb = fw.sbuf("prm_sb", [128, NP], F32)
        self.b_prm = B()
        self.mods = fw.sbuf("mods", [128, 4, 144], F32)
        self.b_mods = [B() for _ in range(4)]
        self.derA = fw.sbuf("derA", [128, 4, 3, 16], F32)
        self.derG = fw.sbuf("derG", [128, 4, 3, 16], F32)
        self.b_der = [B() for _ in range(4)]
        self.ones_f = fw.sbuf("ones_f", [128, 128], F32)
        self.ones_b = fw.sbuf("ones_b", [128, 128], BF16)
        self.ident_f = fw.sbuf("ident_f", [128, 128], F32)
        self.ident_b = fw.sbuf("ident_b", [128, 128], BF16)
        self.mtri = fw.sbuf("mtri", [128, 128], BF16)
        self.maskC = fw.sbuf("maskC", [128, TB], BF16)
        self.ones16 = fw.sbuf("ones16", [16, TB], BF16)
        self.b_const = B()
        self.scb = fw.sbuf("scb", [128, 16], BF16)
        self.b_scb = B()
        self.pad = fw.sbuf("pad", [128, T + 8], F32)
        self.b_pad = B()
        self.tail = fw.sbuf("tail", [128, 16, 3], F32)
        self.b_tail = B()
        self.tail2 = fw.sbuf("tail2", [128, 16, 2], F32)
        self.b_tail2 = B()
        self.hstate = fw.sbuf("hstate", [128, 16], F32)
        self.b_hstate = B()
        self.small = fw.sbuf("small", [128, 160], F32)
        self.b_small = B()
        self.flw = fw.sbuf("flw", [128, 16, 16], BF16)
        self.b_flw = B()
        self.CsT = fw.sbuf("CsT", [128, 8, 16], F32)
        self.b_CsT = B()
        self.DkT = fw.sbuf("DkT", [128, 8, 16], F32)
        self.b_DkT = B()
        self.dec = fw.sbuf("dec", [128, 8], F32)
        self.b_dec = B()
        self.S = fw.sbuf("S", [128, 128], F32)
        self.b_S = B()
        self.Sbf = fw.sbuf("Sbf", [128, 128], BF16)
        self.b_Sbf = B()
        self.scbf = [fw.sbuf(f"scbf{i}", [128, 128], BF16) for i in range(2)]
        self.b_scbf = [B(), B()]
        self.ps = [fw.psum(f"ps{i}", [128, TB], F32) for i in range(7)]
        self.psb = [B() for _ in range(7)]
        self.pst = fw.psum("pst", [128, T], BF16)
        self.pstb = B()

    def TF(self, i):
        return self.TFt[i], self.TFb[i]

    def TH(self, i):
        return self.THt[i], self.THb[i]

    def P_(self, i):
        return self.ps[i], self.psb[i]

    def pcol(self, name, w=16, off=0):
        o = PC[name] + off
        return self.prm_sb[:, o:o + w]

    def ws_reset(self):
        self.ws_next = 0
        self.ws_issued = 0

    def ws_issue(self, i):
        tag, src, view = self.ws_units[i]
        slot = i % NS
        if view == "col":
            kt = src.shape[0] // 128
            dst = self.ring[slot][:, 0:kt * 128].rearrange("p (kt n) -> p kt n", n=128)
            s = src.rearrange("(kt p) n -> p kt n", p=128)
        elif view == "row":
            dst = self.ring[slot][:, :]
            s = src
        else:
            dst = self.ring[slot][:, 0:128]
            s = src
        self.fw.dma(self.fw.pool, dst, s, writes=[self.ringb[slot]])

    def ws_get(self, tag, src, view):
        i = self.ws_next
        self.ws_next += 1
        if self.fw.planning:
            self.ws_units.append((tag, src, view))
        else:
            assert self.ws_units[i][0] == tag, (self.ws_units[i][0], tag)
            while self.ws_issued <= i:
                self.ws_issue(self.ws_issued)
                self.ws_issued += 1
        return self.ring[i % NS], self.ringb[i % NS]

    def ws_pump(self):
        if self.fw.planning:
            return
        lim = min(len(self.ws_units), self.ws_next + NS)
        while self.ws_issued < lim:
            self.ws_issue(self.ws_issued)
            self.ws_issued += 1

    def mm16(self, p, pb, u, ub, tb, extra_reads=()):
        h = self.h
        fns = [(I("matmul", p[:, :], lhsT=u[:, kt * 128:(kt + 1) * 128], rhs=h[:, kt, tbs(tb)],
                                          start=(kt == 0), stop=(kt == 15))) for kt in range(16)]
        self.fw.multi(self.fw.pe, fns, reads=[ub, self.hb[tb]] + list(extra_reads), writes=[pb])

    def stop_here(self, l, s):
        return self.stop is not None and (l, s) >= tuple(self.stop)

    def setup(self):
        fw = self.fw
        A, Dv, Pl = fw.act, fw.dve, fw.pool
        bc = self.b_const
        fw.dma(fw.sp, self.prm_sb[:, :], self.prm, writes=[self.b_prm])
        fw.op(Dv, I("memset", self.ones_f[:, :], 1.0), writes=[bc])
        fw.op(Dv, I("memset", self.ones_b[:, :], 1.0), writes=[bc])
        fw.op(Dv, I("memset", self.ones16[:, :], 1.0), writes=[bc])
        fw.op(Dv, I("memset", self.maskC[:, :], 1.0), writes=[bc])
        for j in range(4):
            fw.op(Dv, I("memset", self.maskC[:, j * 128:j * 128 + 1], 0.0), reads=[bc], writes=[bc])
        fw.op(Pl, I("memset", self.ident_f[:, :], 0.0), writes=[bc])
        fw.op(Pl, I("affine_select", out=self.ident_f[:, :], in_=self.ident_f[:, :], pattern=[[-1, 128]],
                                            compare_op=ALU.not_equal, fill=1.0, base=0, channel_multiplier=1),
              reads=[bc], writes=[bc])
        self.mtri_f = self.TFt[0][:, 0:128]
        fw.op(Pl, I("memset", self.mtri_f, 1.0), reads=[bc, self.TFb[0]], writes=[bc, self.TFb[0]])
        fw.op(Pl, I("affine_select", out=self.mtri_f, in_=self.mtri_f, pattern=[[1, 128]],
                                            compare_op=ALU.is_ge, fill=0.0, base=0, channel_multiplier=-1),
              reads=[bc, self.TFb[0]], writes=[bc, self.TFb[0]])
        fw.op(Dv, I("tensor_copy", out=self.ident_b[:, :], in_=self.ident_f[:, :]), reads=[bc], writes=[bc])
        fw.op(Dv, I("tensor_copy", out=self.mtri[:, :], in_=self.mtri_f), reads=[bc, self.TFb[0]], writes=[bc])
        fw.op(A, I("activation", out=self.scb[:, :], in_=self.pcol("c"), func=AF.Silu),
              reads=[self.b_prm], writes=[self.b_scb])
        sm = self.small
        bs = self.b_small
        fw.op(A, I("activation", out=sm[:, 0:16], in_=self.pcol("rg_lam"), func=AF.Exp, scale=-1.0),
              reads=[self.b_prm], writes=[bs])
        fw.op(A, I("activation", out=sm[:, 0:16], in_=sm[:, 0:16], func=AF.Ln, bias=1.0), reads=[bs], writes=[bs])
        fw.op(Dv, I("tensor_single_scalar", out=sm[:, 16:32], in_=sm[:, 0:16], scalar=-16.0, op=ALU.mult),
              reads=[bs], writes=[bs])
        fw.op(Dv, I("tensor_single_scalar", out=sm[:, 0:16], in_=sm[:, 0:16], scalar=-8.0, op=ALU.mult),
              reads=[bs], writes=[bs])
        fw.op(A, I("activation", out=sm[:, 64:128], in_=self.pcol("hg_lg", 64), func=AF.Exp),
              reads=[self.b_prm], writes=[bs])
        fw.op(Dv, I("tensor_tensor", out=sm[:, 128:144], in0=sm[:, 64:80], in1=sm[:, 80:96], op=ALU.add), reads=[bs], writes=[bs])
        fw.op(Dv, I("tensor_tensor", out=sm[:, 128:144], in0=sm[:, 128:144], in1=sm[:, 96:112], op=ALU.add), reads=[bs], writes=[bs])
        fw.op(Dv, I("tensor_tensor", out=sm[:, 128:144], in0=sm[:, 128:144], in1=sm[:, 112:128], op=ALU.add), reads=[bs], writes=[bs])
        fw.op(Dv, I("reciprocal", out=sm[:, 128:144], in_=sm[:, 128:144]), reads=[bs], writes=[bs])
        fw.op(Dv, I("tensor_tensor", out=sm[:, 32:48], in0=sm[:, 112:128], in1=sm[:, 128:144], op=ALU.mult), reads=[bs], writes=[bs])
        fw.op(Dv, I("tensor_single_scalar", out=sm[:, 48:64], in_=sm[:, 32:48], scalar=-1.0, op=ALU.mult), reads=[bs], writes=[bs])
        fw.op(Dv, I("tensor_single_scalar", out=sm[:, 144:145], in_=self.pcol("fox_bf", 1), scalar=-1.0, op=ALU.mult),
              reads=[self.b_prm, bs], writes=[bs])
        fw.op(Dv, I("tensor_single_scalar", out=sm[:, 145:146], in_=self.pcol("fox_qg", 1), scalar=float(128 ** -0.5), op=ALU.mult),
              reads=[self.b_prm, bs], writes=[bs])
        if "fox_w_in" in self.W:
            fw.dma(fw.pool, self.flw[:, :, :], self.W["fox_w_in"][:, 8192:8208].rearrange("(kt p) n -> p kt n", p=128),
                   writes=[self.b_flw])
        fw.op(Dv, I("memset", self.tail[:, :, :], 0.0), writes=[self.b_tail])
        fw.op(Dv, I("memset", self.tail2[:, :, :], 0.0), writes=[self.b_tail2])
        fw.op(Dv, I("memset", self.hstate[:, :], 0.0), writes=[self.b_hstate])

    def ada_begin(self, l):
        self.ada_l, self.ada_j, self.ada_fl = l, 0, 0

    def ada_some(self, n):
        fw = self.fw
        l = self.ada_l
        if l is None:
            return
        pN, pNb = self.P_(6)
        Wl = self.W[f"ada_w{l}"]
        for _ in range(n):
            j = self.ada_j
            if j >= 144:
                return
            u, ub = self.ws_get(("ada", l, j), Wl[:, j * 128:(j + 1) * 128], "col")
            fns = [I("matmul", pN[:, j:j + 1], lhsT=u[:, kt * 128:(kt + 1) * 128],
                     rhs=self.scb[:, kt:kt + 1], start=(kt == 0), stop=(kt == 15))
                   for kt in range(16)]
            fw.multi(fw.pe, fns, reads=[ub, self.b_scb], writes=[pNb])
            self.ws_pump()
            self.ada_j += 1

    def ada_flush(self):
        fw = self.fw
        l = self.ada_l
        if l is None:
            return
        pN, pNb = self.P_(6)
        a, b = self.ada_fl, self.ada_j
        bm = self.b_mods[l]
        if b > a:
            o = PC[f"ada_b{l}"]
            fw.op(fw.dve, I("tensor_tensor", out=self.mods[:, l, a:b], in0=pN[:, a:b], in1=self.prm_sb[:, o + a:o + b], op=ALU.add),
                  reads=[pNb, self.b_prm, bm], writes=[bm])
            self.ada_fl = b
        if b >= 144:
            bd = self.b_der[l]
            for s in range(3):
                sc_ = self.mods[:, l, (s * 3 + 1) * 16:(s * 3 + 2) * 16]
                gt_ = self.mods[:, l, (s * 3 + 2) * 16:(s * 3 + 3) * 16]
                fw.op(fw.dve, I("scalar_tensor_tensor", out=self.derA[:, l, s, :], in0=sc_, scalar=1.0,
                                in1=self.pcol(f"ng{l}_{s}"), op0=ALU.add, op1=ALU.mult),
                      reads=[bm, self.b_prm, bd], writes=[bd])
                fw.op(fw.dve, I("tensor_single_scalar", out=self.derG[:, l, s, :], in_=gt_,
                                scalar=(1.0 if s == 1 else 0.5), op=ALU.mult),
                      reads=[bm, bd], writes=[bd])
            self.ada_l = None

    def ada_flush_partial(self):
        if self.ada_l is not None:
            self.ada_flush()

    def ada(self, l):
        self.ada_begin(l)
        self.ada_some(144)
        self.ada_flush()

    def norm(self, l, s):
        fw = self.fw
        x, h = self.x, self.h
        bd = self.b_der[l]
        bm = self.b_mods[l]
        pN, pNb = self.P_(6)
        for tb in range(2):
            for d in range(NT):
                sq, sqb = self.TF(d % 2)
                fw.op(fw.act, I("activation", out=sq[:, :], in_=x[:, d, tbs(tb)], func=AF.Square),
                      reads=[self.xb[d][tb]], writes=[sqb])
                fw.op(fw.pe, I("matmul", pN[:, :], lhsT=self.ones_f[:, :], rhs=sq[:, :], start=(d == 0), stop=(d == 15)),
                      reads=[sqb, self.b_const], writes=[pNb])
            rt, rtb = self.TF(2)
            fw.op(fw.act, I("activation", out=rt[:, :], in_=pN[:, :], func=AF.Sqrt, scale=1.0 / D, bias=EPS),
                  reads=[pNb], writes=[rtb])
            rs, rsb = self.TF(3)
            fw.op(fw.dve, I("reciprocal", out=rs[:, :], in_=rt[:, :]), reads=[rtb], writes=[rsb])
            for d in range(NT):
                tn, tnb = self.TF(4 + d % 2)
                fw.op(fw.dve, I("scalar_tensor_tensor", out=tn[:, :], in0=x[:, d, tbs(tb)], scalar=self.derA[:, l, s, d:d + 1],
                                                                          in1=rs[:, :], op0=ALU.mult, op1=ALU.mult),
                      reads=[self.xb[d][tb], rsb, bd], writes=[tnb])
                sh = self.mods[:, l, (s * 3) * 16 + d:(s * 3) * 16 + d + 1]
                fw.op(fw.act, I("activation", out=h[:, d, tbs(tb)], in_=tn[:, :], func=AF.Identity, bias=sh),
                      reads=[tnb, bm], writes=[self.hb[tb]])

    def ffn(self, l, w):
        fw = self.fw
        s = 0 if w == 0 else 2
        self.norm(l, s)
        bd = self.b_der[l]
        Wg, Wu, Wd = self.W[f"ffn_g{l}_{w}"], self.W[f"ffn_u{l}_{w}"], self.W[f"ffn_d{l}_{w}"]
        x, y = self.x, self.y
        rot = 0
        import os
        nch_dbg = int(os.environ.get("KCH", str(NCH)))
        nodown = int(os.environ.get("KNODOWN", "0"))
        for ch in range(nch_dbg):
            for fi in range(FC):
                ft = ch * FC + fi
                sl = (ch % 2) * FC + fi
                ug, ugb = self.ws_get(("g", l, w, ft), Wg[:, ft * 128:(ft + 1) * 128], "col")
                uu, uub = self.ws_get(("u", l, w, ft), Wu[:, ft * 128:(ft + 1) * 128], "col")
                for tb in range(2):
                    pg, pgb = self.P_(tb)
                    pu, pub = self.P_(2 + tb)
                    self.mm16(pg, pgb, ug, ugb, tb)
                    self.mm16(pu, pub, uu, uub, tb)
                    sg, sgb = self.TF(6 + tb)
                    fw.op(fw.act, I("activation", out=sg[:, :], in_=pg[:, :], func=AF.Silu),
                          reads=[pgb], writes=[sgb])
                    fw.op(fw.dve, I("tensor_tensor", out=y[:, sl, tbs(tb)], in0=sg[:, :], in1=pu[:, :], op=ALU.mult),
                          reads=[sgb, pub], writes=[self.yb[sl][tb]])
                self.ws_pump()
                self.ada_some(2)
            if nodown:
                continue
            uds = [self.ws_get(("d", l, w, ch * FC + fi), Wd[(ch * FC + fi) * 128:(ch * FC + fi + 1) * 128, :], "row")
                   for fi in range(FC)]
            for d in range(NT):
                for tb in range(2):
                    pd, pdb = self.P_(4 + rot % 2)
                    rot += 1
                    fns = [(I("matmul", pd[:, :], lhsT=uds[fi][0][:, d * 128:(d + 1) * 128],
                                                                        rhs=y[:, (ch % 2) * FC + fi, tbs(tb)],
                                                                        start=(fi == 0), stop=(fi == FC - 1)))
                           for fi in range(FC)]
                    fw.multi(fw.pe, fns, reads=[u[1] for u in uds] + [self.yb[(ch % 2) * FC + fi][tb] for fi in range(FC)],
                             writes=[pdb])
                    fw.op(fw.dve, I("scalar_tensor_tensor", out=x[:, d, tbs(tb)], in0=pd[:, :],
                                                                                     scalar=self.derG[:, l, s, d:d + 1], in1=x[:, d, tbs(tb)],
                                                                                     op0=ALU.mult, op1=ALU.add),
                          reads=[pdb, self.xb[d][tb], bd], writes=[self.xb[d][tb]])
            self.ws_pump()
        self.ada_flush_partial()

    def outproj(self, l, Wout, name):
        fw = self.fw
        bd = self.b_der[l]
        x, y = self.x, self.y
        rot = 0
        for d in range(NT):
            u, ub = self.ws_get((name, "out", d), Wout[:, d * 128:(d + 1) * 128], "col")
            for tb in range(2):
                pd, pdb = self.P_(4 + rot % 2)
                rot += 1
                fns = [(I("matmul", pd[:, :], lhsT=u[:, c * 128:(c + 1) * 128], rhs=y[:, c, tbs(tb)],
                                                                  start=(c == 0), stop=(c == 15))) for c in range(16)]
                fw.multi(fw.pe, fns, reads=[ub] + [self.yb[c][tb] for c in range(16)], writes=[pdb])
                fw.op(fw.dve, I("scalar_tensor_tensor", out=x[:, d, tbs(tb)], in0=pd[:, :],
                                                                                 scalar=self.derG[:, l, 1, d:d + 1], in1=x[:, d, tbs(tb)],
                                                                                 op0=ALU.mult, op1=ALU.add),
                      reads=[pdb, self.xb[d][tb], bd], writes=[self.xb[d][tb]])
            self.ws_pump()

    def inproj(self, Win, col0, tag, banks):
        u, ub = self.ws_get(tag, Win[:, col0:col0 + 128], "col")
        out = []
        for tb in range(2):
            p, pb = self.P_(banks[tb])
            self.mm16(p, pb, u, ub, tb)
            out.append((p, pb))
        self.ws_pump()
        return out

    def mixer_rg(self, l, hf):
        fw = self.fw
        A, Dv = fw.act, fw.dve
        self.norm(l, 1)
        Win = self.W["rg_w_in"]
        pad, bp = self.pad, self.b_pad
        sm, bs = self.small, self.b_small
        y = self.y
        for c in range(16):
            pg = self.inproj(Win, c * 128, ("rg", "g", c), (0, 1))
            px = self.inproj(Win, 2048 + c * 128, ("rg", "x", c), (2, 3))
            wr, wrb = self.ws_get(("rg", "wr", c), self.W["rg_w_r"][c], "blk")
            wi, wib = self.ws_get(("rg", "wi", c), self.W["rg_w_i"][c], "blk")
            fw.op(Dv, I("tensor_copy", out=pad[:, 0:3], in_=self.tail[:, c, :]), reads=[self.b_tail], writes=[bp])
            for tb in range(2):
                fw.op(A, I("activation", out=pad[:, 3 + tb * TB:3 + (tb + 1) * TB], in_=px[tb][0][:, :], func=AF.Copy),
                      reads=[px[tb][1]], writes=[bp])
            fw.op(Dv, I("tensor_copy", out=self.tail[:, c, :], in_=pad[:, T:T + 3]), reads=[bp], writes=[self.b_tail])
            for tb in range(2):
                o = tb * TB
                gl, glb = self.TF(0)
                fw.op(A, I("activation", out=gl[:, :], in_=pg[tb][0][:, :], func=AF.Gelu), reads=[pg[tb][1]], writes=[glb])
                xc, xcb = self.TF(1)
                cw = lambda k, c=c: self.pcol("rg_cw", 1, k * 16 + c)
                fw.op(A, I("activation", out=xc[:, :], in_=pad[:, 3 + o:3 + o + TB], func=AF.Identity,
                                                                scale=cw(3), bias=self.pcol("rg_cb", 1, c)),
                      reads=[bp, self.b_prm], writes=[xcb])
                for k in range(3):
                    fw.op(Dv, I("scalar_tensor_tensor", out=xc[:, :], in0=pad[:, k + o:k + o + TB], scalar=cw(k),
                                                                               in1=xc[:, :], op0=ALU.mult, op1=ALU.add),
                          reads=[bp, xcb, self.b_prm], writes=[xcb])
                xh, xhb = self.TH(0)
                fw.op(A, I("activation", out=xh[:, 0:TB], in_=xc[:, :], func=AF.Copy), reads=[xcb], writes=[xhb])
                pr, prb = self.P_(4)
                pi, pib = self.P_(5)
                fw.op(fw.pe, I("matmul", pr[:, :], lhsT=wr[:, 0:128], rhs=xh[:, 0:TB], start=True, stop=True),
                      reads=[wrb, xhb], writes=[prb])
                fw.op(fw.pe, I("matmul", pi[:, :], lhsT=wi[:, 0:128], rhs=xh[:, 0:TB], start=True, stop=True),
                      reads=[wib, xhb], writes=[pib])
                rm, rmb = self.TF(2)
                ig, igb = self.TF(3)
                fw.op(A, I("activation", out=rm[:, :], in_=pr[:, :], func=AF.Sigmoid, bias=self.pcol("rg_br", 1, c)),
                      reads=[prb, self.b_prm], writes=[rmb])
                fw.op(A, I("activation", out=ig[:, :], in_=pi[:, :], func=AF.Sigmoid, bias=self.pcol("rg_bi", 1, c)),
                      reads=[pib, self.b_prm], writes=[igb])
                a_, ab = self.TF(4)
                fw.op(A, I("activation", out=a_[:, :], in_=rm[:, :], func=AF.Exp, scale=sm[:, c:c + 1]),
                      reads=[rmb, bs], writes=[ab])
                fw.op(A, I("activation", out=rm[:, :], in_=rm[:, :], func=AF.Exp, scale=sm[:, 16 + c:17 + c]),
                      reads=[rmb, bs], writes=[rmb])
                fw.op(A, I("activation", out=rm[:, :], in_=rm[:, :], func=AF.Sqrt, scale=-1.0, bias=1.0),
                      reads=[rmb], writes=[rmb])
                fw.op(Dv, I("tensor_tensor", out=ig[:, :], in0=ig[:, :], in1=xc[:, :], op=ALU.mult),
                      reads=[igb, xcb], writes=[igb])
                fw.op(Dv, I("tensor_tensor", out=ig[:, :], in0=ig[:, :], in1=rm[:, :], op=ALU.mult),
                      reads=[igb, rmb], writes=[igb])
                hs, hsb = self.TF(5)
                fw.op(Dv, I("tensor_tensor_scan", out=hs[:, :], data0=a_[:, :], data1=ig[:, :],
                                                                                 initial=self.hstate[:, c:c + 1], op0=ALU.mult, op1=ALU.add),
                      reads=[ab, igb, self.b_hstate], writes=[hsb])
                fw.op(Dv, I("tensor_copy", out=self.hstate[:, c:c + 1], in_=hs[:, TB - 1:TB]),
                      reads=[hsb], writes=[self.b_hstate])
                fw.op(Dv, I("tensor_tensor", out=y[:, c, tbs(tb)], in0=hs[:, :], in1=gl[:, :], op=ALU.mult),
                      reads=[hsb, glb], writes=[self.yb[c][tb]])
        self.outproj(l, self.W["rg_w_out"], "rg")

    def mixer_sc(self, l, hf):
        fw = self.fw
        A, Dv = fw.act, fw.dve
        self.norm(l, 1)
        Win = self.W["sc_w_in"]
        pad, bp = self.pad, self.b_pad
        y = self.y
        for c in range(16):
            pbg = self.inproj(Win, c * 128, ("sc", "b", c), (0, 1))
            pcg = self.inproj(Win, 2048 + c * 128, ("sc", "c", c), (2, 3))
            bgs = []
            for tb in range(2):
                t, tb_ = self.TF(0 + tb)
                fw.op(A, I("activation", out=t[:, :], in_=pbg[tb][0][:, :], func=AF.Copy), reads=[pbg[tb][1]], writes=[tb_])
                bgs.append((t, tb_))
            cgs = []
            for tb in range(2):
                t, tb_ = self.TF(2 + tb)
                fw.op(A, I("activation", out=t[:, :], in_=pcg[tb][0][:, :], func=AF.Copy), reads=[pcg[tb][1]], writes=[tb_])
                cgs.append((t, tb_))
            pxv = self.inproj(Win, 4096 + c * 128, ("sc", "x", c), (0, 1))
            fw.op(Dv, I("tensor_copy", out=pad[:, 0:2], in_=self.tail2[:, c, 0:2]), reads=[self.b_tail2], writes=[bp])
            for tb in range(2):
                fw.op(Dv, I("tensor_tensor", out=pad[:, 2 + tb * TB:2 + (tb + 1) * TB], in0=cgs[tb][0][:, :], in1=pxv[tb][0][:, :], op=ALU.mult),
                      reads=[cgs[tb][1], pxv[tb][1]], writes=[bp])
            fw.op(Dv, I("tensor_copy", out=self.tail2[:, c, 0:2], in_=pad[:, T:T + 2]), reads=[bp], writes=[self.b_tail2])
            for tb in range(2):
                o = tb * TB
                uc, ucb = self.TF(4 + tb)
                cw = lambda k, c=c: self.pcol("sc_cw", 1, k * 16 + c)
                fw.op(A, I("activation", out=uc[:, :], in_=pad[:, 2 + o:2 + o + TB], func=AF.Identity, scale=cw(2)),
                      reads=[bp, self.b_prm], writes=[ucb])
                for k in range(2):
                    fw.op(Dv, I("scalar_tensor_tensor", out=uc[:, :], in0=pad[:, k + o:k + o + TB], scalar=cw(k),
                                                                               in1=uc[:, :], op0=ALU.mult, op1=ALU.add),
                          reads=[bp, ucb, self.b_prm], writes=[ucb])
                fw.op(Dv, I("tensor_tensor", out=y[:, c, tbs(tb)], in0=bgs[tb][0][:, :], in1=uc[:, :], op=ALU.mult),
                      reads=[ucb, bgs[tb][1]], writes=[self.yb[c][tb]])
        self.outproj(l, self.W["sc_w_out"], "sc")

    def rms_part(self, src, srcb, tf_sq, tf_rs, bank):
        fw = self.fw
        sq, sqb = self.TF(tf_sq)
        fw.op(fw.act, I("activation", out=sq[:, :], in_=src[:, :], func=AF.Square), reads=[srcb], writes=[sqb])
        pn, pnb = self.P_(bank)
        fw.op(fw.pe, I("matmul", pn[:, :], lhsT=self.ones_f[:, :], rhs=sq[:, :], start=True, stop=True),
              reads=[sqb, self.b_const], writes=[pnb])
        fw.op(fw.act, I("activation", out=sq[:, :], in_=pn[:, :], func=AF.Sqrt, scale=1.0 / 128, bias=EPS),
              reads=[pnb], writes=[sqb])
        rs, rsb = self.TF(tf_rs)
        fw.op(fw.dve, I("reciprocal", out=rs[:, :], in_=sq[:, :]), reads=[sqb], writes=[rsb])
        return rs, rsb

    def mixer_fox(self, l, hf):
        fw = self.fw
        A, Dv, PE = fw.act, fw.dve, fw.pe
        self.norm(l, 1)
        Win = self.W["fox_w_in"]
        sm, bs = self.small, self.b_small
        y = self.y
        Cs = [self.TF(8), self.TF(9)]
        for tb in range(2):
            pF, pFb = self.P_(4)
            fns = [(I("matmul", pF[0:16, :], lhsT=self.flw[:, kt, :], rhs=self.h[:, kt, tbs(tb)], start=(kt == 0), stop=(kt == 15)))
                   for kt in range(16)]
            fw.multi(PE, fns, reads=[self.b_flw, self.hb[tb]], writes=[pFb])
            t0, t0b = self.TF(0)
            fw.op(A, I("activation", out=t0[0:16, :], in_=pF[0:16, :], func=AF.Exp, scale=-1.0, bias=sm[0:16, 144:145]),
                  reads=[pFb, bs], writes=[t0b])
            fw.op(A, I("activation", out=t0[0:16, :], in_=t0[0:16, :], func=AF.Ln, bias=1.0), reads=[t0b], writes=[t0b])
            cs, csb = Cs[tb]
            init = 0.0 if tb == 0 else Cs[0][0][0:16, TB - 1:TB]
            fw.op(Dv, I("tensor_tensor_scan", out=cs[0:16, :], data0=self.ones16[:, :], data1=t0[0:16, :],
                                                                      initial=init, op0=ALU.mult, op1=ALU.add),
                  reads=[t0b, self.b_const] + ([Cs[0][1]] if tb else []), writes=[csb])
        hi, hib = self.TH(0)
        mid, midb = self.TH(1)
        lo, lob = self.TH(2)
        for tb in range(2):
            cs, csb = Cs[tb]
            r1, r1b = self.TF(0)
            fw.op(Dv, I("tensor_single_scalar", out=hi[0:16, tbs(tb)], in_=cs[0:16, :], scalar=-1.0, op=ALU.mult), reads=[csb], writes=[hib])
            fw.op(Dv, I("scalar_tensor_tensor", out=r1[0:16, :], in0=cs[0:16, :], scalar=-1.0, in1=hi[0:16, tbs(tb)], op0=ALU.mult, op1=ALU.subtract),
                  reads=[csb, hib], writes=[r1b])
            fw.op(Dv, I("tensor_copy", out=mid[0:16, tbs(tb)], in_=r1[0:16, :]), reads=[r1b], writes=[midb])
            fw.op(Dv, I("tensor_tensor", out=lo[0:16, tbs(tb)], in0=r1[0:16, :], in1=mid[0:16, tbs(tb)], op=ALU.subtract), reads=[r1b, midb], writes=[lob])
        for i, (t, tb_) in enumerate(((hi, hib), (mid, midb), (lo, lob))):
            fw.dma(fw.sp, self.F3d[i], t[0:16, :], reads=[tb_], writes=[self.b_F3d])
        pT, pTb = self.P_(5)
        if hf == 0:
            for kt in range(8):
                cs, csb = Cs[kt // 4]
                fw.op(PE, I("transpose", out=pT[:, kt * 16:(kt + 1) * 16], in_=cs[0:16, (kt % 4) * 128:(kt % 4 + 1) * 128],
                                                             identity=self.ident_f[0:16, 0:16]), reads=[csb, self.b_const], writes=[pTb])
        else:
            for kt in range(8):
                cs, csb = Cs[kt // 4]
                fw.op(PE, I("transpose", out=pT[:, kt * 16:(kt + 1) * 16], in_=cs[0:16, (kt % 4) * 128:(kt % 4 + 1) * 128],
                                                             identity=self.ident_f[0:16, 0:16]), reads=[csb, self.b_const], writes=[pTb])
        fw.op(Dv, I("tensor_copy", out=self.CsT[:, :, :], in_=pT[:, 0:128].rearrange("p (k h) -> p k h", h=16)), reads=[pTb], writes=[self.b_CsT])
        if hf == 0 and self.n_halves > 1:
            for tb in range(2):
                cs, csb = Cs[tb]
                d0, d0b = self.TF(0 + tb)
                fw.op(Dv, I("tensor_single_scalar", out=d0[0:16, :], in_=cs[0:16, :], scalar=Cs[1][0][0:16, TB - 1:TB], op=ALU.subtract),
                      reads=[csb, Cs[1][1]], writes=[d0b])
            pT2, pT2b = self.P_(4)
            for kt in range(8):
                d0, d0b = self.TF(kt // 4)
                fw.op(PE, I("transpose", out=pT2[:, kt * 16:(kt + 1) * 16], in_=d0[0:16, (kt % 4) * 128:(kt % 4 + 1) * 128],
                                                             identity=self.ident_f[0:16, 0:16]), reads=[d0b, self.b_const], writes=[pT2b])
            fw.op(Dv, I("tensor_copy", out=self.DkT[:, :, :], in_=pT2[:, 0:128].rearrange("p (k h) -> p k h", h=16)), reads=[pT2b], writes=[self.b_DkT])
        f3, f3b = self.TH(6)
        fw.op(Dv, I("memset", f3[:, :], 0.0), writes=[f3b])
        for hd in range(16):
            pq = self.inproj(Win, hd * 128, ("fox", "q", hd), (0, 1))
            pk = self.inproj(Win, 2048 + hd * 128, ("fox", "k", hd), (2, 3))
            qn, qnb = self.TH(0)
            kn, knb = self.TH(1)
            for (pp, dst, dstb, gcol) in ((pq, qn, qnb, sm[:, 145:146]), (pk, kn, knb, self.pcol("fox_kg", 1))):
                for tb in range(2):
                    qs, qsb = self.TF(2)
                    fw.op(A, I("activation", out=qs[:, :], in_=pp[tb][0][:, :], func=AF.Copy), reads=[pp[tb][1]], writes=[qsb])
                    rs, rsb = self.rms_part(qs, qsb, 3, 4, 6)
                    fw.op(Dv, I("scalar_tensor_tensor", out=dst[:, tbs(tb)], in0=qs[:, :], scalar=gcol, in1=rs[:, :],
                                                                                                        op0=ALU.mult, op1=ALU.mult),
                          reads=[qsb, rsb, bs, self.b_prm], writes=[dstb])
            pv = self.inproj(Win, 4096 + hd * 128, ("fox", "v", hd), (0, 1))
            pg = self.inproj(Win, 6144 + hd * 128, ("fox", "g", hd), (2, 3))
            vfm, vfmb = self.TH(2)
            for tb in range(2):
                fw.op(A, I("activation", out=vfm[:, tbs(tb)], in_=pv[tb][0][:, :], func=AF.Copy), reads=[pv[tb][1]], writes=[vfmb])
            for n in range(8):
                fw.op(PE, I("transpose", out=self.pst[:, n * 128:(n + 1) * 128], in_=vfm[:, n * 128:(n + 1) * 128], identity=self.ident_b[:, :]),
                      reads=[vfmb, self.b_const], writes=[self.pstb])
            vt, vtb = self.TH(3)
            fw.op(Dv, I("tensor_copy", out=vt[:, :], in_=self.pst[:, :]), reads=[self.pstb], writes=[vtb])
            sgm = []
            for tb in range(2):
                t, tb_ = self.TF(6 + tb)
                fw.op(A, I("activation", out=t[:, :], in_=pg[tb][0][:, :], func=AF.Sigmoid), reads=[pg[tb][1]], writes=[tb_])
                sgm.append((t, tb_))
            fw.dma(fw.sp, f3[0:3, :], self.F3d[:, hd, :], reads=[self.b_F3d], writes=[f3b])
            if hf == 0 and self.n_halves > 1:
                fw.dma(fw.sp, self.Kd[hd], kn[:, :], reads=[knb], writes=[self.b_Kd[hd]])
                fw.dma(fw.sp, self.Vd[hd], vt[:, :], reads=[vtb], writes=[self.b_Vd[hd]])
            if hf == 1:
                kp, kpb = self.TH(4)
                vp, vpb = self.TH(5)
                fw.dma(fw.sp, kp[:, :], self.Kd[hd], reads=[self.b_Kd[hd]], writes=[kpb])
                fw.dma(fw.sp, vp[:, :], self.Vd[hd], reads=[self.b_Vd[hd]], writes=[vpb])
            for qb in range(2):
                tiles = []
                if hf == 1:
                    tiles += [("p", kt, 0) for kt in range(8)]
                for kt in range(4 * qb + 4):
                    j = kt - 4 * qb
                    tiles.append(("o", kt, 0 if j < 0 else 128 * j))
                po, pob = self.P_(6)
                pl, plb = self.P_(3)
                nt_ = len(tiles)
                for i, (kind, kt, q0) in enumerate(tiles):
                    psc, pscb = self.P_(4 + i % 2)
                    Ksrc, Kb = (kp, kpb) if kind == "p" else (kn, knb)
                    Vsrc, Vb = (vp, vpb) if kind == "p" else (vt, vtb)
                    qsl = slice(qb * TB + q0, (qb + 1) * TB)
                    fns = [I("matmul", psc[:, q0:TB], lhsT=Ksrc[:, kt * 128:(kt + 1) * 128], rhs=qn[:, qsl], start=True, stop=False),
                           I("matmul", psc[:, q0:TB], lhsT=self.ones_b[:, :], rhs=f3[:, qsl], start=False, stop=True)]
                    fw.multi(PE, fns, reads=[Kb, qnb, f3b, self.b_const], writes=[pscb])
                    if kind == "p":
                        bias = self.DkT[:, kt, hd:hd + 1]
                        bb = self.b_DkT
                    else:
                        bias = self.CsT[:, kt, hd:hd + 1]
                        bb = self.b_CsT
                    ptt, ptb = self.PT[i % 2], self.PTb[i % 2]
                    fw.op(A, I("activation", out=ptt[:, q0:TB], in_=psc[:, q0:TB], func=AF.Exp, bias=bias),
                          reads=[pscb, bb], writes=[ptb])
                    if kind == "o" and kt >= 4 * qb:
                        fw.op(Dv, I("tensor_tensor", out=ptt[:, q0:q0 + 128], in0=ptt[:, q0:q0 + 128], in1=self.mtri[:, :], op=ALU.mult),
                              reads=[ptb, self.b_const], writes=[ptb])
                    fns = [I("matmul", po[:, q0:TB], lhsT=Vsrc[:, kt * 128:(kt + 1) * 128], rhs=ptt[:, q0:TB], start=(i == 0), stop=(i == nt_ - 1)),
                           I("matmul", pl[:, q0:TB], lhsT=self.ones_b[:, :], rhs=ptt[:, q0:TB], start=(i == 0), stop=(i == nt_ - 1))]
                    fw.multi(PE, fns, reads=[Vb, ptb, self.b_const], writes=[pob, plb])
                rl, rlb = self.TF(2)
                fw.op(Dv, I("reciprocal", out=rl[:, :], in_=pl[:, :]), reads=[plb], writes=[rlb])
                on, onb = self.TF(3)
                fw.op(Dv, I("tensor_tensor", out=on[:, :], in0=po[:, :], in1=rl[:, :], op=ALU.mult), reads=[pob, rlb], writes=[onb])
                fw.op(Dv, I("tensor_tensor", out=y[:, hd, tbs(qb)], in0=on[:, :], in1=sgm[qb][0][:, :], op=ALU.mult),
                      reads=[onb, sgm[qb][1]], writes=[self.yb[hd][qb]])
        self.outproj(l, self.W["fox_w_out"], "fox")

    def mixer_hg(self, l, hf):
        fw = self.fw
        A, Dv, PE = fw.act, fw.dve, fw.pe
        self.norm(l, 1)
        Win = self.W["hg_w_in"]
        sm, bs = self.small, self.b_small
        y = self.y
        for hd in range(16):
            pq = self.inproj(Win, hd * 128, ("hg", "q", hd), (0, 1))
            pf = self.inproj(Win, 2048 + hd * 128, ("hg", "f", hd), (2, 3))
            qin, qinb = self.TH(0)
            kin, kinb = self.TH(1)
            kdec, kdecb = self.TH(2)
            for tb in range(2):
                qs, qsb = self.TF(0)
                fw.op(A, I("activation", out=qs[:, :], in_=pq[tb][0][:, :], func=AF.Copy), reads=[pq[tb][1]], writes=[qsb])
                sn, snb = self.TF(1)
                fw.op(A, I("activation", out=sn[:, :], in_=pf[tb][0][:, :], func=AF.Sigmoid, scale=-1.0), reads=[pf[tb][1]], writes=[snb])
                kk, kkb = self.TF(2)
                fw.op(Dv, I("tensor_single_scalar", out=kk[:, :], in_=sn[:, :], scalar=sm[:, 32 + hd:33 + hd], op=ALU.mult), reads=[snb, bs], writes=[kkb])
                fw.op(Dv, I("tensor_scalar", out=sn[:, :], in0=sn[:, :], scalar1=sm[:, 48 + hd:49 + hd], scalar2=1.0, op0=ALU.mult, op1=ALU.add),
                      reads=[snb, bs], writes=[snb])
                fw.op(A, I("activation", out=sn[:, :], in_=sn[:, :], func=AF.Ln), reads=[snb], writes=[snb])
                bc_, bcb = self.TF(3)
                fw.op(Dv, I("tensor_tensor_scan", out=bc_[:, :], data0=self.maskC[:, :], data1=sn[:, :], initial=0.0, op0=ALU.mult, op1=ALU.add),
                      reads=[snb, self.b_const], writes=[bcb])
                eb, ebb = self.TF(4)
                fw.op(A, I("activation", out=eb[:, :], in_=bc_[:, :], func=AF.Exp), reads=[bcb], writes=[ebb])
                fw.op(Dv, I("tensor_tensor", out=qin[:, tbs(tb)], in0=qs[:, :], in1=eb[:, :], op=ALU.mult), reads=[qsb, ebb], writes=[qinb])
                enb, enbb = self.TF(5)
                fw.op(A, I("activation", out=enb[:, :], in_=bc_[:, :], func=AF.Exp, scale=-1.0), reads=[bcb], writes=[enbb])
                fw.op(Dv, I("tensor_tensor", out=kin[:, tbs(tb)], in0=kk[:, :], in1=enb[:, :], op=ALU.mult), reads=[kkb, enbb], writes=[kinb])
                for n in range(4):
                    fw.op(A, I("activation", out=self.dec[:, tb * 4 + n:tb * 4 + n + 1], in_=bc_[:, n * 128 + 127:n * 128 + 128], func=AF.Exp),
                          reads=[bcb], writes=[self.b_dec])
                for n in range(4):
                    c0 = tb * TB + n * 128
                    fw.op(Dv, I("tensor_single_scalar", out=kdec[:, c0:c0 + 128], in_=kin[:, c0:c0 + 128],
                                                                                scalar=self.dec[:, tb * 4 + n:tb * 4 + n + 1], op=ALU.mult),
                          reads=[kinb, self.b_dec], writes=[kdecb])
            pv = self.inproj(Win, 4096 + hd * 128, ("hg", "v", hd), (0, 1))
            pg = self.inproj(Win, 6144 + hd * 128, ("hg", "g", hd), (2, 3))
            vfm, vfmb = self.TH(3)
            for tb in range(2):
                fw.op(A, I("activation", out=vfm[:, tbs(tb)], in_=pv[tb][0][:, :], func=AF.Copy), reads=[pv[tb][1]], writes=[vfmb])
            sg = []
            for tb in range(2):
                t, tb_ = self.TF(6 + tb)
                fw.op(A, I("activation", out=t[:, :], in_=pg[tb][0][:, :], func=AF.Silu), reads=[pg[tb][1]], writes=[tb_])
                sg.append((t, tb_))
            vt, vtb = self.TH(4)
            kdt, kdtb = self.TH(5)
            for (src, srcb, dst, dstb, eng) in ((vfm, vfmb, vt, vtb, Dv), (kdec, kdecb, kdt, kdtb, A)):
                for n in range(8):
                    fw.op(PE, I("transpose", out=self.pst[:, n * 128:(n + 1) * 128], in_=src[:, n * 128:(n + 1) * 128], identity=self.ident_b[:, :]),
                          reads=[srcb, self.b_const], writes=[self.pstb])
                if eng is Dv:
                    fw.op(Dv, I("tensor_copy", out=dst[:, :], in_=self.pst[:, :]), reads=[self.pstb], writes=[dstb])
                else:
                    fw.op(A, I("activation", out=dst[:, :], in_=self.pst[:, :], func=AF.Copy), reads=[self.pstb], writes=[dstb])
            S, Sb, Sbf, Sbfb = self.S, self.b_S, self.Sbf, self.b_Sbf
            if hf == 0:
                fw.op(Dv, I("memset", S[:, :], 0.0), reads=[Sb], writes=[Sb])
            else:
                fw.dma(fw.sp, S[:, :], self.Sd[hd], reads=[self.b_Sd[hd]], writes=[Sb])
            fw.op(A, I("activation", out=Sbf[:, :], in_=S[:, :], func=AF.Copy), reads=[Sb], writes=[Sbfb])
            po, pob = self.P_(6)
            for n in range(8):
                tb = n // 4
                cs = slice(n * 128, (n + 1) * 128)
                psc, pscb = self.P_(4)
                fw.op(PE, I("matmul", psc[:, 0:128], lhsT=kin[:, cs], rhs=qin[:, cs], start=True, stop=True),
                      reads=[kinb, qinb], writes=[pscb])
                scb_, scbb = self.scbf[n % 2], self.b_scbf[n % 2]
                fw.op(Dv, I("tensor_tensor", out=scb_[:, :], in0=psc[:, 0:128], in1=self.mtri[:, :], op=ALU.mult),
                      reads=[pscb, self.b_const], writes=[scbb])
                oc = slice((n % 4) * 128, (n % 4 + 1) * 128)
                fns = [I("matmul", po[:, oc], lhsT=vt[:, cs], rhs=scb_[:, :], start=True, stop=False),
                       I("matmul", po[:, oc], lhsT=Sbf[:, :], rhs=qin[:, cs], start=False, stop=True)]
                fw.multi(PE, fns, reads=[vtb, scbb, Sbfb, qinb], writes=[pob])
                pU, pUb = self.P_(5)
                fw.op(PE, I("matmul", pU[:, 0:128], lhsT=kdt[:, cs], rhs=vt[:, cs], start=True, stop=True),
                      reads=[kdtb, vtb], writes=[pUb])
                fw.op(Dv, I("scalar_tensor_tensor", out=S[:, :], in0=S[:, :], scalar=self.dec[:, n:n + 1], in1=pU[:, 0:128], op0=ALU.mult, op1=ALU.add),
                      reads=[Sb, self.b_dec, pUb], writes=[Sb])
                fw.op(A, I("activation", out=Sbf[:, :], in_=S[:, :], func=AF.Copy), reads=[Sb], writes=[Sbfb])
                if n % 4 == 3:
                    osq, osqb = self.TF(8)
                    fw.op(A, I("activation", out=osq[:, :], in_=po[:, :], func=AF.Square), reads=[pob], writes=[osqb])
                    pn, pnb = self.P_(2)
                    fw.op(PE, I("matmul", pn[:, :], lhsT=self.ones_f[:, :], rhs=osq[:, :], start=True, stop=True), reads=[osqb, self.b_const], writes=[pnb])
                    fw.op(A, I("activation", out=osq[:, :], in_=pn[:, :], func=AF.Sqrt, scale=1.0 / 128, bias=EPS), reads=[pnb], writes=[osqb])
                    rs, rsb = self.TF(9)
                    fw.op(Dv, I("reciprocal", out=rs[:, :], in_=osq[:, :]), reads=[osqb], writes=[rsb])
                    fw.op(Dv, I("scalar_tensor_tensor", out=rs[:, :], in0=po[:, :], scalar=self.pcol("hg_ng", 1), in1=rs[:, :], op0=ALU.mult, op1=ALU.mult),
                          reads=[pob, rsb, self.b_prm], writes=[rsb])
                    fw.op(Dv, I("tensor_tensor", out=y[:, hd, tbs(tb)], in0=rs[:, :], in1=sg[tb][0][:, :], op=ALU.mult),
                          reads=[rsb, sg[tb][1]], writes=[self.yb[hd][tb]])
            if hf == 0 and self.n_halves > 1:
                fw.dma(fw.sp, self.Sd[hd], S[:, :], reads=[Sb], writes=[self.b_Sd[hd]])
        self.outproj(l, self.W["hg_w_out"], "hg")

    def emit_program(self):
        fw = self.fw
        self.ada_l = None
        self.ws_reset()
        self.setup()
        for hf in range(self.n_halves):
            for d in range(NT):
                fw.dma(fw.sp, self.x[:, d, :], self.xT[hf, d * 128:(d + 1) * 128, :], writes=[self.xb[d][0], self.xb[d][1]])
            stop_all = False
            for l in self.layers:
                if self.dbg >= 2 and hf == 0:
                    if l == self.layers[0]:
                        self.ada(l)
                    else:
                        if self.ada_l == l:
                            self.ada_some(144)
                            self.ada_flush()
                    nxt = [m for m in self.layers if m > l]
                    if nxt and self.stop is None:
                        self.ada_begin(nxt[0])
                if self.dbg == 3:
                    self.norm(l, 0)
                if self.dbg < 4:
                    break
                for s_ in range(3):
                    if s_ == 1:
                        [self.mixer_rg, self.mixer_sc, self.mixer_fox, self.mixer_hg][l % 4](l, hf)
                    else:
                        self.ffn(l, 0 if s_ == 0 else 1)
                    if self.stop is not None and tuple(self.stop) == (l, s_):
                        stop_all = True
                        break
                if stop_all:
                    break
            for d in range(NT):
                fw.dma(fw.sp, self.oT[hf, d * 128:(d + 1) * 128, :], self.x[:, d, :], reads=[self.xb[d][0], self.xb[d][1]])

    def build(self):
        with ExitStack() as st:
            self.alloc(st)
            self.ws_units = []
            self.fw.planning = True
            self.emit_program()
            self.fw.planning = False
            self.emit_program()
            self.fw.finish()
            self.fw.emit()
        return self.nc


_CACHE = {}


ACTIVE = (0, 1, 4, 5)


def run(inputs, layers=(0, 1, 2, 3), n_halves=2, stop=None, trace=False, n_cores=N_CORES, spread=False):
    key = (tuple(layers), n_halves, stop)
    if key not in _CACHE:
        _CACHE[key] = Prog(layers, n_halves, stop).build()
    nc = _CACHE[key]
    x = np.asarray(inputs["x"], np.float32)
    wts = weight_arrays(inputs, layers)
    maps = []
    for b in range(n_cores):
        xt = np.ascontiguousarray(x[b].T.reshape(D, 2, T).transpose(1, 0, 2))
        m = {"xT": xt, "prm": pack_params(inputs, b)}
        m.update(wts)
        maps.append(m)
    if spread:
        zero = {k: np.zeros_like(v) for k, v in maps[0].items()}
        in_maps = [zero] * 8
        in_maps = list(in_maps)
        for b in range(n_cores):
            in_maps[ACTIVE[b]] = maps[b]
        cores = [ACTIVE[b] for b in range(n_cores)]
    else:
        in_maps = maps
        cores = list(range(n_cores))
    res = run_bass_kernel_spmd(nc, in_maps, core_ids=list(range(len(in_maps))), **({"trace": True} if trace else {}))
    out = np.empty((n_cores, 2 * T, D), np.float32)
    for b in range(n_cores):
        o = res.results[cores[b]]["oT"]
        out[b] = o.transpose(0, 2, 1).reshape(2 * T, D)
    return out, res


def kernel(**inputs):
    out, _ = run(inputs, spread=True)
    return out
```

```python
import numpy as np
from contextlib import ExitStack
import concourse.bass as bass
import concourse.mybir as mybir
from concourse.bass_utils import run_bass_kernel_spmd

F32 = mybir.dt.float32
BF16 = mybir.dt.bfloat16
AF = mybir.ActivationFunctionType
ALU = mybir.AluOpType

D = 2048
NT = 16
T = 1024
TB = 512
FF = 5632
NF = 44
FC = 4
NCH = NF // FC
EPS = 1e-6
NS = 6
NTF = 10
NTH = 7
N_CORES = 4


def I(name, *args, **kwargs):
    return (name, args, kwargs)


class Eng:
    def __init__(self, name, sem, unit=1):
        self.name, self.sem, self.unit = name, sem, unit
        self.count = 0
        self.seen = {}
        self.ops = []


class Buf:
    __slots__ = ("name", "w", "r")

    def __init__(self, name=""):
        self.name = name
        self.w = None
        self.r = {}


class FW:
    def __init__(self, nc, stack, n_chan=6):
        self.nc = nc
        self.stack = stack
        self.planning = False
        sem = lambda n: stack.enter_context(nc.semaphore(n))
        self.pe = Eng("pe", sem("s_pe"))
        self.act = Eng("act", sem("s_act"))
        self.dve = Eng("dve", sem("s_dve"))
        self.pool = Eng("pool", sem("s_pool"))
        self.sp = Eng("sp", sem("s_sp"))
        self.chans = {
            "sp": [Eng(f"chs{i}", sem(f"s_chs{i}"), unit=16) for i in range(n_chan)],
            "pool": [Eng(f"chp{i}", sem(f"s_chp{i}"), unit=16) for i in range(n_chan)],
        }
        self.chan_i = {"sp": 0, "pool": 0}
        self.cch = Eng("cch", sem("s_cch"), unit=16)

    def buf(self, name=""):
        return Buf(name)

    def sbuf(self, name, shape, dt):
        return self.stack.enter_context(self.nc.sbuf_tensor(name, list(shape), dt))

    def psum(self, name, shape, dt=F32):
        return self.stack.enter_context(self.nc.psum_tensor(name, list(shape), dt))

    def _deps(self, E, reads, writes):
        deps = {}
        for b in reads:
            if b.w is not None:
                e, i = b.w
                if deps.get(e, 0) < i:
                    deps[e] = i
        for b in writes:
            if b.w is not None:
                e, i = b.w
                if deps.get(e, 0) < i:
                    deps[e] = i
            for e, i in b.r.items():
                if deps.get(e, 0) < i:
                    deps[e] = i
        waits = []
        for e, i in deps.items():
            if e is E and E is self.pe:
                continue
            if E.seen.get(e, 0) < i:
                E.seen[e] = i
                waits.append((e.sem, i * e.unit))
        return waits

    def _mark(self, E, idx, reads, writes):
        for b in writes:
            b.w = (E, idx)
            b.r = {}
        for b in reads:
            if b.w is not None and b.w[0] is E and b.w[1] == idx:
                continue
            b.r[E] = idx

    def op(self, E, fn, reads=(), writes=()):
        if self.planning:
            return
        waits = self._deps(E, reads, writes)
        E.count += 1
        idx = E.count
        sem = E.sem

        def run(h, waits=waits, fn=fn, sem=sem):
            for s, v in waits:
                h.wait_ge(s, v)
            getattr(h, fn[0])(*fn[1], **fn[2]).then_inc(sem, 1)

        E.ops.append(run)
        self._mark(E, idx, reads, writes)

    def multi(self, E, fns, reads=(), writes=()):
        if self.planning:
            return
        waits = self._deps(E, reads, writes)
        E.count += 1
        idx = E.count
        sem = E.sem

        def run(h, waits=waits, fns=fns, sem=sem):
            for s, v in waits:
                h.wait_ge(s, v)
            for f in fns[:-1]:
                getattr(h, f[0])(*f[1], **f[2])
            f = fns[-1]
            getattr(h, f[0])(*f[1], **f[2]).then_inc(sem, 1)

        E.ops.append(run)
        self._mark(E, idx, reads, writes)

    def dma(self, Q, out_ap, in_ap, reads=(), writes=()):
        if self.planning:
            return
        chl = self.chans[Q.name]
        ch = chl[self.chan_i[Q.name] % len(chl)]
        self.chan_i[Q.name] += 1
        waits = self._deps(Q, reads, writes)
        if ch.count > 0 and Q.seen.get(ch, 0) < ch.count:
            Q.seen[ch] = ch.count
            waits.append((ch.sem, ch.count * 16))
        ch.count += 1
        idx = ch.count
        sem = ch.sem

        def run(h, waits=waits, sem=sem):
            for s, v in waits:
                h.wait_ge(s, v)
            h.dma_start(out=out_ap, in_=in_ap).then_inc(sem, 16)

        Q.ops.append(run)
        for b in writes:
            b.w = (ch, idx)
            b.r = {}
        for b in reads:
            b.r[ch] = idx

    def coll(self, src, dst, groups, reads=(), writes=()):
        if self.planning:
            return
        Q = self.pool
        ch = self.cch
        waits = self._deps(Q, reads, writes)
        if ch.count > 0 and Q.seen.get(ch, 0) < ch.count:
            Q.seen[ch] = ch.count
            waits.append((ch.sem, ch.count * 16))
        ch.count += 1
        idx = ch.count
        sem = ch.sem

        def run(h, waits=waits, sem=sem):
            for s_, v in waits:
                h.wait_ge(s_, v)
            h.collective_compute("AllGather", ALU.bypass, replica_groups=groups, ins=[src], outs=[dst]).then_inc(sem, 16)

        Q.ops.append(run)
        for b in writes:
            b.w = (ch, idx)
            b.r = {}
        for b in reads:
            b.r[ch] = idx

    def finish(self):
        waits = []
        for chl in self.chans.values():
            for ch in chl:
                if ch.count:
                    waits.append((ch.sem, ch.count * 16))
        for E in (self.pe, self.act, self.dve, self.pool):
            if E.count:
                waits.append((E.sem, E.count))
        if self.cch.count:
            waits.append((self.cch.sem, self.cch.count * 16))

        def run(h):
            for s, v in waits:
                h.wait_ge(s, v)

        self.sp.ops.append(run)

    def emit(self):
        with self.nc.Block() as block:
            @block.sync
            def _(h):
                for f in self.sp.ops:
                    f(h)

            @block.tensor
            def _(h):
                for f in self.pe.ops:
                    f(h)

            @block.scalar
            def _(h):
                for f in self.act.ops:
                    f(h)

            @block.vector
            def _(h):
                for f in self.dve.ops:
                    f(h)

            @block.gpsimd
            def _(h):
                for f in self.pool.ops:
                    f(h)


PC = {}


def _pc_layout():
    off = 0

    def add(n, w):
        nonlocal off
        PC[n] = off
        off += w

    add("c", 16)
    for l in range(4):
        add(f"ada_b{l}", 144)
    for l in range(4):
        for s in range(3):
            add(f"ng{l}_{s}", 16)
    add("rg_cw", 64)
    add("rg_cb", 16)
    add("rg_br", 16)
    add("rg_bi", 16)
    add("rg_lam", 16)
    add("sc_cw", 48)
    add("fox_qg", 1)
    add("fox_kg", 1)
    add("fox_bf", 1)
    add("hg_lg", 64)
    add("hg_ng", 1)
    return off


NP = _pc_layout()


def _v16(v):
    return np.asarray(v, np.float32).reshape(-1, 128).T


def pack_params(inp, b):
    P = np.zeros((128, NP), np.float32)

    def put(name, arr):
        a = np.asarray(arr, np.float32)
        P[: a.shape[0], PC[name]: PC[name] + a.shape[1]] = a

    put("c", _v16(inp["c"][b]))
    for l in range(4):
        put(f"ada_b{l}", _v16(inp["ada_b"][l]))
        for s in range(3):
            put(f"ng{l}_{s}", _v16(inp["norm_g"][l, s]))
    put("rg_cw", np.concatenate([_v16(inp["rg_conv_w"][0, k]) for k in range(4)], axis=1))
    put("rg_cb", _v16(inp["rg_conv_b"][0]))
    put("rg_br", _v16(inp["rg_b_r"][0]))
    put("rg_bi", _v16(inp["rg_b_i"][0]))
    put("rg_lam", _v16(inp["rg_lam"][0]))
    put("sc_cw", np.concatenate([_v16(inp["sc_conv_w"][0, k]) for k in range(3)], axis=1))
    put("fox_qg", np.asarray(inp["fox_q_gain"][0]).reshape(128, 1))
    put("fox_kg", np.asarray(inp["fox_k_gain"][0]).reshape(128, 1))
    put("fox_bf", np.asarray(inp["fox_b_f"][0]).reshape(16, 1))
    put("hg_lg", np.concatenate([_v16(inp["hg_lb_logits"][l]) for l in range(4)], axis=1))
    put("hg_ng", np.asarray(inp["hg_norm_gain"][0]).reshape(128, 1))
    return P


MIXW = {
    0: {"rg_w_in": [2048, 4096], "rg_w_r": [16, 128, 128], "rg_w_i": [16, 128, 128], "rg_w_out": [2048, 2048]},
    1: {"sc_w_in": [2048, 6144], "sc_w_out": [2048, 2048]},
    2: {"fox_w_in": [2048, 8208], "fox_w_out": [2048, 2048]},
    3: {"hg_w_in": [2048, 8192], "hg_w_out": [2048, 2048]},
}


def weight_shapes(layers):
    ws = {}
    for l in layers:
        ws[f"ada_w{l}"] = [2048, 18432]
        for w in range(2):
            ws[f"ffn_g{l}_{w}"] = [2048, 5632]
            ws[f"ffn_u{l}_{w}"] = [2048, 5632]
            ws[f"ffn_d{l}_{w}"] = [5632, 2048]
        ws.update(MIXW[l % 4])
    return ws


def weight_arrays(inp, layers):
    out = {}
    for name in weight_shapes(layers):
        if name.startswith("ada_w"):
            a = inp["ada_w"][int(name[5:])]
        elif name.startswith("ffn_"):
            l, w = int(name[5]), int(name[7])
            a = inp[{"g": "ffn_w_gate", "u": "ffn_w_up", "d": "ffn_w_down"}[name[4]]][l, w]
        else:
            a = inp[name][0]
        out[name] = np.ascontiguousarray(np.asarray(a, np.float32))
    return out


def tbs(tb):
    return slice(tb * TB, (tb + 1) * TB)


class Prog:
    def __init__(self, layers=(0, 1, 2, 3), n_halves=2, stop=None):
        self.layers = tuple(layers)
        self.n_halves = n_halves
        self.stop = stop
        import os
        self.dbg = int(os.environ.get("KDBG", "9"))
        self.nc = bass.Bass("TRN2", target_bir_lowering=False)

    def alloc(self, st):
        nc = self.nc
        fw = self.fw = FW(nc, st)
        dr = lambda n, s, dt, kind: nc.dram_tensor(n, list(s), dt, kind=kind).ap()
        self.xT = dr("xT", [2, D, T], F32, "ExternalInput")
        self.oT = dr("oT", [2, D, T], F32, "ExternalOutput")
        self.prm = dr("prm", [128, NP], F32, "ExternalInput")
        self.W = {k: dr(k, s, F32, "ExternalInput") for k, s in weight_shapes(self.layers).items()}
        self.Kd = dr("Kd", [16, 128, T], BF16, "Internal")
        self.Vd = dr("Vd", [16, 128, T], BF16, "Internal")
        self.Sd = dr("Sd", [16, 128, 128], F32, "Internal")
        self.F3d = dr("F3d", [3, 16, T], BF16, "Internal")
        B = fw.buf
        self.b_Kd = [B() for _ in range(16)]
        self.b_Vd = [B() for _ in range(16)]
        self.b_Sd = [B() for _ in range(16)]
        self.b_F3d = B()

        self.x = fw.sbuf("x", [128, NT, T], F32)
        self.xb = [[B() for _ in range(2)] for _ in range(NT)]
        self.h = fw.sbuf("h", [128, NT, T], BF16)
        self.hb = [B(), B()]
        self.y = fw.sbuf("y", [128, NT, T], BF16)
        self.yb = [[B() for _ in range(2)] for _ in range(NT)]
        self.ring = [fw.sbuf(f"ring{i}", [128, 2048], BF16) for i in range(NS)]
        self.ringb = [B() for _ in range(NS)]
        self.TFt = [fw.sbuf(f"tf{i}", [128, TB], F32) for i in range(NTF)]
        self.TFb = [B() for _ in range(NTF)]
        self.THt = [fw.sbuf(f"th{i}", [128, T], BF16) for i in range(NTH)]
        self.THb = [B() for _ in range(NTH)]
        self.PT = [fw.sbuf(f"pt{i}", [128, TB], BF16) for i in range(2)]
        self.PTb = [B(), B()]
        self.prm_sb = fw.sbuf("prm_sb", [128, NP], F32)
        self.b_prm = B()
        self.mods = fw.sbuf("mods", [128, 4, 144], F32)
        self.b_mods = [B() for _ in range(4)]
        self.derA = fw.sbuf("derA", [128, 4, 3, 16], F32)
        self.derG = fw.sbuf("derG", [128, 4, 3, 16], F32)
        self.b_der = [B() for _ in range(4)]
        self.ones_f = fw.sbuf("ones_f", [128, 128], F32)
        self.ones_b = fw.sbuf("ones_b", [128, 128], BF16)
        self.ident_f = fw.sbuf("ident_f", [128, 128], F32)
        self.ident_b = fw.sbuf("ident_b", [128, 128], BF16)
        self.mtri = fw.sbuf("mtri", [128, 128], BF16)
        self.maskC = fw.sbuf("maskC", [128, TB], BF16)
        self.ones16 = fw.sbuf("ones16", [16, TB], BF16)
        self.b_const = B()
        self.scb = fw.sbuf("scb", [128, 16], BF16)
        self.b_scb = B()
        self.pad = fw.sbuf("pad", [128, T + 8], F32)
        self.b_pad = B()
        self.tail = fw.sbuf("tail", [128, 16, 3], F32)
        self.b_tail = B()
        self.tail2 = fw.sbuf("tail2", [128, 16, 2], F32)
        self.b_tail2 = B()
        self.hstate = fw.sbuf("hstate", [128, 16], F32)
        self.b_hstate = B()
        self.small = fw.sbuf("small", [128, 160], F32)
        self.b_small = B()
        self.flw = fw.sbuf("flw", [128, 16, 16], BF16)
        self.b_flw = B()
        self.CsT = fw.sbuf("CsT", [128, 8, 16], F32)
        self.b_CsT = B()
        self.DkT = fw.sbuf("DkT", [128, 8, 16], F32)
        self.b_DkT = B()
        self.dec = fw.sbuf("dec", [128, 8], F32)
        self.b_dec = B()
        self.S = fw.sbuf("S", [128, 128], F32)
        self.b_S = B()
        self.Sbf = fw.sbuf("Sbf", [128, 128], BF16)
        self.b_Sbf = B()
        self.scbf = [fw.sbuf(f"scbf{i}", [128, 128], BF16) for i in range(2)]
        self.b_scbf = [B(), B()]
        self.ps = [fw.psum(f"ps{i}", [128, TB], F32) for i in range(7)]
        self.psb = [B() for _ in range(7)]
        self.pst = fw.psum("pst", [128, T], BF16)
        self.pstb = B()

    def TF(self, i):
        return self.TFt[i], self.TFb[i]

    def TH(self, i):
        return self.THt[i], self.THb[i]

    def P_(self, i):
        return self.ps[i], self.psb[i]

    def pcol(self, name, w=16, off=0):
        o = PC[name] + off
        return self.prm_sb[:, o:o + w]

    def ws_reset(self):
        self.ws_next = 0
        self.ws_issued = 0

    def ws_issue(self, i):
        tag, src, view = self.ws_units[i]
        slot = i % NS
        if view == "col":
            kt = src.shape[0] // 128
            dst = self.ring[slot][:, 0:kt * 128].rearrange("p (kt n) -> p kt n", n=128)
            s = src.rearrange("(kt p) n -> p kt n", p=128)
        elif view == "row":
            dst = self.ring[slot][:, :]
            s = src
        else:
            dst = self.ring[slot][:, 0:128]
            s = src
        self.fw.dma(self.fw.pool, dst, s, writes=[self.ringb[slot]])

    def ws_get(self, tag, src, view):
        i = self.ws_next
        self.ws_next += 1
        if self.fw.planning:
            self.ws_units.append((tag, src, view))
        else:
            assert self.ws_units[i][0] == tag, (self.ws_units[i][0], tag)
            while self.ws_issued <= i:
                self.ws_issue(self.ws_issued)
                self.ws_issued += 1
        return self.ring[i % NS], self.ringb[i % NS]

    def ws_pump(self):
        if self.fw.planning:
            return
        lim = min(len(self.ws_units), self.ws_next + NS)
        while self.ws_issued < lim:
            self.ws_issue(self.ws_issued)
            self.ws_issued += 1

    def mm16(self, p, pb, u, ub, tb, extra_reads=()):
        h = self.h
        fns = [(I("matmul", p[:, :], lhsT=u[:, kt * 128:(kt + 1) * 128], rhs=h[:, kt, tbs(tb)],
                                          start=(kt == 0), stop=(kt == 15))) for kt in range(16)]
        self.fw.multi(self.fw.pe, fns, reads=[ub, self.hb[tb]] + list(extra_reads), writes=[pb])

    def stop_here(self, l, s):
        return self.stop is not None and (l, s) >= tuple(self.stop)

    def setup(self):
        fw = self.fw
        A, Dv, Pl = fw.act, fw.dve, fw.pool
        bc = self.b_const
        fw.dma(fw.sp, self.prm_sb[:, :], self.prm, writes=[self.b_prm])
        fw.op(Dv, I("memset", self.ones_f[:, :], 1.0), writes=[bc])
        fw.op(Dv, I("memset", self.ones_b[:, :], 1.0), writes=[bc])
        fw.op(Dv, I("memset", self.ones16[:, :], 1.0), writes=[bc])
        fw.op(Dv, I("memset", self.maskC[:, :], 1.0), writes=[bc])
        for j in range(4):
            fw.op(Dv, I("memset", self.maskC[:, j * 128:j * 128 + 1], 0.0), reads=[bc], writes=[bc])
        fw.op(Pl, I("memset", self.ident_f[:, :], 0.0), writes=[bc])
        fw.op(Pl, I("affine_select", out=self.ident_f[:, :], in_=self.ident_f[:, :], pattern=[[-1, 128]],
                                            compare_op=ALU.not_equal, fill=1.0, base=0, channel_multiplier=1),
              reads=[bc], writes=[bc])
        self.mtri_f = self.TFt[0][:, 0:128]
        fw.op(Pl, I("memset", self.mtri_f, 1.0), reads=[bc, self.TFb[0]], writes=[bc, self.TFb[0]])
        fw.op(Pl, I("affine_select", out=self.mtri_f, in_=self.mtri_f, pattern=[[1, 128]],
                                            compare_op=ALU.is_ge, fill=0.0, base=0, channel_multiplier=-1),
              reads=[bc, self.TFb[0]], writes=[bc, self.TFb[0]])
        fw.op(Dv, I("tensor_copy", out=self.ident_b[:, :], in_=self.ident_f[:, :]), reads=[bc], writes=[bc])
        fw.op(Dv, I("tensor_copy", out=self.mtri[:, :], in_=self.mtri_f), reads=[bc, self.TFb[0]], writes=[bc])
        fw.op(A, I("activation", out=self.scb[:, :], in_=self.pcol("c"), func=AF.Silu),
              reads=[self.b_prm], writes=[self.b_scb])
        sm = self.small
        bs = self.b_small
        fw.op(A, I("activation", out=sm[:, 0:16], in_=self.pcol("rg_lam"), func=AF.Exp, scale=-1.0),
              reads=[self.b_prm], writes=[bs])
        fw.op(A, I("activation", out=sm[:, 0:16], in_=sm[:, 0:16], func=AF.Ln, bias=1.0), reads=[bs], writes=[bs])
        fw.op(Dv, I("tensor_single_scalar", out=sm[:, 16:32], in_=sm[:, 0:16], scalar=-16.0, op=ALU.mult),
              reads=[bs], writes=[bs])
        fw.op(Dv, I("tensor_single_scalar", out=sm[:, 0:16], in_=sm[:, 0:16], scalar=-8.0, op=ALU.mult),
              reads=[bs], writes=[bs])
        fw.op(A, I("activation", out=sm[:, 64:128], in_=self.pcol("hg_lg", 64), func=AF.Exp),
              reads=[self.b_prm], writes=[bs])
        fw.op(Dv, I("tensor_tensor", out=sm[:, 128:144], in0=sm[:, 64:80], in1=sm[:, 80:96], op=ALU.add), reads=[bs], writes=[bs])
        fw.op(Dv, I("tensor_tensor", out=sm[:, 128:144], in0=sm[:, 128:144], in1=sm[:, 96:112], op=ALU.add), reads=[bs], writes=[bs])
        fw.op(Dv, I("tensor_tensor", out=sm[:, 128:144], in0=sm[:, 128:144], in1=sm[:, 112:128], op=ALU.add), reads=[bs], writes=[bs])
        fw.op(Dv, I("reciprocal", out=sm[:, 128:144], in_=sm[:, 128:144]), reads=[bs], writes=[bs])
        fw.op(Dv, I("tensor_tensor", out=sm[:, 32:48], in0=sm[:, 112:128], in1=sm[:, 128:144], op=ALU.mult), reads=[bs], writes=[bs])
        fw.op(Dv, I("tensor_single_scalar", out=sm[:, 48:64], in_=sm[:, 32:48], scalar=-1.0, op=ALU.mult), reads=[bs], writes=[bs])
        fw.op(Dv, I("tensor_single_scalar", out=sm[:, 144:145], in_=self.pcol("fox_bf", 1), scalar=-1.0, op=ALU.mult),
              reads=[self.b_prm, bs], writes=[bs])
        fw.op(Dv, I("tensor_single_scalar", out=sm[:, 145:146], in_=self.pcol("fox_qg", 1), scalar=float(128 ** -0.5), op=ALU.mult),
              reads=[self.b_prm, bs], writes=[bs])
        if "fox_w_in" in self.W:
            fw.dma(fw.pool, self.flw[:, :, :], self.W["fox_w_in"][:, 8192:8208].rearrange("(kt p) n -> p kt n", p=128),
                   writes=[self.b_flw])
        fw.op(Dv, I("memset", self.tail[:, :, :], 0.0), writes=[self.b_tail])
        fw.op(Dv, I("memset", self.tail2[:, :, :], 0.0), writes=[self.b_tail2])
        fw.op(Dv, I("memset", self.hstate[:, :], 0.0), writes=[self.b_hstate])

    def ada_begin(self, l):
        self.ada_l, self.ada_j, self.ada_fl = l, 0, 0

    def ada_some(self, n):
        fw = self.fw
        l = self.ada_l
        if l is None:
            return
        pN, pNb = self.P_(6)
        Wl = self.W[f"ada_w{l}"]
        for _ in range(n):
            j = self.ada_j
            if j >= 144:
                return
            u, ub = self.ws_get(("ada", l, j), Wl[:, j * 128:(j + 1) * 128], "col")
            fns = [I("matmul", pN[:, j:j + 1], lhsT=u[:, kt * 128:(kt + 1) * 128],
                     rhs=self.scb[:, kt:kt + 1], start=(kt == 0), stop=(kt == 15))
                   for kt in range(16)]
            fw.multi(fw.pe, fns, reads=[ub, self.b_scb], writes=[pNb])
            self.ws_pump()
            self.ada_j += 1

    def ada_flush(self):
        fw = self.fw
        l = self.ada_l
        if l is None:
            return
        pN, pNb = self.P_(6)
        a, b = self.ada_fl, self.ada_j
        bm = self.b_mods[l]
        if b > a:
            o = PC[f"ada_b{l}"]
            fw.op(fw.dve, I("tensor_tensor", out=self.mods[:, l, a:b], in0=pN[:, a:b], in1=self.prm_sb[:, o + a:o + b], op=ALU.add),
                  reads=[pNb, self.b_prm, bm], writes=[bm])
            self.ada_fl = b
        if b >= 144:
            bd = self.b_der[l]
            for s in range(3):
                sc_ = self.mods[:, l, (s * 3 + 1) * 16:(s * 3 + 2) * 16]
                gt_ = self.mods[:, l, (s * 3 + 2) * 16:(s * 3 + 3) * 16]
                fw.op(fw.dve, I("scalar_tensor_tensor", out=self.derA[:, l, s, :], in0=sc_, scalar=1.0,
                                in1=self.pcol(f"ng{l}_{s}"), op0=ALU.add, op1=ALU.mult),
                      reads=[bm, self.b_prm, bd], writes=[bd])
                fw.op(fw.dve, I("tensor_single_scalar", out=self.derG[:, l, s, :], in_=gt_,
                                scalar=(1.0 if s == 1 else 0.5), op=ALU.mult),
                      reads=[bm, bd], writes=[bd])
            self.ada_l = None

    def ada_flush_partial(self):
        if self.ada_l is not None:
            self.ada_flush()

    def ada(self, l):
        self.ada_begin(l)
        self.ada_some(144)
        self.ada_flush()

    def norm(self, l, s):
        fw = self.fw
        x, h = self.x, self.h
        bd = self.b_der[l]
        bm = self.b_mods[l]
        pN, pNb = self.P_(6)
        for tb in range(2):
            for d in range(NT):
                sq, sqb = self.TF(d % 2)
                fw.op(fw.act, I("activation", out=sq[:, :], in_=x[:, d, tbs(tb)], func=AF.Square),
                      reads=[self.xb[d][tb]], writes=[sqb])
                fw.op(fw.pe, I("matmul", pN[:, :], lhsT=self.ones_f[:, :], rhs=sq[:, :], start=(d == 0), stop=(d == 15)),
                      reads=[sqb, self.b_const], writes=[pNb])
            rt, rtb = self.TF(2)
            fw.op(fw.act, I("activation", out=rt[:, :], in_=pN[:, :], func=AF.Sqrt, scale=1.0 / D, bias=EPS),
                  reads=[pNb], writes=[rtb])
            rs, rsb = self.TF(3)
            fw.op(fw.dve, I("reciprocal", out=rs[:, :], in_=rt[:, :]), reads=[rtb], writes=[rsb])
            for d in range(NT):
                tn, tnb = self.TF(4 + d % 2)
                fw.op(fw.dve, I("scalar_tensor_tensor", out=tn[:, :], in0=x[:, d, tbs(tb)], scalar=self.derA[:, l, s, d:d + 1],
                                                                          in1=rs[:, :], op0=ALU.mult, op1=ALU.mult),
                      reads=[self.xb[d][tb], rsb, bd], writes=[tnb])
                sh = self.mods[:, l, (s * 3) * 16 + d:(s * 3) * 16 + d + 1]
                fw.op(fw.act, I("activation", out=h[:, d, tbs(tb)], in_=tn[:, :], func=AF.Identity, bias=sh),
                      reads=[tnb, bm], writes=[self.hb[tb]])

    def ffn(self, l, w):
        fw = self.fw
        s = 0 if w == 0 else 2
        self.norm(l, s)
        bd = self.b_der[l]
        Wg, Wu, Wd = self.W[f"ffn_g{l}_{w}"], self.W[f"ffn_u{l}_{w}"], self.W[f"ffn_d{l}_{w}"]
        x, y = self.x, self.y
        rot = 0
        import os
        nch_dbg = int(os.environ.get("KCH", str(NCH)))
        nodown = int(os.environ.get("KNODOWN", "0"))
        for ch in range(nch_dbg):
            for fi in range(FC):
                ft = ch * FC + fi
                sl = (ch % 2) * FC + fi
                ug, ugb = self.ws_get(("g", l, w, ft), Wg[:, ft * 128:(ft + 1) * 128], "col")
                uu, uub = self.ws_get(("u", l, w, ft), Wu[:, ft * 128:(ft + 1) * 128], "col")
                for tb in range(2):
                    pg, pgb = self.P_(tb)
                    pu, pub = self.P_(2 + tb)
                    self.mm16(pg, pgb, ug, ugb, tb)
                    self.mm16(pu, pub, uu, uub, tb)
                    sg, sgb = self.TF(6 + tb)
                    fw.op(fw.act, I("activation", out=sg[:, :], in_=pg[:, :], func=AF.Silu),
                          reads=[pgb], writes=[sgb])
                    fw.op(fw.dve, I("tensor_tensor", out=y[:, sl, tbs(tb)], in0=sg[:, :], in1=pu[:, :], op=ALU.mult),
                          reads=[sgb, pub], writes=[self.yb[sl][tb]])
                self.ws_pump()
                self.ada_some(2)
            if nodown:
                continue
            uds = [self.ws_get(("d", l, w, ch * FC + fi), Wd[(ch * FC + fi) * 128:(ch * FC + fi + 1) * 128, :], "row")
                   for fi in range(FC)]
            for d in range(NT):
                for tb in range(2):
                    pd, pdb = self.P_(4 + rot % (3 if self.ada_l is None else 2))
                    rot += 1
                    fns = [(I("matmul", pd[:, :], lhsT=uds[fi][0][:, d * 128:(d + 1) * 128],
                                                                        rhs=y[:, (ch % 2) * FC + fi, tbs(tb)],
                                                                        start=(fi == 0), stop=(fi == FC - 1)))
                           for fi in range(FC)]
                    fw.multi(fw.pe, fns, reads=[u[1] for u in uds] + [self.yb[(ch % 2) * FC + fi][tb] for fi in range(FC)],
                             writes=[pdb])
                    fw.op(fw.dve, I("scalar_tensor_tensor", out=x[:, d, tbs(tb)], in0=pd[:, :],
                                                                                     scalar=self.derG[:, l, s, d:d + 1], in1=x[:, d, tbs(tb)],
                                                                                     op0=ALU.mult, op1=ALU.add),
                          reads=[pdb, self.xb[d][tb], bd], writes=[self.xb[d][tb]])
            self.ws_pump()
        self.ada_flush_partial()

    def outproj(self, l, Wout, name):
        fw = self.fw
        bd = self.b_der[l]
        x, y = self.x, self.y
        rot = 0
        for d in range(NT):
            u, ub = self.ws_get((name, "out", d), Wout[:, d * 128:(d + 1) * 128], "col")
            for tb in range(2):
                pd, pdb = self.P_(4 + rot % 3)
                rot += 1
                fns = [(I("matmul", pd[:, :], lhsT=u[:, c * 128:(c + 1) * 128], rhs=y[:, c, tbs(tb)],
                                                                  start=(c == 0), stop=(c == 15))) for c in range(16)]
                fw.multi(fw.pe, fns, reads=[ub] + [self.yb[c][tb] for c in range(16)], writes=[pdb])
                fw.op(fw.dve, I("scalar_tensor_tensor", out=x[:, d, tbs(tb)], in0=pd[:, :],
                                                                                 scalar=self.derG[:, l, 1, d:d + 1], in1=x[:, d, tbs(tb)],
                                                                                 op0=ALU.mult, op1=ALU.add),
                      reads=[pdb, self.xb[d][tb], bd], writes=[self.xb[d][tb]])
            self.ws_pump()

    def inproj(self, Win, col0, tag, banks):
        u, ub = self.ws_get(tag, Win[:, col0:col0 + 128], "col")
        out = []
        for tb in range(2):
            p, pb = self.P_(banks[tb])
            self.mm16(p, pb, u, ub, tb)
            out.append((p, pb))
        self.ws_pump()
        return out

    def mixer_rg(self, l, hf):
        fw = self.fw
        A, Dv = fw.act, fw.dve
        self.norm(l, 1)
        Win = self.W["rg_w_in"]
        pad, bp = self.pad, self.b_pad
        sm, bs = self.small, self.b_small
        y = self.y
        for c in range(16):
            pg = self.inproj(Win, c * 128, ("rg", "g", c), (0, 1))
            px = self.inproj(Win, 2048 + c * 128, ("rg", "x", c), (2, 3))
            wr, wrb = self.ws_get(("rg", "wr", c), self.W["rg_w_r"][c], "blk")
            wi, wib = self.ws_get(("rg", "wi", c), self.W["rg_w_i"][c], "blk")
            fw.op(Dv, I("tensor_copy", out=pad[:, 0:3], in_=self.tail[:, c, :]), reads=[self.b_tail], writes=[bp])
            for tb in range(2):
                fw.op(A, I("activation", out=pad[:, 3 + tb * TB:3 + (tb + 1) * TB], in_=px[tb][0][:, :], func=AF.Copy),
                      reads=[px[tb][1]], writes=[bp])
            fw.op(Dv, I("tensor_copy", out=self.tail[:, c, :], in_=pad[:, T:T + 3]), reads=[bp], writes=[self.b_tail])
            for tb in range(2):
                o = tb * TB
                gl, glb = self.TF(0 + tb)
                fw.op(A, I("activation", out=gl[:, :], in_=pg[tb][0][:, :], func=AF.Gelu), reads=[pg[tb][1]], writes=[glb])
                xc, xcb = self.TF(2 + tb)
                cw = lambda k, c=c: self.pcol("rg_cw", 1, k * 16 + c)
                fw.op(A, I("activation", out=xc[:, :], in_=pad[:, 3 + o:3 + o + TB], func=AF.Identity,
                                                                scale=cw(3), bias=self.pcol("rg_cb", 1, c)),
                      reads=[bp, self.b_prm], writes=[xcb])
                for k in range(3):
                    fw.op(Dv, I("scalar_tensor_tensor", out=xc[:, :], in0=pad[:, k + o:k + o + TB], scalar=cw(k),
                                                                               in1=xc[:, :], op0=ALU.mult, op1=ALU.add),
                          reads=[bp, xcb, self.b_prm], writes=[xcb])
                xh, xhb = self.TH(0)
                fw.op(A, I("activation", out=xh[:, 0:TB], in_=xc[:, :], func=AF.Copy), reads=[xcb], writes=[xhb])
                pr, prb = self.P_(4)
                pi, pib = self.P_(5)
                fw.op(fw.pe, I("matmul", pr[:, :], lhsT=wr[:, 0:128], rhs=xh[:, 0:TB], start=True, stop=True),
                      reads=[wrb, xhb], writes=[prb])
                fw.op(fw.pe, I("matmul", pi[:, :], lhsT=wi[:, 0:128], rhs=xh[:, 0:TB], start=True, stop=True),
                      reads=[wib, xhb], writes=[pib])
                rm, rmb = self.TF(4 + tb)
                ig, igb = self.TF(6 + tb)
                fw.op(A, I("activation", out=rm[:, :], in_=pr[:, :], func=AF.Sigmoid, bias=self.pcol("rg_br", 1, c)),
                      reads=[prb, self.b_prm], writes=[rmb])
                fw.op(A, I("activation", out=ig[:, :], in_=pi[:, :], func=AF.Sigmoid, bias=self.pcol("rg_bi", 1, c)),
                      reads=[pib, self.b_prm], writes=[igb])
                a_, ab = self.TF(8)
                fw.op(A, I("activation", out=a_[:, :], in_=rm[:, :], func=AF.Exp, scale=sm[:, c:c + 1]),
                      reads=[rmb, bs], writes=[ab])
                fw.op(A, I("activation", out=rm[:, :], in_=rm[:, :], func=AF.Exp, scale=sm[:, 16 + c:17 + c]),
                      reads=[rmb, bs], writes=[rmb])
                fw.op(A, I("activation", out=rm[:, :], in_=rm[:, :], func=AF.Sqrt, scale=-1.0, bias=1.0),
                      reads=[rmb], writes=[rmb])
                fw.op(Dv, I("tensor_tensor", out=ig[:, :], in0=ig[:, :], in1=xc[:, :], op=ALU.mult),
                      reads=[igb, xcb], writes=[igb])
                fw.op(Dv, I("tensor_tensor", out=ig[:, :], in0=ig[:, :], in1=rm[:, :], op=ALU.mult),
                      reads=[igb, rmb], writes=[igb])
                hs, hsb = self.TF(9)
                fw.op(Dv, I("tensor_tensor_scan", out=hs[:, :], data0=a_[:, :], data1=ig[:, :],
                                                                                 initial=self.hstate[:, c:c + 1], op0=ALU.mult, op1=ALU.add),
                      reads=[ab, igb, self.b_hstate], writes=[hsb])
                fw.op(Dv, I("tensor_copy", out=self.hstate[:, c:c + 1], in_=hs[:, TB - 1:TB]),
                      reads=[hsb], writes=[self.b_hstate])
                fw.op(Dv, I("tensor_tensor", out=y[:, c, tbs(tb)], in0=hs[:, :], in1=gl[:, :], op=ALU.mult),
                      reads=[hsb, glb], writes=[self.yb[c][tb]])
        self.outproj(l, self.W["rg_w_out"], "rg")

    def mixer_sc(self, l, hf):
        fw = self.fw
        A, Dv = fw.act, fw.dve
        self.norm(l, 1)
        Win = self.W["sc_w_in"]
        pad, bp = self.pad, self.b_pad
        y = self.y
        for c in range(16):
            pbg = self.inproj(Win, c * 128, ("sc", "b", c), (0, 1))
            pcg = self.inproj(Win, 2048 + c * 128, ("sc", "c", c), (2, 3))
            bgs = []
            for tb in range(2):
                t, tb_ = self.TF(0 + tb)
                fw.op(A, I("activation", out=t[:, :], in_=pbg[tb][0][:, :], func=AF.Copy), reads=[pbg[tb][1]], writes=[tb_])
                bgs.append((t, tb_))
            cgs = []
            for tb in range(2):
                t, tb_ = self.TF(2 + tb)
                fw.op(A, I("activation", out=t[:, :], in_=pcg[tb][0][:, :], func=AF.Copy), reads=[pcg[tb][1]], writes=[tb_])
                cgs.append((t, tb_))
            pxv = self.inproj(Win, 4096 + c * 128, ("sc", "x", c), (0, 1))
            fw.op(Dv, I("tensor_copy", out=pad[:, 0:2], in_=self.tail2[:, c, 0:2]), reads=[self.b_tail2], writes=[bp])
            for tb in range(2):
                fw.op(Dv, I("tensor_tensor", out=pad[:, 2 + tb * TB:2 + (tb + 1) * TB], in0=cgs[tb][0][:, :], in1=pxv[tb][0][:, :], op=ALU.mult),
                      reads=[cgs[tb][1], pxv[tb][1]], writes=[bp])
            fw.op(Dv, I("tensor_copy", out=self.tail2[:, c, 0:2], in_=pad[:, T:T + 2]), reads=[bp], writes=[self.b_tail2])
            for tb in range(2):
                o = tb * TB
                uc, ucb = self.TF(4 + tb)
                cw = lambda k, c=c: self.pcol("sc_cw", 1, k * 16 + c)
                fw.op(A, I("activation", out=uc[:, :], in_=pad[:, 2 + o:2 + o + TB], func=AF.Identity, scale=cw(2)),
                      reads=[bp, self.b_prm], writes=[ucb])
                for k in range(2):
                    fw.op(Dv, I("scalar_tensor_tensor", out=uc[:, :], in0=pad[:, k + o:k + o + TB], scalar=cw(k),
                                                                               in1=uc[:, :], op0=ALU.mult, op1=ALU.add),
                          reads=[bp, ucb, self.b_prm], writes=[ucb])
                fw.op(Dv, I("tensor_tensor", out=y[:, c, tbs(tb)], in0=bgs[tb][0][:, :], in1=uc[:, :], op=ALU.mult),
                      reads=[ucb, bgs[tb][1]], writes=[self.yb[c][tb]])
        self.outproj(l, self.W["sc_w_out"], "sc")

    def rms_part(self, src, srcb, tf_sq, tf_rs, bank):
        fw = self.fw
        sq, sqb = self.TF(tf_sq)
        fw.op(fw.act, I("activation", out=sq[:, :], in_=src[:, :], func=AF.Square), reads=[srcb], writes=[sqb])
        pn, pnb = self.P_(bank)
        fw.op(fw.pe, I("matmul", pn[:, :], lhsT=self.ones_f[:, :], rhs=sq[:, :], start=True, stop=True),
              reads=[sqb, self.b_const], writes=[pnb])
        fw.op(fw.act, I("activation", out=sq[:, :], in_=pn[:, :], func=AF.Sqrt, scale=1.0 / 128, bias=EPS),
              reads=[pnb], writes=[sqb])
        rs, rsb = self.TF(tf_rs)
        fw.op(fw.dve, I("reciprocal", out=rs[:, :], in_=sq[:, :]), reads=[sqb], writes=[rsb])
        return rs, rsb

    def mixer_fox(self, l, hf):
        fw = self.fw
        A, Dv, PE = fw.act, fw.dve, fw.pe
        self.norm(l, 1)
        Win = self.W["fox_w_in"]
        sm, bs = self.small, self.b_small
        y = self.y
        Cs = [self.TF(8), self.TF(9)]
        for tb in range(2):
            pF, pFb = self.P_(4)
            fns = [(I("matmul", pF[0:16, :], lhsT=self.flw[:, kt, :], rhs=self.h[:, kt, tbs(tb)], start=(kt == 0), stop=(kt == 15)))
                   for kt in range(16)]
            fw.multi(PE, fns, reads=[self.b_flw, self.hb[tb]], writes=[pFb])
            t0, t0b = self.TF(0)
            fw.op(A, I("activation", out=t0[0:16, :], in_=pF[0:16, :], func=AF.Exp, scale=-1.0, bias=sm[0:16, 144:145]),
                  reads=[pFb, bs], writes=[t0b])
            fw.op(A, I("activation", out=t0[0:16, :], in_=t0[0:16, :], func=AF.Ln, bias=1.0), reads=[t0b], writes=[t0b])
            cs, csb = Cs[tb]
            init = 0.0 if tb == 0 else Cs[0][0][0:16, TB - 1:TB]
            fw.op(Dv, I("tensor_tensor_scan", out=cs[0:16, :], data0=self.ones16[:, :], data1=t0[0:16, :],
                                                                      initial=init, op0=ALU.mult, op1=ALU.add),
                  reads=[t0b, self.b_const] + ([Cs[0][1]] if tb else []), writes=[csb])
        hi, hib = self.TH(0)
        mid, midb = self.TH(1)
        lo, lob = self.TH(2)
        for tb in range(2):
            cs, csb = Cs[tb]
            r1, r1b = self.TF(0)
            fw.op(Dv, I("tensor_single_scalar", out=hi[0:16, tbs(tb)], in_=cs[0:16, :], scalar=-1.0, op=ALU.mult), reads=[csb], writes=[hib])
            fw.op(Dv, I("scalar_tensor_tensor", out=r1[0:16, :], in0=cs[0:16, :], scalar=-1.0, in1=hi[0:16, tbs(tb)], op0=ALU.mult, op1=ALU.subtract),
                  reads=[csb, hib], writes=[r1b])
            fw.op(Dv, I("tensor_copy", out=mid[0:16, tbs(tb)], in_=r1[0:16, :]), reads=[r1b], writes=[midb])
            fw.op(Dv, I("tensor_tensor", out=lo[0:16, tbs(tb)], in0=r1[0:16, :], in1=mid[0:16, tbs(tb)], op=ALU.subtract), reads=[r1b, midb], writes=[lob])
        for i, (t, tb_) in enumerate(((hi, hib), (mid, midb), (lo, lob))):
            fw.dma(fw.sp, self.F3d[i], t[0:16, :], reads=[tb_], writes=[self.b_F3d])
        pT, pTb = self.P_(5)
        if hf == 0:
            for kt in range(8):
                cs, csb = Cs[kt // 4]
                fw.op(PE, I("transpose", out=pT[:, kt * 16:(kt + 1) * 16], in_=cs[0:16, (kt % 4) * 128:(kt % 4 + 1) * 128],
                                                             identity=self.ident_f[0:16, 0:16]), reads=[csb, self.b_const], writes=[pTb])
        else:
            for kt in range(8):
                cs, csb = Cs[kt // 4]
                fw.op(PE, I("transpose", out=pT[:, kt * 16:(kt + 1) * 16], in_=cs[0:16, (kt % 4) * 128:(kt % 4 + 1) * 128],
                                                             identity=self.ident_f[0:16, 0:16]), reads=[csb, self.b_const], writes=[pTb])
        fw.op(Dv, I("tensor_copy", out=self.CsT[:, :, :], in_=pT[:, 0:128].rearrange("p (k h) -> p k h", h=16)), reads=[pTb], writes=[self.b_CsT])
        if hf == 0 and self.n_halves > 1:
            for tb in range(2):
                cs, csb = Cs[tb]
                d0, d0b = self.TF(0 + tb)
                fw.op(Dv, I("tensor_single_scalar", out=d0[0:16, :], in_=cs[0:16, :], scalar=Cs[1][0][0:16, TB - 1:TB], op=ALU.subtract),
                      reads=[csb, Cs[1][1]], writes=[d0b])
            pT2, pT2b = self.P_(4)
            for kt in range(8):
                d0, d0b = self.TF(kt // 4)
                fw.op(PE, I("transpose", out=pT2[:, kt * 16:(kt + 1) * 16], in_=d0[0:16, (kt % 4) * 128:(kt % 4 + 1) * 128],
                                                             identity=self.ident_f[0:16, 0:16]), reads=[d0b, self.b_const], writes=[pT2b])
            fw.op(Dv, I("tensor_copy", out=self.DkT[:, :, :], in_=pT2[:, 0:128].rearrange("p (k h) -> p k h", h=16)), reads=[pT2b], writes=[self.b_DkT])
        f3, f3b = self.TH(6)
        fw.op(Dv, I("memset", f3[:, :], 0.0), writes=[f3b])
        for hd in range(16):
            pq = self.inproj(Win, hd * 128, ("fox", "q", hd), (0, 1))
            pk = self.inproj(Win, 2048 + hd * 128, ("fox", "k", hd), (2, 3))
            qn, qnb = self.TH(0)
            kn, knb = self.TH(1)
            for (pp, dst, dstb, gcol) in ((pq, qn, qnb, sm[:, 145:146]), (pk, kn, knb, self.pcol("fox_kg", 1))):
                for tb in range(2):
                    qs, qsb = self.TF(2)
                    fw.op(A, I("activation", out=qs[:, :], in_=pp[tb][0][:, :], func=AF.Copy), reads=[pp[tb][1]], writes=[qsb])
                    rs, rsb = self.rms_part(qs, qsb, 3, 4, 6)
                    fw.op(Dv, I("scalar_tensor_tensor", out=dst[:, tbs(tb)], in0=qs[:, :], scalar=gcol, in1=rs[:, :],
                                                                                                        op0=ALU.mult, op1=ALU.mult),
                          reads=[qsb, rsb, bs, self.b_prm], writes=[dstb])
            pv = self.inproj(Win, 4096 + hd * 128, ("fox", "v", hd), (0, 1))
            pg = self.inproj(Win, 6144 + hd * 128, ("fox", "g", hd), (2, 3))
            vfm, vfmb = self.TH(2)
            for tb in range(2):
                fw.op(A, I("activation", out=vfm[:, tbs(tb)], in_=pv[tb][0][:, :], func=AF.Copy), reads=[pv[tb][1]], writes=[vfmb])
            for n in range(8):
                fw.op(PE, I("transpose", out=self.pst[:, n * 128:(n + 1) * 128], in_=vfm[:, n * 128:(n + 1) * 128], identity=self.ident_b[:, :]),
                      reads=[vfmb, self.b_const], writes=[self.pstb])
            vt, vtb = self.TH(3)
            fw.op(Dv, I("tensor_copy", out=vt[:, :], in_=self.pst[:, :]), reads=[self.pstb], writes=[vtb])
            sgm = []
            for tb in range(2):
                t, tb_ = self.TF(6 + tb)
                fw.op(A, I("activation", out=t[:, :], in_=pg[tb][0][:, :], func=AF.Sigmoid), reads=[pg[tb][1]], writes=[tb_])
                sgm.append((t, tb_))
            fw.dma(fw.sp, f3[0:3, :], self.F3d[:, hd, :], reads=[self.b_F3d], writes=[f3b])
            if hf == 0 and self.n_halves > 1:
                fw.dma(fw.sp, self.Kd[hd], kn[:, :], reads=[knb], writes=[self.b_Kd[hd]])
                fw.dma(fw.sp, self.Vd[hd], vt[:, :], reads=[vtb], writes=[self.b_Vd[hd]])
            if hf == 1:
                kp, kpb = self.TH(4)
                vp, vpb = self.TH(5)
                fw.dma(fw.sp, kp[:, :], self.Kd[hd], reads=[self.b_Kd[hd]], writes=[kpb])
                fw.dma(fw.sp, vp[:, :], self.Vd[hd], reads=[self.b_Vd[hd]], writes=[vpb])
            for qb in range(2):
                tiles = []
                if hf == 1:
                    tiles += [("p", kt, 0) for kt in range(8)]
                for kt in range(4 * qb + 4):
                    j = kt - 4 * qb
                    tiles.append(("o", kt, 0 if j < 0 else 128 * j))
                po, pob = self.P_(6)
                pl, plb = self.P_(3)
                nt_ = len(tiles)
                for i, (kind, kt, q0) in enumerate(tiles):
                    psc, pscb = self.P_(4 + i % 2)
                    Ksrc, Kb = (kp, kpb) if kind == "p" else (kn, knb)
                    Vsrc, Vb = (vp, vpb) if kind == "p" else (vt, vtb)
                    qsl = slice(qb * TB + q0, (qb + 1) * TB)
                    fns = [I("matmul", psc[:, q0:TB], lhsT=Ksrc[:, kt * 128:(kt + 1) * 128], rhs=qn[:, qsl], start=True, stop=False),
                           I("matmul", psc[:, q0:TB], lhsT=self.ones_b[:, :], rhs=f3[:, qsl], start=False, stop=True)]
                    fw.multi(PE, fns, reads=[Kb, qnb, f3b, self.b_const], writes=[pscb])
                    if kind == "p":
                        bias = self.DkT[:, kt, hd:hd + 1]
                        bb = self.b_DkT
                    else:
                        bias = self.CsT[:, kt, hd:hd + 1]
                        bb = self.b_CsT
                    ptt, ptb = self.PT[i % 2], self.PTb[i % 2]
                    fw.op(A, I("activation", out=ptt[:, q0:TB], in_=psc[:, q0:TB], func=AF.Exp, bias=bias),
                          reads=[pscb, bb], writes=[ptb])
                    if kind == "o" and kt >= 4 * qb:
                        fw.op(Dv, I("tensor_tensor", out=ptt[:, q0:q0 + 128], in0=ptt[:, q0:q0 + 128], in1=self.mtri[:, :], op=ALU.mult),
                              reads=[ptb, self.b_const], writes=[ptb])
                    fns = [I("matmul", po[:, q0:TB], lhsT=Vsrc[:, kt * 128:(kt + 1) * 128], rhs=ptt[:, q0:TB], start=(i == 0), stop=(i == nt_ - 1)),
                           I("matmul", pl[:, q0:TB], lhsT=self.ones_b[:, :], rhs=ptt[:, q0:TB], start=(i == 0), stop=(i == nt_ - 1))]
                    fw.multi(PE, fns, reads=[Vb, ptb, self.b_const], writes=[pob, plb])
                rl, rlb = self.TF(2)
                fw.op(Dv, I("reciprocal", out=rl[:, :], in_=pl[:, :]), reads=[plb], writes=[rlb])
                on, onb = self.TF(3)
                fw.op(Dv, I("tensor_tensor", out=on[:, :], in0=po[:, :], in1=rl[:, :], op=ALU.mult), reads=[pob, rlb], writes=[onb])
                fw.op(Dv, I("tensor_tensor", out=y[:, hd, tbs(qb)], in0=on[:, :], in1=sgm[qb][0][:, :], op=ALU.mult),
                      reads=[onb, sgm[qb][1]], writes=[self.yb[hd][qb]])
        self.outproj(l, self.W["fox_w_out"], "fox")

    def mixer_hg(self, l, hf):
        fw = self.fw
        A, Dv, PE = fw.act, fw.dve, fw.pe
        self.norm(l, 1)
        Win = self.W["hg_w_in"]
        sm, bs = self.small, self.b_small
        y = self.y
        for hd in range(16):
            pq = self.inproj(Win, hd * 128, ("hg", "q", hd), (0, 1))
            pf = self.inproj(Win, 2048 + hd * 128, ("hg", "f", hd), (2, 3))
            qin, qinb = self.TH(0)
            kin, kinb = self.TH(1)
            kdec, kdecb = self.TH(2)
            for tb in range(2):
                qs, qsb = self.TF(0)
                fw.op(A, I("activation", out=qs[:, :], in_=pq[tb][0][:, :], func=AF.Copy), reads=[pq[tb][1]], writes=[qsb])
                sn, snb = self.TF(1)
                fw.op(A, I("activation", out=sn[:, :], in_=pf[tb][0][:, :], func=AF.Sigmoid, scale=-1.0), reads=[pf[tb][1]], writes=[snb])
                kk, kkb = self.TF(2)
                fw.op(Dv, I("tensor_single_scalar", out=kk[:, :], in_=sn[:, :], scalar=sm[:, 32 + hd:33 + hd], op=ALU.mult), reads=[snb, bs], writes=[kkb])
                fw.op(Dv, I("tensor_scalar", out=sn[:, :], in0=sn[:, :], scalar1=sm[:, 48 + hd:49 + hd], scalar2=1.0, op0=ALU.mult, op1=ALU.add),
                      reads=[snb, bs], writes=[snb])
                fw.op(A, I("activation", out=sn[:, :], in_=sn[:, :], func=AF.Ln), reads=[snb], writes=[snb])
                bc_, bcb = self.TF(3)
                fw.op(Dv, I("tensor_tensor_scan", out=bc_[:, :], data0=self.maskC[:, :], data1=sn[:, :], initial=0.0, op0=ALU.mult, op1=ALU.add),
                      reads=[snb, self.b_const], writes=[bcb])
                eb, ebb = self.TF(4)
                fw.op(A, I("activation", out=eb[:, :], in_=bc_[:, :], func=AF.Exp), reads=[bcb], writes=[ebb])
                fw.op(Dv, I("tensor_tensor", out=qin[:, tbs(tb)], in0=qs[:, :], in1=eb[:, :], op=ALU.mult), reads=[qsb, ebb], writes=[qinb])
                enb, enbb = self.TF(5)
                fw.op(A, I("activation", out=enb[:, :], in_=bc_[:, :], func=AF.Exp, scale=-1.0), reads=[bcb], writes=[enbb])
                fw.op(Dv, I("tensor_tensor", out=kin[:, tbs(tb)], in0=kk[:, :], in1=enb[:, :], op=ALU.mult), reads=[kkb, enbb], writes=[kinb])
                for n in range(4):
                    fw.op(A, I("activation", out=self.dec[:, tb * 4 + n:tb * 4 + n + 1], in_=bc_[:, n * 128 + 127:n * 128 + 128], func=AF.Exp),
                          reads=[bcb], writes=[self.b_dec])
                for n in range(4):
                    c0 = tb * TB + n * 128
                    fw.op(Dv, I("tensor_single_scalar", out=kdec[:, c0:c0 + 128], in_=kin[:, c0:c0 + 128],
                                                                                scalar=self.dec[:, tb * 4 + n:tb * 4 + n + 1], op=ALU.mult),
                          reads=[kinb, self.b_dec], writes=[kdecb])
            pv = self.inproj(Win, 4096 + hd * 128, ("hg", "v", hd), (0, 1))
            pg = self.inproj(Win, 6144 + hd * 128, ("hg", "g", hd), (2, 3))
            vfm, vfmb = self.TH(3)
            for tb in range(2):
                fw.op(A, I("activation", out=vfm[:, tbs(tb)], in_=pv[tb][0][:, :], func=AF.Copy), reads=[pv[tb][1]], writes=[vfmb])
            sg = []
            for tb in range(2):
                t, tb_ = self.TF(6 + tb)
                fw.op(A, I("activation", out=t[:, :], in_=pg[tb][0][:, :], func=AF.Silu), reads=[pg[tb][1]], writes=[tb_])
                sg.append((t, tb_))
            vt, vtb = self.TH(4)
            kdt, kdtb = self.TH(5)
            for (src, srcb, dst, dstb, eng) in ((vfm, vfmb, vt, vtb, Dv), (kdec, kdecb, kdt, kdtb, A)):
                for n in range(8):
                    fw.op(PE, I("transpose", out=self.pst[:, n * 128:(n + 1) * 128], in_=src[:, n * 128:(n + 1) * 128], identity=self.ident_b[:, :]),
                          reads=[srcb, self.b_const], writes=[self.pstb])
                if eng is Dv:
                    fw.op(Dv, I("tensor_copy", out=dst[:, :], in_=self.pst[:, :]), reads=[self.pstb], writes=[dstb])
                else:
                    fw.op(A, I("activation", out=dst[:, :], in_=self.pst[:, :], func=AF.Copy), reads=[self.pstb], writes=[dstb])
            S, Sb, Sbf, Sbfb = self.S, self.b_S, self.Sbf, self.b_Sbf
            if hf == 0:
                fw.op(Dv, I("memset", S[:, :], 0.0), reads=[Sb], writes=[Sb])
            else:
                fw.dma(fw.sp, S[:, :], self.Sd[hd], reads=[self.b_Sd[hd]], writes=[Sb])
            fw.op(A, I("activation", out=Sbf[:, :], in_=S[:, :], func=AF.Copy), reads=[Sb], writes=[Sbfb])
            po, pob = self.P_(6)
            for n in range(8):
                tb = n // 4
                cs = slice(n * 128, (n + 1) * 128)
                psc, pscb = self.P_(4)
                fw.op(PE, I("matmul", psc[:, 0:128], lhsT=kin[:, cs], rhs=qin[:, cs], start=True, stop=True),
                      reads=[kinb, qinb], writes=[pscb])
                scb_, scbb = self.scbf[n % 2], self.b_scbf[n % 2]
                fw.op(Dv, I("tensor_tensor", out=scb_[:, :], in0=psc[:, 0:128], in1=self.mtri[:, :], op=ALU.mult),
                      reads=[pscb, self.b_const], writes=[scbb])
                oc = slice((n % 4) * 128, (n % 4 + 1) * 128)
                fns = [I("matmul", po[:, oc], lhsT=vt[:, cs], rhs=scb_[:, :], start=True, stop=False),
                       I("matmul", po[:, oc], lhsT=Sbf[:, :], rhs=qin[:, cs], start=False, stop=True)]
                fw.multi(PE, fns, reads=[vtb, scbb, Sbfb, qinb], writes=[pob])
                pU, pUb = self.P_(5)
                fw.op(PE, I("matmul", pU[:, 0:128], lhsT=kdt[:, cs], rhs=vt[:, cs], start=True, stop=True),
                      reads=[kdtb, vtb], writes=[pUb])
                fw.op(Dv, I("scalar_tensor_tensor", out=S[:, :], in0=S[:, :], scalar=self.dec[:, n:n + 1], in1=pU[:, 0:128], op0=ALU.mult, op1=ALU.add),
                      reads=[Sb, self.b_dec, pUb], writes=[Sb])
                fw.op(A, I("activation", out=Sbf[:, :], in_=S[:, :], func=AF.Copy), reads=[Sb], writes=[Sbfb])
                if n % 4 == 3:
                    osq, osqb = self.TF(8)
                    fw.op(A, I("activation", out=osq[:, :], in_=po[:, :], func=AF.Square), reads=[pob], writes=[osqb])
                    pn, pnb = self.P_(2)
                    fw.op(PE, I("matmul", pn[:, :], lhsT=self.ones_f[:, :], rhs=osq[:, :], start=True, stop=True), reads=[osqb, self.b_const], writes=[pnb])
                    fw.op(A, I("activation", out=osq[:, :], in_=pn[:, :], func=AF.Sqrt, scale=1.0 / 128, bias=EPS), reads=[pnb], writes=[osqb])
                    rs, rsb = self.TF(9)
                    fw.op(Dv, I("reciprocal", out=rs[:, :], in_=osq[:, :]), reads=[osqb], writes=[rsb])
                    fw.op(Dv, I("scalar_tensor_tensor", out=rs[:, :], in0=po[:, :], scalar=self.pcol("hg_ng", 1), in1=rs[:, :], op0=ALU.mult, op1=ALU.mult),
                          reads=[pob, rsb, self.b_prm], writes=[rsb])
                    fw.op(Dv, I("tensor_tensor", out=y[:, hd, tbs(tb)], in0=rs[:, :], in1=sg[tb][0][:, :], op=ALU.mult),
                          reads=[rsb, sg[tb][1]], writes=[self.yb[hd][tb]])
            if hf == 0 and self.n_halves > 1:
                fw.dma(fw.sp, self.Sd[hd], S[:, :], reads=[Sb], writes=[self.b_Sd[hd]])
        self.outproj(l, self.W["hg_w_out"], "hg")

    def emit_program(self):
        fw = self.fw
        self.ada_l = None
        self.ws_reset()
        self.setup()
        for hf in range(self.n_halves):
            for d in range(NT):
                fw.dma(fw.sp, self.x[:, d, :], self.xT[hf, d * 128:(d + 1) * 128, :], writes=[self.xb[d][0], self.xb[d][1]])
            stop_all = False
            for l in self.layers:
                if self.dbg >= 2 and hf == 0:
                    if l == self.layers[0]:
                        self.ada(l)
                    else:
                        if self.ada_l == l:
                            self.ada_some(144)
                            self.ada_flush()
                    nxt = [m for m in self.layers if m > l]
                    if nxt and self.stop is None:
                        self.ada_begin(nxt[0])
                if self.dbg == 3:
                    self.norm(l, 0)
                if self.dbg < 4:
                    break
                for s_ in range(3):
                    if s_ == 1:
                        [self.mixer_rg, self.mixer_sc, self.mixer_fox, self.mixer_hg][l % 4](l, hf)
                    else:
                        self.ffn(l, 0 if s_ == 0 else 1)
                    if self.stop is not None and tuple(self.stop) == (l, s_):
                        stop_all = True
                        break
                if stop_all:
                    break
            for d in range(NT):
                fw.dma(fw.sp, self.oT[hf, d * 128:(d + 1) * 128, :], self.x[:, d, :], reads=[self.xb[d][0], self.xb[d][1]])

    def build(self):
        with ExitStack() as st:
            self.alloc(st)
            self.ws_units = []
            self.fw.planning = True
            self.emit_program()
            self.fw.planning = False
            self.emit_program()
            self.fw.finish()
            self.fw.emit()
        return self.nc


_CACHE = {}


ACTIVE = (0, 1, 4, 5)


def run(inputs, layers=(0, 1, 2, 3), n_halves=2, stop=None, trace=False, n_cores=N_CORES, spread=False):
    key = (tuple(layers), n_halves, stop)
    if key not in _CACHE:
        _CACHE[key] = Prog(layers, n_halves, stop).build()
    nc = _CACHE[key]
    x = np.asarray(inputs["x"], np.float32)
    wts = weight_arrays(inputs, layers)
    maps = []
    for b in range(n_cores):
        xt = np.ascontiguousarray(x[b].T.reshape(D, 2, T).transpose(1, 0, 2))
        m = {"xT": xt, "prm": pack_params(inputs, b)}
        m.update(wts)
        maps.append(m)
    if spread:
        zero = {k: np.zeros_like(v) for k, v in maps[0].items()}
        in_maps = [zero] * 8
        in_maps = list(in_maps)
        for b in range(n_cores):
            in_maps[ACTIVE[b]] = maps[b]
        cores = [ACTIVE[b] for b in range(n_cores)]
    else:
        in_maps = maps
        cores = list(range(n_cores))
    res = run_bass_kernel_spmd(nc, in_maps, core_ids=list(range(len(in_maps))), **({"trace": True} if trace else {}))
    out = np.empty((n_cores, 2 * T, D), np.float32)
    for b in range(n_cores):
        o = res.results[cores[b]]["oT"]
        out[b] = o.transpose(0, 2, 1).reshape(2 * T, D)
    return out, res


def kernel(**inputs):
    out, _ = run(inputs, spread=True)
    return out
```
